# Optimizing a Trainium2 kernel written in Bass

```python
import jax, jax.numpy as jnp
from jax import lax
import numpy as np

D_MODEL = 1024
BATCH = 4
SEQ = 8192
DEPTH = 1

EPS = 1e-6
D_FF = 2816
Q_BLOCK = 128
NEG = -1e30
FORCE_SCORE = 1e4
NSA_HEADS = 8
NSA_KV_GROUPS = 2
NSA_HPG = NSA_HEADS // NSA_KV_GROUPS
NSA_DK = 64
NSA_DV = 64
CMP_LEN = 32
CMP_STRIDE = 16
CMP_HID = 256
SEL_LEN = 64
SEL_TOPK = 16
WINDOW = 512
MLA_HEADS = 8
MLA_NOPE = 64
MLA_ROPE = 32
MLA_V = 64
MLA_Q_RANK = 256
MLA_KV_RANK = 128
ROPE_THETA = 10000.0
IN_SIZES = (NSA_HEADS * NSA_DK, 6 * NSA_KV_GROUPS * NSA_DK, 3 * NSA_HEADS, MLA_Q_RANK, MLA_KV_RANK, MLA_ROPE, 2 * D_MODEL)
D_IN = sum(IN_SIZES)

kernel_name = 'hybrid_nsa_mla_macaron_block'


def rms_norm(x, g):
    xf = x.astype(jnp.float32)
    y = xf * lax.rsqrt(jnp.mean(xf * xf, axis=-1, keepdims=True) + EPS)
    return (y * g.astype(jnp.float32)).astype(x.dtype)


def swiglu(x, w_gate, w_up, w_down):
    return (jax.nn.silu(x @ w_gate) * (x @ w_up)) @ w_down


def masked_softmax(s, mask):
    s = jnp.where(mask, s, NEG)
    m = jnp.max(s, axis=-1, keepdims=True)
    e = jnp.exp(s - m) * mask
    return e / jnp.maximum(jnp.sum(e, axis=-1, keepdims=True), 1e-30)


def alibi_slopes(n):
    return (2.0 ** (-8.0 * np.arange(1, n + 1, dtype=np.float32) / n)).astype(np.float32)


def apply_rope(x, pos):
    half = x.shape[-1] // 2
    freqs = jnp.asarray(ROPE_THETA ** (-np.arange(half, dtype=np.float32) / half), jnp.float32)
    ang = pos.astype(jnp.float32)[:, None] * freqs[None, :]
    cos = jnp.cos(ang)[None, :, None, :]
    sin = jnp.sin(ang)[None, :, None, :]
    xf = x.astype(jnp.float32)
    x1, x2 = xf[..., :half], xf[..., half:]
    return jnp.concatenate([x1 * cos - x2 * sin, x1 * sin + x2 * cos], axis=-1).astype(x.dtype)


def nsa_compress(kv, pos_emb, w1, w2):
    b, s, g, d = kv.shape
    n_c = (s - CMP_LEN) // CMP_STRIDE + 1
    idx = np.arange(n_c)[:, None] * CMP_STRIDE + np.arange(CMP_LEN)[None, :]
    blocks = kv[:, idx] + pos_emb[None, None, :, None, :]
    blocks = blocks.transpose(0, 1, 3, 2, 4).reshape(b, n_c, g, CMP_LEN * d)
    return jax.nn.gelu(blocks @ w1) @ w2


def nsa_attention(q, k_cmp, v_cmp, k_slc, v_slc, k_win, v_win, branch_gates,
                  cmp_pos_k, cmp_w1_k, cmp_w2_k, cmp_pos_v, cmp_w1_v, cmp_w2_v):
    b, s, h, d = q.shape
    g, hg = NSA_KV_GROUPS, NSA_HPG
    n_c = (s - CMP_LEN) // CMP_STRIDE + 1
    n_sel = s // SEL_LEN
    top_k = min(SEL_TOPK, n_sel)
    n_tok = top_k * SEL_LEN
    kc = nsa_compress(k_cmp, cmp_pos_k, cmp_w1_k, cmp_w2_k)
    vc = nsa_compress(v_cmp, cmp_pos_v, cmp_w1_v, cmp_w2_v)
    cmp_end = jnp.asarray(np.arange(n_c) * CMP_STRIDE + CMP_LEN - 1, jnp.int32)
    c0 = np.arange(n_c) * CMP_STRIDE
    s0 = np.arange(n_sel) * SEL_LEN
    overlap = np.clip(np.minimum(c0[:, None] + CMP_LEN, s0[None, :] + SEL_LEN)
                      - np.maximum(c0[:, None], s0[None, :]), 0, None)
    overlap = jnp.asarray(overlap / CMP_LEN, jnp.float32)
    ks_blocks = k_slc.reshape(b, n_sel, SEL_LEN, g, d).transpose(0, 3, 1, 2, 4)
    vs_blocks = v_slc.reshape(b, n_sel, SEL_LEN, g, NSA_DV).transpose(0, 3, 1, 2, 4)
    kw = jnp.pad(k_win, ((0, 0), (WINDOW, 0), (0, 0), (0, 0)))
    vw = jnp.pad(v_win, ((0, 0), (WINDOW, 0), (0, 0), (0, 0)))
    qg = (q * NSA_DK ** -0.5).reshape(b, s, g, hg, d)
    gates = jax.nn.sigmoid(branch_gates.astype(jnp.float32)).reshape(b, s, g, hg, 3)
    slopes = jnp.asarray(alibi_slopes(h)).reshape(g, hg)[None, :, :, None, None]
    gather_blocks = jax.vmap(jax.vmap(lambda blk, ix: blk[ix]))
    sel_offsets = jnp.arange(SEL_LEN)
    blk_ids = jnp.arange(n_sel)
    win_offsets = jnp.arange(Q_BLOCK + WINDOW) - WINDOW

    def one_block(qi):
        q0 = qi * Q_BLOCK
        t = q0 + jnp.arange(Q_BLOCK)
        qb = lax.dynamic_slice_in_dim(qg, q0, Q_BLOCK, axis=1)
        gb = lax.dynamic_slice_in_dim(gates, q0, Q_BLOCK, axis=1)
        dist_c = t[:, None] - cmp_end[None, :]
        s_c = jnp.einsum('btghd,bigd->bghti', qb, kc).astype(jnp.float32) - slopes * dist_c.astype(jnp.float32)
        p_c = masked_softmax(s_c, dist_c >= 0)
        o_c = jnp.einsum('bghti,bigd->btghd', p_c.astype(vc.dtype), vc)
        imp = jnp.einsum('bghti,ij->bgtj', p_c, overlap)
        cur = (t // SEL_LEN)[:, None]
        forced = (blk_ids[None, :] == 0) | (blk_ids[None, :] == cur) | (blk_ids[None, :] == cur - 1)
        imp = jnp.where(blk_ids[None, :] <= cur, jnp.where(forced, FORCE_SCORE, imp), NEG)
        _, sel = lax.top_k(imp, top_k)
        ksel = gather_blocks(ks_blocks, sel).reshape(b, g, Q_BLOCK, n_tok, d)
        vsel = gather_blocks(vs_blocks, sel).reshape(b, g, Q_BLOCK, n_tok, NSA_DV)
        pos_s = (sel[..., None] * SEL_LEN + sel_offsets).reshape(b, g, Q_BLOCK, n_tok)
        dist_s = (t[None, None, :, None] - pos_s)[:, :, None]
        s_s = jnp.einsum('btghd,bgtnd->bghtn', qb, ksel).astype(jnp.float32) - slopes * dist_s.astype(jnp.float32)
        p_s = masked_softmax(s_s, dist_s >= 0)
        o_s = jnp.einsum('bghtn,bgtnd->btghd', p_s.astype(vsel.dtype), vsel)
        kwb = lax.dynamic_slice_in_dim(kw, q0, Q_BLOCK + WINDOW, axis=1)
        vwb = lax.dynamic_slice_in_dim(vw, q0, Q_BLOCK + WINDOW, axis=1)
        pos_w = q0 + win_offsets
        dist_w = t[:, None] - pos_w[None, :]
        mask_w = (dist_w >= 0) & (dist_w < WINDOW) & (pos_w[None, :] >= 0)
        s_w = jnp.einsum('btghd,bsgd->bghts', qb, kwb).astype(jnp.float32) - slopes * dist_w.astype(jnp.float32)
        p_w = masked_softmax(s_w, mask_w)
        o_w = jnp.einsum('bghts,bsgd->btghd', p_w.astype(vwb.dtype), vwb)
        o = gb[..., 0:1] * o_c + gb[..., 1:2] * o_s + gb[..., 2:3] * o_w
        return o.astype(q.dtype).reshape(b, Q_BLOCK, h * NSA_DV)

    out = lax.map(one_block, jnp.arange(s // Q_BLOCK))
    return out.transpose(1, 0, 2, 3).reshape(b, s, h * NSA_DV)


def mla_attention(c_q, c_kv, k_pe, q_norm_g, w_uq, kv_norm_g, w_ukv):
    b, s, _ = c_q.shape
    h = MLA_HEADS
    pos = jnp.arange(s)
    q = (rms_norm(c_q, q_norm_g) @ w_uq).reshape(b, s, h, MLA_NOPE + MLA_ROPE)
    kv = (rms_norm(c_kv, kv_norm_g) @ w_ukv).reshape(b, s, h, MLA_NOPE + MLA_V)
    q_pe = apply_rope(q[..., MLA_NOPE:], pos)
    k_rot = apply_rope(k_pe[:, :, None, :], pos)
    qf = jnp.concatenate([q[..., :MLA_NOPE], q_pe], axis=-1) * (MLA_NOPE + MLA_ROPE) ** -0.5
    k = jnp.concatenate([kv[..., :MLA_NOPE], jnp.broadcast_to(k_rot, (b, s, h, MLA_ROPE))], axis=-1)
    v = kv[..., MLA_NOPE:]

    def one_block(qi):
        q0 = qi * Q_BLOCK
        t = q0 + jnp.arange(Q_BLOCK)
        qb = lax.dynamic_slice_in_dim(qf, q0, Q_BLOCK, axis=1)
        sc = jnp.einsum('bthd,bshd->bhts', qb, k).astype(jnp.float32)
        sc = jnp.where(pos[None, :] <= t[:, None], sc, NEG)
        p = jax.nn.softmax(sc, axis=-1)
        o = jnp.einsum('bhts,bshd->bthd', p.astype(v.dtype), v)
        return o.reshape(b, Q_BLOCK, h * MLA_V)

    out = lax.map(one_block, jnp.arange(s // Q_BLOCK))
    return out.transpose(1, 0, 2, 3).reshape(b, s, h * MLA_V)


def setup_inputs(seed: int = 0) -> dict:
    key = jax.random.key(seed)
    ks = jax.random.split(key, 32)

    def dense(k, shape, fan_in):
        return jax.random.normal(k, shape, jnp.float32) * fan_in ** -0.5

    def gain(k, n):
        return 1.0 + 0.05 * jax.random.normal(k, (n,), jnp.float32)

    d_nsa = NSA_HEADS * NSA_DV
    d_mla = MLA_HEADS * MLA_V
    return {
        'x': jax.random.normal(ks[0], (BATCH, SEQ, D_MODEL), jnp.float32),
        'ff1_pre_g': gain(ks[1], D_MODEL),
        'ff1_post_g': gain(ks[2], D_MODEL),
        'ff1_w_gate': dense(ks[3], (D_MODEL, D_FF), D_MODEL),
        'ff1_w_up': dense(ks[4], (D_MODEL, D_FF), D_MODEL),
        'ff1_w_down': dense(ks[5], (D_FF, D_MODEL), D_FF),
        'mix_pre_g': gain(ks[6], D_MODEL),
        'mix_post_g': gain(ks[7], D_MODEL),
        'w_in': dense(ks[8], (D_MODEL, D_IN), D_MODEL),
        'cmp_pos_k': 0.1 * jax.random.normal(ks[9], (CMP_LEN, NSA_DK), jnp.float32),
        'cmp_w1_k': dense(ks[10], (CMP_LEN * NSA_DK, CMP_HID), CMP_LEN * NSA_DK),
        'cmp_w2_k': dense(ks[11], (CMP_HID, NSA_DK), CMP_HID),
        'cmp_pos_v': 0.1 * jax.random.normal(ks[12], (CMP_LEN, NSA_DV), jnp.float32),
        'cmp_w1_v': dense(ks[13], (CMP_LEN * NSA_DV, CMP_HID), CMP_LEN * NSA_DV),
        'cmp_w2_v': dense(ks[14], (CMP_HID, NSA_DV), CMP_HID),
        'mla_q_norm_g': gain(ks[15], MLA_Q_RANK),
        'mla_w_uq': dense(ks[16], (MLA_Q_RANK, MLA_HEADS * (MLA_NOPE + MLA_ROPE)), MLA_Q_RANK),
        'mla_kv_norm_g': gain(ks[17], MLA_KV_RANK),
        'mla_w_ukv': dense(ks[18], (MLA_KV_RANK, MLA_HEADS * (MLA_NOPE + MLA_V)), MLA_KV_RANK),
        'w_proj_nsa': dense(ks[19], (d_nsa, D_MODEL), d_nsa),
        'w_proj_mla': dense(ks[20], (d_mla, D_MODEL), d_mla),
        'w_out': dense(ks[21], (D_MODEL, D_MODEL), D_MODEL),
        'ff2_pre_g': gain(ks[22], D_MODEL),
        'ff2_post_g': gain(ks[23], D_MODEL),
        'ff2_w_gate': dense(ks[24], (D_MODEL, D_FF), D_MODEL),
        'ff2_w_up': dense(ks[25], (D_MODEL, D_FF), D_MODEL),
        'ff2_w_down': dense(ks[26], (D_FF, D_MODEL), D_FF),
    }


def reference(x, ff1_pre_g, ff1_post_g, ff1_w_gate, ff1_w_up, ff1_w_down, mix_pre_g, mix_post_g, w_in,
              cmp_pos_k, cmp_w1_k, cmp_w2_k, cmp_pos_v, cmp_w1_v, cmp_w2_v,
              mla_q_norm_g, mla_w_uq, mla_kv_norm_g, mla_w_ukv, w_proj_nsa, w_proj_mla, w_out,
              ff2_pre_g, ff2_post_g, ff2_w_gate, ff2_w_up, ff2_w_down):
    b, s, _ = x.shape
    split_at = [int(v) for v in np.cumsum(IN_SIZES)[:-1]]
    for _layer in range(DEPTH):
        x = x + 0.5 * rms_norm(swiglu(rms_norm(x, ff1_pre_g), ff1_w_gate, ff1_w_up, ff1_w_down), ff1_post_g)
        hmix = rms_norm(x, mix_pre_g)
        z = hmix @ w_in
        q_nsa, kv_nsa, g_nsa, c_q, c_kv, k_pe, g_merge = jnp.split(z, split_at, axis=-1)
        q_nsa = q_nsa.reshape(b, s, NSA_HEADS, NSA_DK)
        kv_nsa = kv_nsa.reshape(b, s, 6, NSA_KV_GROUPS, NSA_DK)
        g_nsa = g_nsa.reshape(b, s, NSA_HEADS, 3)
        y_nsa = nsa_attention(q_nsa, kv_nsa[:, :, 0], kv_nsa[:, :, 1], kv_nsa[:, :, 2], kv_nsa[:, :, 3],
                              kv_nsa[:, :, 4], kv_nsa[:, :, 5], g_nsa,
                              cmp_pos_k, cmp_w1_k, cmp_w2_k, cmp_pos_v, cmp_w1_v, cmp_w2_v)
        y_mla = mla_attention(c_q, c_kv, k_pe, mla_q_norm_g, mla_w_uq, mla_kv_norm_g, mla_w_ukv)
        gate_a, gate_b = jnp.split(jax.nn.sigmoid(g_merge.astype(jnp.float32)), 2, axis=-1)
        merged = (gate_a * (y_nsa @ w_proj_nsa) + gate_b * (y_mla @ w_proj_mla)).astype(x.dtype)
        x = x + rms_norm(merged @ w_out, mix_post_g)
        x = x + 0.5 * rms_norm(swiglu(rms_norm(x, ff2_pre_g), ff2_w_gate, ff2_w_up, ff2_w_down), ff2_post_g)
    return x
```

```python
import numpy as np
import ml_dtypes
import concourse.bass as bass
import concourse.mybir as mybir
from concourse.bass_utils import run_bass_kernel_spmd

F32 = mybir.dt.float32
BF16 = mybir.dt.bfloat16
ALU = mybir.AluOpType
AF = mybir.ActivationFunctionType
AX = mybir.AxisListType


class T:
    def __init__(self, name, h):
        self.name = name
        self.h = h

    def __getitem__(self, idx):
        return self.h[idx]


class Sched:
    ENGS = ("pe", "act", "dve", "pool", "sp")

    def __init__(self, nc, n_dma_sems=32):
        self.nc = nc
        self.sem = {e: nc.alloc_semaphore("sem_" + e) for e in self.ENGS}
        self.cnt = {e: 0 for e in self.ENGS}
        self.prog = {e: [] for e in self.ENGS}
        self.seen = {e: {} for e in self.ENGS}
        self.dsem = [nc.alloc_semaphore("dsem%d" % i) for i in range(n_dma_sems)]
        self.dcnt = [0] * n_dma_sems
        self.dnext = 0
        self.state = {}
        self.n_ops = 0

    SB_BASE = 16512
    SB_END = 229376 - 64

    def sbuf(self, name, shape, dtype):
        esz = {F32: 4, BF16: 2}.get(dtype, 4)
        n = 1
        for d in shape[1:]:
            n *= d
        nbytes = (n * esz + 63) // 64 * 64
        off = getattr(self, "_sb", self.SB_BASE)
        assert off + nbytes <= self.SB_END, ("SBUF overflow", name, off, nbytes)
        self._sb = off + nbytes
        self._uid = getattr(self, "_uid", 0) + 1
        uname = "%s_%d" % (name, self._uid)
        return T(uname, self.nc.alloc_sbuf_tensor_at(uname, list(shape), dtype, offset=off))

    def mark(self):
        return getattr(self, "_sb", self.SB_BASE)

    def release(self, mark):
        self.barrier()
        self._sb = mark

    def barrier(self):
        targets = []
        for k, c in enumerate(self.dcnt):
            if c > 0:
                targets.append((("d", k), c))
        for e in self.ENGS:
            if self.cnt[e] > 0:
                targets.append((("e", e), self.cnt[e]))
        for e in self.ENGS:
            waits = []
            for sk, val in targets:
                if sk == ("e", e):
                    continue
                if self.seen[e].get(sk, 0) >= val:
                    continue
                self.seen[e][sk] = val
                waits.append((sk, val))
            if waits:
                self.prog[e].append((waits, None, None))

    def psum(self, name, shape, dtype=F32):
        return T(name, self.nc.alloc_psum_tensor(name, list(shape), dtype))

    def _key(self, b):
        if isinstance(b, tuple):
            t, slot = b
        else:
            t, slot = b, None
        name = t.name if isinstance(t, T) else t
        return name, slot

    def _states(self, b, create=True):
        name, slot = self._key(b)
        d = self.state.setdefault(name, {})
        if None not in d:
            d[None] = {"w": None, "r": {}}
        if slot is None:
            return list(d.values())
        if slot not in d:
            d[slot] = {"w": d[None]["w"], "r": dict(d[None]["r"])}
        return [d[slot]]

    def _deps(self, reads, writes):
        deps = {}

        def add(sk, val):
            if val > deps.get(sk, 0):
                deps[sk] = val

        for b in reads:
            for st in self._states(b):
                if st["w"] is not None:
                    add(*st["w"])
        for b in writes:
            for st in self._states(b):
                if st["w"] is not None:
                    add(*st["w"])
                for sk, val in st["r"].items():
                    add(sk, val)
        return deps

    def _update(self, reads, writes, sk, val):
        for b in reads:
            for st in self._states(b):
                if val > st["r"].get(sk, 0):
                    st["r"][sk] = val
        for b in writes:
            for st in self._states(b):
                st["w"] = (sk, val)
                st["r"] = {}

    def _waits(self, eng, deps):
        waits = []
        for sk, val in deps.items():
            if sk == ("e", "pe") and eng == "pe":
                continue
            if self.seen[eng].get(sk, 0) >= val:
                continue
            self.seen[eng][sk] = val
            waits.append((sk, val))
        return waits

    def op(self, eng, fn, reads=(), writes=()):
        deps = self._deps(reads, writes)
        waits = self._waits(eng, deps)
        self.cnt[eng] += 1
        val = self.cnt[eng]
        self.prog[eng].append((waits, fn, ("e", eng)))
        self._update(reads, writes, ("e", eng), val)
        self.n_ops += 1

    def dma(self, eng, out, in_, reads=(), writes=()):
        k = self.dnext
        self.dnext = (self.dnext + 1) % len(self.dsem)
        deps = self._deps(reads, writes)
        if self.dcnt[k] > 0:
            sk = ("d", k)
            if self.dcnt[k] > deps.get(sk, 0):
                deps[sk] = self.dcnt[k]
        waits = self._waits(eng, deps)
        self.dcnt[k] += 16
        val = self.dcnt[k]
        self.prog[eng].append((waits, lambda e, o=out, i=in_: e.dma_start(out=o, in_=i), ("d", k)))
        self._update(reads, writes, ("d", k), val)
        self.n_ops += 1

    def _semh(self, sk):
        return self.sem[sk[1]] if sk[0] == "e" else self.dsem[sk[1]]

    def finish(self):
        final = []
        for k, c in enumerate(self.dcnt):
            if c > 0:
                final.append((("d", k), c))
        for e in self.ENGS:
            if e != "sp" and self.cnt[e] > 0:
                final.append((("e", e), self.cnt[e]))
        nc = self.nc
        prog = self.prog
        semh = self._semh

        def emit(engname, e):
            for waits, fn, inc in prog[engname]:
                for sk, val in waits:
                    e.wait_ge(semh(sk), val)
                if fn is None:
                    continue
                ins = fn(e)
                if inc[0] == "e":
                    ins.then_inc(semh(inc), 1)
                else:
                    ins.then_inc(semh(inc), 16)
            if engname == "sp":
                for sk, val in final:
                    e.wait_ge(semh(sk), val)

        with nc.Block() as block:
            @block.sync
            def _(e):
                emit("sp", e)

            if prog["pe"]:
                @block.tensor
                def _(e):
                    emit("pe", e)

            if prog["act"]:
                @block.scalar
                def _(e):
                    emit("act", e)

            if prog["dve"]:
                @block.vector
                def _(e):
                    emit("dve", e)

            if prog["pool"]:
                @block.gpsimd
                def _(e):
                    emit("pool", e)


D = 1024
DFF = 2816
NJ = DFF // 128
EPS = 1e-6
NEGM = -1.0e4
ALIBI_CUT = 164.0
OQ = 0
OKC = 512
OVC = 640
OKS = 768
OKW = 896
OVS = 1024
OVW = 1152
OCKV = 1280
OKPE = 1408
OCQ = 1440
OGN = 1696
OGM = 1720
WIN_A = 1720


class G:
    pass


def bc(ap_col):
    return ap_col


def load_ffn_weights(S, g, wg_d, wu_d, wd_d, precol_d, postrow_d, tag):
    W = {}
    W["wg"] = S.sbuf(tag + "wg", [128, 8, DFF], BF16)
    W["wu"] = S.sbuf(tag + "wu", [128, 8, DFF], BF16)
    W["wd"] = S.sbuf(tag + "wd", [128, NJ, D], BF16)
    W["ghalf"] = S.sbuf(tag + "ghalf", [128, D], F32)
    W["precol"] = S.sbuf(tag + "precol", [128, 8], F32)
    m = S.mark()
    st = [S.sbuf(tag + "st%d" % i, [128, DFF], F32) for i in range(3)]
    S.dma("sp", W["precol"][:], precol_d, writes=[W["precol"]])
    S.dma("sp", st[2][:, 0:D], postrow_d, writes=[st[2]])
    S.op("dve", lambda e: e.tensor_scalar(W["ghalf"][:], st[2][:, 0:D], 0.5, None, ALU.mult), reads=[st[2]], writes=[W["ghalf"]])
    i = 0
    for name, src in (("wg", wg_d), ("wu", wu_d)):
        for k in range(8):
            s_ = st[i % 3]
            S.dma(("sp", "act")[i % 2], s_[:], src[k * 128:(k + 1) * 128, :], writes=[s_])
            dst = W[name]
            if i % 2 == 0:
                S.op("act", lambda e, dst=dst, s_=s_, k=k: e.activation(dst[:, k, :], s_[:], AF.Copy, scale=W["precol"][:, k:k + 1]),
                     reads=[s_, W["precol"]], writes=[(dst, k)])
            else:
                S.op("dve", lambda e, dst=dst, s_=s_, k=k: e.tensor_scalar(dst[:, k, :], s_[:], W["precol"][:, k:k + 1], None, ALU.mult),
                     reads=[s_, W["precol"]], writes=[(dst, k)])
            i += 1
    S.release(m)
    wdst = S.sbuf(tag + "wdst", [128, 2 * D], F32)
    wdv = wd_d.rearrange("(j p) n -> p j n", p=128)

    def load_wd_pair(j0):
        S.dma("sp", wdst[:].rearrange("p (j n) -> p j n", j=2), wdv[:, j0:j0 + 2, :], writes=[wdst])
        if (j0 // 2) % 2 == 0:
            S.op("act", lambda e: e.activation(W["wd"][:, j0:j0 + 2, :], wdst[:].rearrange("p (j n) -> p j n", j=2), AF.Copy),
                 reads=[wdst], writes=[(W["wd"], j0), (W["wd"], j0 + 1)])
        else:
            S.op("dve", lambda e: e.tensor_copy(W["wd"][:, j0:j0 + 2, :], wdst[:].rearrange("p (j n) -> p j n", j=2)),
                 reads=[wdst], writes=[(W["wd"], j0), (W["wd"], j0 + 1)])

    W["load_wd_pair"] = load_wd_pair
    return W


def alloc_ffn_work(S, tag):
    B = {}
    B["xn"] = S.sbuf(tag + "xn", [128, 4, D], BF16)
    B["xnT"] = S.sbuf(tag + "xnT", [128, 8, 512], BF16)
    B["hT"] = S.sbuf(tag + "hT", [128, NJ, 512], BF16)
    B["sg"] = [S.sbuf(tag + "sg%d" % i, [128, 512], F32) for i in range(2)]
    B["tmp"] = S.sbuf(tag + "tmp", [128, 512], F32)
    B["junk"] = S.sbuf(tag + "junk", [128, D], BF16)
    B["st"] = S.sbuf(tag + "stat", [128, 32], F32)
    return B


def rms_rstd(S, g, B, ss_ap, n, out_ap, rd, nfeat):
    st = B["st"]
    S.op("dve", lambda e: e.tensor_scalar(st[:, 16:16 + n], ss_ap, 1.0 / nfeat, EPS, ALU.mult, ALU.add), reads=rd, writes=[(st, "ms")])
    S.op("act", lambda e: e.activation(st[:, 24:24 + n], st[:, 16:16 + n], AF.Sqrt), reads=[(st, "ms")], writes=[(st, "sd")])
    S.op("dve", lambda e: e.reciprocal(out_ap, st[:, 24:24 + n]), reads=[(st, "sd")], writes=[(st, "rstd")])


def norm_transpose(S, g, B, xt):
    st, xn, xnT, junk = B["st"], B["xn"], B["xnT"], B["junk"]
    for c in range(4):
        if c % 2 == 0:
            S.op("act", lambda e, c=c: e.activation(junk[:], xt[:, c, :], AF.Square, accum_out=st[:, c:c + 1]),
                 reads=[(xt, c)], writes=[junk, (st, "ss%d" % c)])
        else:
            S.op("dve", lambda e, c=c: e.scalar_tensor_tensor(xn[:, c, :], xt[:, c, :], 1.0, xt[:, c, :], ALU.mult, ALU.mult, accum_out=st[:, c:c + 1]),
                 reads=[(xt, c)], writes=[(xn, c), (st, "ss%d" % c)])
    rms_rstd(S, g, B, st[:, 0:4], 4, st[:, 8:12], [(st, "ss%d" % c) for c in range(4)], D)
    for c in range(4):
        if c % 2 == 0:
            S.op("act", lambda e, c=c: e.activation(xn[:, c, :], xt[:, c, :], AF.Copy, scale=st[:, 8 + c:9 + c]),
                 reads=[(xt, c), (st, "rstd")], writes=[(xn, c)])
        else:
            S.op("dve", lambda e, c=c: e.tensor_scalar(xn[:, c, :], xt[:, c, :], st[:, 8 + c:9 + c], None, ALU.mult),
                 reads=[(xt, c), (st, "rstd")], writes=[(xn, c)])
    transpose_to(S, g, xn, xnT)


def transpose_to(S, g, src, dstT, nk=8):
    for k0 in range(0, nk, 2):
        kk = min(2, nk - k0)
        pt = g.pTb[(k0 // 2) % 2]
        for k in range(k0, k0 + kk):
            for c in range(4):
                S.op("pe", lambda e, k=k, c=c, pt=pt, k0=k0: e.transpose(pt[:, (k - k0) * 512 + c * 128:(k - k0) * 512 + (c + 1) * 128], src[:, c, k * 128:(k + 1) * 128], g.ident_bf[:]),
                     reads=[(src, c), g.ident_bf], writes=[pt])
        dst = dstT[:, k0:k0 + kk, :]
        srcp = pt[:, 0:kk * 512].rearrange("p (k n) -> p k n", n=512)
        if (k0 // 2) % 2 == 0:
            S.op("dve", lambda e, dst=dst, srcp=srcp: e.tensor_copy(dst, srcp), reads=[pt], writes=[(dstT, k0), (dstT, k0 + kk - 1)])
        else:
            S.op("act", lambda e, dst=dst, srcp=srcp: e.activation(dst, srcp, AF.Copy), reads=[pt], writes=[(dstT, k0), (dstT, k0 + kk - 1)])


def ffn_tile(S, g, W, B, xt, after_chunk=None, gu_hook=None):
    st, xnT, hT, sg, tmp, junk = B["st"], B["xnT"], B["hT"], B["sg"], B["tmp"], B["junk"]
    norm_transpose(S, g, B, xt)
    P = g.pG
    for j in range(NJ):
        pg = P[(j % 2) * 2]
        pu = P[(j % 2) * 2 + 1]
        for k in range(8):
            S.op("pe", lambda e, j=j, k=k, pg=pg: e.matmul(pg[:], W["wg"][:, k, j * 128:(j + 1) * 128], xnT[:, k, :], start=(k == 0), stop=(k == 7)),
                 reads=[(W["wg"], k), (xnT, k)], writes=[pg])
        for k in range(8):
            S.op("pe", lambda e, j=j, k=k, pu=pu: e.matmul(pu[:], W["wu"][:, k, j * 128:(j + 1) * 128], xnT[:, k, :], start=(k == 0), stop=(k == 7)),
                 reads=[(W["wu"], k), (xnT, k)], writes=[pu])
        s_ = sg[j % 2]
        S.op("act", lambda e, pg=pg, s_=s_: e.activation(s_[:], pg[:], AF.Silu), reads=[pg], writes=[s_])
        S.op("dve", lambda e, pu=pu, s_=s_, j=j: e.tensor_tensor(hT[:, j, :], s_[:], pu[:], ALU.mult), reads=[s_, pu], writes=[(hT, j)])
        if gu_hook is not None:
            gu_hook(j)
    for c in range(4):
        pd = [P[(c % 2) * 2], P[(c % 2) * 2 + 1]]
        for nh in range(2):
            for j in range(NJ):
                S.op("pe", lambda e, c=c, nh=nh, j=j, pd=pd: e.matmul(pd[nh][:], hT[:, j, c * 128:(c + 1) * 128], W["wd"][:, j, nh * 512:(nh + 1) * 512], start=(j == 0), stop=(j == NJ - 1)),
                     reads=[(hT, j), (W["wd"], j)], writes=[pd[nh]])
            S.op("act", lambda e, nh=nh, pd=pd: e.activation(junk[:, 0:512], pd[nh][:], AF.Square, accum_out=st[:, 4 + nh:5 + nh]),
                 reads=[pd[nh]], writes=[junk, (st, "ss2")])
        S.op("dve", lambda e: e.tensor_tensor(st[:, 6:7], st[:, 4:5], st[:, 5:6], ALU.add), reads=[(st, "ss2")], writes=[(st, "ss2s")])
        rms_rstd(S, g, B, st[:, 6:7], 1, st[:, 12:13], [(st, "ss2s")], D)
        for nh in range(2):
            S.op("dve", lambda e, nh=nh, pd=pd: e.scalar_tensor_tensor(tmp[:], pd[nh][:], st[:, 12:13], W["ghalf"][:, nh * 512:(nh + 1) * 512], ALU.mult, ALU.mult),
                 reads=[pd[nh], (st, "rstd"), W["ghalf"]], writes=[tmp])
            S.op("dve", lambda e, c=c, nh=nh: e.tensor_tensor(xt[:, c, nh * 512:(nh + 1) * 512], tmp[:], xt[:, c, nh * 512:(nh + 1) * 512], ALU.add),
                 reads=[tmp, (xt, c)], writes=[(xt, c)])
        if after_chunk is not None:
            after_chunk(c)


class Rot:
    def __init__(self, banks):
        self.banks = banks
        self.i = 0

    def __call__(self):
        b = self.banks[self.i % len(self.banks)]
        self.i += 1
        return b


def cast_copy(S, i, out_ap, in_ap, reads, writes, scale=None):
    if i % 2 == 0:
        if scale is None:
            S.op("act", lambda e: e.activation(out_ap, in_ap, AF.Copy), reads=reads, writes=writes)
        else:
            S.op("act", lambda e: e.activation(out_ap, in_ap, AF.Copy, scale=scale), reads=reads, writes=writes)
    else:
        if scale is None:
            S.op("dve", lambda e: e.tensor_copy(out_ap, in_ap), reads=reads, writes=writes)
        else:
            S.op("dve", lambda e: e.tensor_scalar(out_ap, in_ap, scale, None, ALU.mult), reads=reads, writes=writes)


def phase_1b(S, g, I):
    SEQ, NT, NP = g.SEQ, g.NT, g.NP
    rot = Rot(g.pG)
    WinA = S.sbuf("WinA", [128, 8, WIN_A], BF16)
    Wrot = S.sbuf("Wrot", [128, 8, 32], BF16)
    Wuq = S.sbuf("Wuq", [128, 2, 768], BF16)
    Wuqr = S.sbuf("Wuqr", [128, 2, 768], BF16)
    Wukv = S.sbuf("Wukv", [128, 1024], BF16)
    cols = S.sbuf("cols1b", [128, 16], F32)
    S.dma("sp", cols[:, 0:8], I["mix_precol"], writes=[(cols, 0)])
    S.dma("sp", cols[:, 8:10], I["q_norm_col"], writes=[(cols, 1)])
    S.dma("sp", cols[:, 10:11], I["kv_norm_col"], writes=[(cols, 2)])
    S.dma("sp", cols[:, 12:14], I["hm"], writes=[(cols, 3)])
    m = S.mark()
    st = [S.sbuf("st1b%d" % i, [128, WIN_A], F32) for i in range(2)]
    for k in range(8):
        s_ = st[k % 2]
        S.dma(("sp", "act")[k % 2], s_[:], I["w_in_p"][k * 128:(k + 1) * 128, 0:WIN_A], writes=[s_])
        cast_copy(S, k, WinA[:, k, :], s_[:], [s_, (cols, 0)], [(WinA, k)], scale=cols[:, k:k + 1])
    S.op("act", lambda e: e.activation(Wrot[:, :, 0:16], WinA[:, :, OKPE + 16:OKPE + 32], AF.Copy, scale=-1.0), reads=[WinA], writes=[(Wrot, 0)])
    S.op("dve", lambda e: e.tensor_copy(Wrot[:, :, 16:32], WinA[:, :, OKPE:OKPE + 16]), reads=[WinA], writes=[(Wrot, 1)])
    for k2 in range(2):
        s_ = st[k2 % 2]
        S.dma("sp", s_[:, 0:768], I["mla_w_uq"][k2 * 128:(k2 + 1) * 128, :], writes=[s_])
        cast_copy(S, k2, Wuq[:, k2, :], s_[:, 0:768], [s_, (cols, 1)], [(Wuq, k2)], scale=cols[:, 8 + k2:9 + k2])
    S.op("pool", lambda e: e.memset(Wuqr[:], 0.0), writes=[Wuqr])
    Wuq4 = Wuq[:].rearrange("p k (h e) -> p k h e", e=96)
    Wuqr4 = Wuqr[:].rearrange("p k (h e) -> p k h e", e=96)
    for k2 in range(2):
        S.op("act", lambda e, k2=k2: e.activation(Wuqr4[:, k2, :, 64:80], Wuq4[:, k2, :, 80:96], AF.Copy, scale=-1.0), reads=[Wuq], writes=[Wuqr])
        S.op("dve", lambda e, k2=k2: e.tensor_copy(Wuqr4[:, k2, :, 80:96], Wuq4[:, k2, :, 64:80]), reads=[Wuq], writes=[Wuqr])
    s_ = st[0]
    S.dma("sp", s_[:, 0:1024], I["mla_w_ukv_p"], writes=[s_])
    cast_copy(S, 1, Wukv[:], s_[:, 0:1024], [s_, (cols, 2)], [Wukv], scale=cols[:, 10:11])
    hA = S.sbuf("hA", [128, 8, 512], BF16)
    hB = S.sbuf("hB", [128, 8, 512], BF16)
    hO = S.sbuf("hO", [128, 8, 512], BF16)
    kst = S.sbuf("kst", [64, 8, 512], BF16)
    vst = S.sbuf("vst", [128, 4, 4, 65], BF16)
    kn = S.sbuf("kn", [128, 4, 128], BF16)
    knT = S.sbuf("knT", [128, 1, 512], BF16)
    knst = S.sbuf("knst", [64, 8, 512], BF16)
    vmst = S.sbuf("vmst", [128, 4, 8, 65], BF16)
    krst = S.sbuf("krst", [32, 512], BF16)
    t1 = S.sbuf("t1", [96, 512], F32)
    t2 = S.sbuf("t2", [96, 512], F32)
    ck = S.sbuf("ck", [32, 512], F32)
    sk = S.sbuf("sk", [32, 512], F32)
    cq = S.sbuf("cq", [96, 512], F32)
    sq = S.sbuf("sq", [96, 512], F32)
    qst = S.sbuf("qst", [64, 8, 512], BF16)
    gst = S.sbuf("gst", [128, 4, 24], F32)
    qn = S.sbuf("qn", [128, 4, 256], BF16)
    qnT = S.sbuf("qnT", [128, 2, 512], BF16)
    qmst = S.sbuf("qmst", [96, 8, 512], BF16)
    junk = S.sbuf("junk1b", [128, 256], BF16)
    B = {"st": S.sbuf("stat1b", [128, 32], F32)}
    stt = B["st"]
    zt = S.sbuf("zt", [64, 8, 16], BF16)
    S.op("pool", lambda e: e.memset(zt[:], 0.0), writes=[zt])
    S.dma("sp", g.kTs.rearrange("t d s -> d t s")[:, :, SEQ:SEQ + 16], zt[:], reads=[zt], writes=["kTs"])
    S.op("pool", lambda e: e.memset(vst[:], 1.0), writes=[vst])
    S.op("pool", lambda e: e.memset(vmst[:], 1.0), writes=[vmst])
    cnt = [0]

    def cc(out_ap, in_ap, reads, writes, scale=None):
        cast_copy(S, cnt[0], out_ap, in_ap, reads, writes, scale)
        cnt[0] += 1

    def kside(T, h, hook=None):
        cs = slice(T * 512, (T + 1) * 512)
        for ti in range(8):
            off = OKC + ti * 64
            ps = rot()
            for k in range(8):
                S.op("pe", lambda e, k=k, ps=ps, off=off: e.matmul(ps[0:64, :], WinA[:, k, off:off + 64], h[:, k, :], start=(k == 0), stop=(k == 7)),
                     reads=[(WinA, k), h], writes=[ps])
            cc(kst[:, ti, :], ps[0:64, :], [ps], [(kst, ti)])
            if hook is not None:
                hook(ti)
        S.dma("sp", g.kTs.rearrange("t d s -> d t s")[:, :, cs], kst[:], reads=[kst], writes=["kTs"])
        for c in range(4):
            ps = rot()
            for k in range(8):
                S.op("pe", lambda e, k=k, c=c, ps=ps: e.matmul(ps[:, 0:384], h[:, k, c * 128:(c + 1) * 128], WinA[:, k, OVS:OVS + 384], start=(k == 0), stop=(k == 7)),
                     reads=[(WinA, k), h], writes=[ps])
            cc(vst[:, c, :, 0:64], ps[:, 0:256].rearrange("p (t e) -> p t e", e=64), [ps], [(vst, c)])
            S.op("act", lambda e, c=c, ps=ps: e.activation(junk[:, 0:128], ps[:, 256:384], AF.Square, accum_out=stt[:, c:c + 1]),
                 reads=[ps], writes=[junk, (stt, "ss")])
            rms_rstd(S, g, B, stt[:, c:c + 1], 1, stt[:, 8 + c:9 + c], [(stt, "ss")], 128)
            S.op("dve", lambda e, c=c, ps=ps: e.tensor_scalar(kn[:, c, :], ps[:, 256:384], stt[:, 8 + c:9 + c], None, ALU.mult),
                 reads=[ps, (stt, "rstd")], writes=[(kn, c)])
        S.dma("act", g.vtm[T * 4:(T + 1) * 4].rearrange("k p t e -> p k t e"), vst[:], reads=[vst], writes=["vtm"])
        transpose_to(S, g, kn, knT, nk=1)
        for hh in range(8):
            ps = rot()
            S.op("pe", lambda e, hh=hh, ps=ps: e.matmul(ps[0:64, :], Wukv[:, hh * 64:(hh + 1) * 64], knT[:, 0, :], start=True, stop=True),
                 reads=[Wukv, knT], writes=[ps])
            cc(knst[:, hh, :], ps[0:64, :], [ps], [(knst, hh)])
        S.dma("sp", g.knopeT.rearrange("h d s -> d h s")[:, :, cs], knst[:], reads=[knst], writes=["knopeT"])
        for c in range(4):
            ps = rot()
            S.op("pe", lambda e, c=c, ps=ps: e.matmul(ps[:], knT[:, 0, c * 128:(c + 1) * 128], Wukv[:, 512:1024], start=True, stop=True),
                 reads=[Wukv, knT], writes=[ps])
            cc(vmst[:, c, :, 0:64], ps[:].rearrange("p (h e) -> p h e", e=64), [ps], [(vmst, c)])
        S.dma("act", g.vmla[T * 4:(T + 1) * 4].rearrange("k p h e -> p k h e"), vmst[:], reads=[vmst], writes=["vmla"])
        S.dma("sp", ck[:], I["cosk"][:, cs], writes=[ck])
        S.dma("sp", sk[:], I["sink"][:, cs], writes=[sk])
        psa = rot()
        psb = rot()
        for k in range(8):
            S.op("pe", lambda e, k=k, psa=psa: e.matmul(psa[0:32, :], WinA[:, k, OKPE:OKPE + 32], h[:, k, :], start=(k == 0), stop=(k == 7)),
                 reads=[(WinA, k), h], writes=[psa])
        for k in range(8):
            S.op("pe", lambda e, k=k, psb=psb: e.matmul(psb[0:32, :], Wrot[:, k, :], h[:, k, :], start=(k == 0), stop=(k == 7)),
                 reads=[Wrot, h], writes=[psb])
        S.op("dve", lambda e, psa=psa: e.tensor_tensor(t1[0:32, :], psa[0:32, :], ck[:], ALU.mult), reads=[psa, ck], writes=[t1])
        S.op("dve", lambda e, psb=psb: e.tensor_tensor(t2[0:32, :], psb[0:32, :], sk[:], ALU.mult), reads=[psb, sk], writes=[t2])
        S.op("pool", lambda e: e.tensor_tensor(krst[:], t1[0:32, :], t2[0:32, :], ALU.add), reads=[t1, t2], writes=[krst])
        S.dma("sp", g.krotT[:, cs], krst[:], reads=[krst], writes=["krotT"])

    def qside(P, h):
        for hh in range(8):
            ps = rot()
            for k in range(8):
                S.op("pe", lambda e, k=k, ps=ps, hh=hh: e.matmul(ps[0:64, :], WinA[:, k, OQ + hh * 64:OQ + (hh + 1) * 64], h[:, k, :], start=(k == 0), stop=(k == 7)),
                     reads=[(WinA, k), h], writes=[ps])
            cc(qst[:, hh, :], ps[0:64, :], [ps], [(qst, hh)], scale=0.125)
        S.dma("sp", g.qnsaT[P], qst[:], reads=[qst], writes=["qnsaT"])
        for c in range(4):
            ps = rot()
            for k in range(8):
                S.op("pe", lambda e, k=k, c=c, ps=ps: e.matmul(ps[:, 0:280], h[:, k, c * 128:(c + 1) * 128], WinA[:, k, OCQ:OCQ + 280], start=(k == 0), stop=(k == 7)),
                     reads=[(WinA, k), h], writes=[ps])
            S.op("act", lambda e, c=c, ps=ps: e.activation(gst[:, c, :], ps[:, 256:280], AF.Sigmoid), reads=[ps], writes=[(gst, c)])
            S.op("act", lambda e, c=c, ps=ps: e.activation(junk[:, 0:256], ps[:, 0:256], AF.Square, accum_out=stt[:, c:c + 1]),
                 reads=[ps], writes=[junk, (stt, "ss")])
            rms_rstd(S, g, B, stt[:, c:c + 1], 1, stt[:, 8 + c:9 + c], [(stt, "ss")], 256)
            S.op("dve", lambda e, c=c, ps=ps: e.tensor_scalar(qn[:, c, :], ps[:, 0:256], stt[:, 8 + c:9 + c], None, ALU.mult),
                 reads=[ps, (stt, "rstd")], writes=[(qn, c)])
        S.dma("act", g.gnsa[P], gst[:], reads=[gst], writes=["gnsa"])
        transpose_to(S, g, qn, qnT, nk=2)
        qs = slice(P * 512, (P + 1) * 512)
        S.dma("sp", cq[64:96, :], I["cosq"][:, qs], writes=[cq])
        S.dma("sp", sq[64:96, :], I["sinq"][:, qs], writes=[sq])
        for hh in range(8):
            psa = rot()
            psb = rot()
            for k2 in range(2):
                S.op("pe", lambda e, k2=k2, psa=psa, hh=hh: e.matmul(psa[0:96, :], Wuq[:, k2, hh * 96:(hh + 1) * 96], qnT[:, k2, :], start=(k2 == 0), stop=(k2 == 1)),
                     reads=[Wuq, qnT], writes=[psa])
            for k2 in range(2):
                S.op("pe", lambda e, k2=k2, psb=psb, hh=hh: e.matmul(psb[0:96, :], Wuqr[:, k2, hh * 96:(hh + 1) * 96], qnT[:, k2, :], start=(k2 == 0), stop=(k2 == 1)),
                     reads=[Wuqr, qnT], writes=[psb])
            cc(qmst[0:64, hh, :], psa[0:64, :], [psa], [(qmst, hh)])
            S.op("dve", lambda e, psa=psa: e.tensor_tensor(t1[64:96, :], psa[64:96, :], cq[64:96, :], ALU.mult), reads=[psa, cq], writes=[t1])
            S.op("dve", lambda e, psb=psb: e.tensor_tensor(t2[64:96, :], psb[64:96, :], sq[64:96, :], ALU.mult), reads=[psb, sq], writes=[t2])
            S.op("pool", lambda e, hh=hh: e.tensor_tensor(qmst[64:96, hh, :], t1[64:96, :], t2[64:96, :], ALU.add), reads=[t1, t2], writes=[(qmst, hh)])
        S.dma("sp", g.qmlaT[P], qmst[:], reads=[qmst], writes=["qmlaT"])

    hAs = [hA, S.sbuf("hA2", [128, 8, 512], BF16)]
    hBs = [hB, S.sbuf("hB2", [128, 8, 512], BF16)]

    def load_pair(P):
        S.dma("sp", hAs[P % 2][:], g.hmTs[2 * P], reads=["hmTs"], writes=[hAs[P % 2]])
        S.dma("sp", hBs[P % 2][:], g.hmTs[2 * P + 1], reads=["hmTs"], writes=[hBs[P % 2]])

    load_pair(0)
    for P in range(NP):
        if P + 1 < NP:
            load_pair(P + 1)
        a_, b_ = hAs[P % 2], hBs[P % 2]

        def sel_k(k, a_=a_, b_=b_):
            S.op("act", lambda e: e.activation(hO[:, k, :], a_[:, k, :], AF.Copy, scale=cols[:, 12:13]), reads=[a_, (cols, 3)], writes=[(hO, k)])
            S.op("dve", lambda e: e.scalar_tensor_tensor(hO[:, k, :], b_[:, k, :], cols[:, 13:14], hO[:, k, :], ALU.mult, ALU.add),
                 reads=[b_, (hO, k), (cols, 3)], writes=[(hO, k)])

        kside(2 * P, a_, hook=sel_k)
        S.dma("act", g.hmTown[P], hO[:], reads=[hO], writes=["hmTown"])
        kside(2 * P + 1, b_)
        qside(P, hO)


def phase_1c(S, g, I):
    SEQ, NCT, NSEL = g.SEQ, g.NCT, g.NSEL
    NCP = NCT * 128
    rot = Rot(g.pG)
    w1b = [S.sbuf("w1b%d" % i, [64, 32, 256], BF16) for i in range(2)]
    w2b = [S.sbuf("w2b%d" % i, [128, 2, 64], BF16) for i in range(2)]
    posT = [S.sbuf("posT%d" % i, [64, 32], BF16) for i in range(2)]
    bias = [S.sbuf("cbias%d" % i, [128, 2], F32) for i in range(2)]
    srcT = [S.sbuf("csrcT%d" % i, [64, SEQ + 16], BF16) for i in range(2)]
    hid = S.sbuf("chid", [128, 2, NCP], BF16)
    u = S.sbuf("cu", [128, 512], F32)
    u2 = S.sbuf("cu2", [128, 512], F32)
    zz = S.sbuf("czz", [128, 512], F32)
    sgm = S.sbuf("csg", [128, 512], F32)
    kcst = S.sbuf("kcst", [64, NCP], BF16)
    vcst = S.sbuf("vcst", [128, NCT, 65 + NSEL], BF16)
    w2s = [S.sbuf("w2s%d" % i, [128, 2, 64], F32) for i in range(2)]
    pss = [S.sbuf("pss%d" % i, [64, 32], F32) for i in range(2)]
    stg = [S.sbuf("c1st%d" % i, [64, 16, 256], F32) for i in range(2)]
    S.op("pool", lambda e: e.memset(vcst[:], 1.0), writes=[vcst])
    S.dma("sp", vcst[:, :, 65:65 + NSEL], I["ovl"], reads=[], writes=[vcst])
    CH = min(512, NCP)
    order = [(0, 0), (0, 1), (1, 0), (1, 1)]
    S.dma("sp", srcT[0][:], g.kTs[0], reads=["kTs"], writes=[srcT[0]])
    for kvi, kvn in enumerate(("k", "v")):
        w1v = I["cmp_w1_" + kvn].rearrange("(l d) m -> d l m", d=64)
        for hh in range(2):
            S.dma(("sp", "act")[hh], stg[hh][:], w1v[:, hh * 16:(hh + 1) * 16, :], writes=[stg[hh]])
            cast_copy(S, hh, w1b[kvi][:, hh * 16:(hh + 1) * 16, :], stg[hh][:], [stg[hh]], [(w1b[kvi], hh)])
        S.dma("sp", w2s[kvi][:], I["cmp_w2_" + kvn].rearrange("(k p) d -> p k d", p=128), writes=[w2s[kvi]])
        cast_copy(S, 1, w2b[kvi][:], w2s[kvi][:], [w2s[kvi]], [w2b[kvi]])
        S.dma("sp", pss[kvi][:], I["cmp_posT_" + kvn], writes=[pss[kvi]])
        cast_copy(S, 1, posT[kvi][:], pss[kvi][:], [pss[kvi]], [posT[kvi]])
    for kvi in range(2):
        for mc in range(2):
            pb = rot()
            for l in range(32):
                S.op("pe", lambda e, l=l, mc=mc, pb=pb, kvi=kvi: e.matmul(pb[:, 0:1], w1b[kvi][:, l, mc * 128:(mc + 1) * 128], posT[kvi][:, l:l + 1], start=(l == 0), stop=(l == 31)),
                     reads=[w1b[kvi], posT[kvi]], writes=[pb])
            S.op("dve", lambda e, mc=mc, pb=pb, kvi=kvi: e.tensor_copy(bias[kvi][:, mc:mc + 1], pb[:, 0:1]), reads=[pb], writes=[(bias[kvi], mc)])
    for oi, (kvi, gi) in enumerate(order):
        src = srcT[oi % 2]
        if oi + 1 < len(order):
            nk, ng = order[oi + 1]
            S.dma("sp", srcT[(oi + 1) % 2][:], g.kTs[2 * nk + ng], reads=["kTs"], writes=[srcT[(oi + 1) % 2]])
        W1, W2, bs = w1b[kvi], w2b[kvi], bias[kvi]
        for mc in range(2):
            for b0 in range(0, NCP, CH):
                ps = rot()
                for l in range(32):
                    S.op("pe", lambda e, l=l, mc=mc, ps=ps, b0=b0, W1=W1, src=src: e.matmul(ps[:, 0:CH], W1[:, l, mc * 128:(mc + 1) * 128], src[:, l + 16 * b0: l + 16 * (b0 + CH - 1) + 1: 16], start=(l == 0), stop=(l == 31)),
                         reads=[W1, src], writes=[ps])
                S.op("dve", lambda e, mc=mc, ps=ps, bs=bs: e.tensor_scalar(u[:, 0:CH], ps[:, 0:CH], bs[:, mc:mc + 1], None, ALU.add), reads=[ps, (bs, mc)], writes=[u])
                S.op("act", lambda e: e.activation(u2[:, 0:CH], u[:, 0:CH], AF.Square), reads=[u], writes=[u2])
                S.op("dve", lambda e: e.tensor_scalar(u2[:, 0:CH], u2[:, 0:CH], 0.044715, 1.0, ALU.mult, ALU.add), reads=[u2], writes=[u2])
                S.op("dve", lambda e: e.tensor_tensor(zz[:, 0:CH], u[:, 0:CH], u2[:, 0:CH], ALU.mult), reads=[u, u2], writes=[zz])
                S.op("act", lambda e: e.activation(sgm[:, 0:CH], zz[:, 0:CH], AF.Sigmoid, scale=1.5957691216057308), reads=[zz], writes=[sgm])
                S.op("dve", lambda e, mc=mc, b0=b0: e.tensor_tensor(hid[:, mc, b0:b0 + CH], u[:, 0:CH], sgm[:, 0:CH], ALU.mult), reads=[u, sgm], writes=[(hid, mc)])
        if kvi == 0:
            for b0 in range(0, NCP, CH):
                ps = rot()
                for mc in range(2):
                    S.op("pe", lambda e, mc=mc, ps=ps, b0=b0, W2=W2: e.matmul(ps[0:64, 0:CH], W2[:, mc, :], hid[:, mc, b0:b0 + CH], start=(mc == 0), stop=(mc == 1)),
                         reads=[W2, hid], writes=[ps])
                cast_copy(S, 0, kcst[:, b0:b0 + CH], ps[0:64, 0:CH], [ps], [kcst])
            S.dma("sp", g.kcT[gi], kcst[:], reads=[kcst], writes=["kcT"])
        else:
            for it in range(NCT):
                ps = rot()
                for mc in range(2):
                    S.op("pe", lambda e, mc=mc, ps=ps, it=it, W2=W2: e.matmul(ps[:, 0:64], hid[:, mc, it * 128:(it + 1) * 128], W2[:, mc, :], start=(mc == 0), stop=(mc == 1)),
                         reads=[W2, hid], writes=[ps])
                cast_copy(S, it, vcst[:, it, 0:64], ps[:, 0:64], [ps], [vcst])
            S.dma("sp", g.vcaug[gi], vcst[:], reads=[vcst], writes=["vcaug"])


class Fin:
    def __init__(self, S, tag):
        self.S = S
        self.sc = S.sbuf(tag + "finsc", [128, 16], F32)

    def run(self, accs, gate_ap, y, col0, first, extra=None):
        S, sc = self.S, self.sc
        banks = []
        for b, _ in accs:
            if b not in banks:
                banks.append(b)
        for c in range(4):
            S.op("dve", lambda e, c=c: e.tensor_scalar(sc[:, c:c + 1], accs[c][1][:, 64:65], 1e-30, None, ALU.max),
                 reads=[accs[c][0]], writes=[(sc, "mx")])
        S.op("dve", lambda e: e.reciprocal(sc[:, 4:8], sc[:, 0:4]), reads=[(sc, "mx")], writes=[(sc, "rs")])
        if gate_ap is not None:
            S.op("dve", lambda e: e.tensor_tensor(sc[:, 8:12], sc[:, 4:8], gate_ap, ALU.mult), reads=[(sc, "rs"), self.gt], writes=[(sc, "cg")])
            off = 8
        else:
            off = 4
        for c in range(4):
            if first:
                S.op("dve", lambda e, c=c: e.tensor_scalar(y[:, c, col0:col0 + 64], accs[c][1][:, 0:64], sc[:, off + c:off + c + 1], None, ALU.mult),
                     reads=[accs[c][0], (sc, "cg"), (sc, "rs")], writes=[(y, c)])
            else:
                S.op("dve", lambda e, c=c: e.scalar_tensor_tensor(y[:, c, col0:col0 + 64], accs[c][1][:, 0:64], sc[:, off + c:off + c + 1], y[:, c, col0:col0 + 64], ALU.mult, ALU.add),
                     reads=[accs[c][0], (sc, "cg"), (sc, "rs"), (y, c)], writes=[(y, c)])
            if extra is not None:
                extra(c, sc[:, 4 + c:5 + c])


class Stream:
    def __init__(self, S, g, tag):
        self.S, self.g = S, g
        self.pT = [S.sbuf(tag + "pT%d" % i, [128, 2, 512], BF16) for i in range(3)]
        self.n = 0
        self.ntm = 0
        self.items = []

    def add(self, **kw):
        self.items.append(kw)

    def _score(self, j):
        S, g = self.S, self.g
        it = self.items[j]
        d = it.get("d", it["slot"] % 2)
        mask = it.get("mask")
        for ti, (lhsT, rhs, reads, extra) in enumerate(it["score"]):
            bank = g.pG[2 * d + ti]
            mms = [(lhsT, rhs, reads)]
            if extra is not None:
                mms.append(extra)
            if mask is not None:
                mms.append((g.ident_bf[:], mask[0][:, ti, :], [g.ident_bf, mask[1]]))
            for mi, (l_, r_, rd_) in enumerate(mms):
                S.op("pe", lambda e, bank=bank, l_=l_, r_=r_, mi=mi, n=len(mms): e.matmul(bank[:], l_, r_, start=(mi == 0), stop=(mi == n - 1)),
                     reads=rd_, writes=[bank])

    def _exp(self, j):
        S, g = self.S, self.g
        it = self.items[j]
        d = it.get("d", it["slot"] % 2)
        nt = len(it["score"])
        pt = self.pT[it["slot"] % 3]
        banks = [g.pG[2 * d + ti] for ti in range(nt)]
        src = g.pD[d].h[:, 0:nt * 512].rearrange("p (t n) -> p t n", n=512)
        scale = it.get("scale", 1.0)
        S.op("act", lambda e: e.activation(pt[:, 0:nt, :], src, AF.Exp, scale=scale), reads=banks, writes=[pt])
        it["pt"] = pt

    def _pv(self, j):
        S = self.S
        it = self.items[j]
        pt = it["pt"]
        nt = len(it["score"])
        for ti in range(nt):
            v_ap, v_reads = it["pv"][ti]
            first = it["first"] and ti == 0
            last = it["last"] and ti == nt - 1
            for c in range(4):
                bank, out_ap, lead = it["acc"][c]
                S.op("pe", lambda e, out_ap=out_ap, pt=pt, ti=ti, c=c, v_ap=v_ap, first=first, last=last, lead=lead:
                     e.matmul(out_ap, pt[:, ti, c * 128:(c + 1) * 128], v_ap, start=(first and lead), stop=last, skip_group_check=True),
                     reads=[pt] + v_reads, writes=[bank])

    def run(self):
        items = self.items
        n = len(items)
        for j, it in enumerate(items):
            it["slot"] = self.n + j
        pending = []
        if n:
            self._score(0)
        fixed = any("d" in it for it in items)
        for j in range(n):
            if fixed:
                self._exp(j)
                if j + 1 < n:
                    self._score(j + 1)
            else:
                if j + 1 < n:
                    self._score(j + 1)
                self._exp(j)
            self._pv(j)
            pending = [(d - 1, f) for d, f in pending]
            for d, f in pending:
                if d <= 0:
                    f()
            pending = [(d, f) for d, f in pending if d > 0]
            if items[j].get("after") is not None:
                pending.append((items[j].get("defer", 2), items[j]["after"]))
        for d, f in pending:
            f()
        self.n += n
        self.items = []


def acc_views(bank, ncols=65):
    a4 = bank[:].rearrange("p (c e) -> p c e", e=128)
    return [(bank, a4[:, c, 0:ncols], c == 0) for c in range(4)]


def phase_2a(S, g, I):
    SEQ, NP, NKT, NSEL, NCT = g.SEQ, g.NP, g.NKT, g.NSEL, g.NCT
    NCP = NCT * 128
    NV = 65 + NSEL
    accb = [g.pG[4], g.pG[5]]
    Kslc = [S.sbuf("Kslc%d" % i, [68, SEQ], BF16) for i in range(2)]
    Kwin = [S.sbuf("Kwin%d" % i, [68, SEQ], BF16) for i in range(2)]
    V4 = S.sbuf("V4", [128, NKT, 4, 65], BF16)
    Kc = [S.sbuf("Kc%d" % i, [68, NCP], BF16) for i in range(2)]
    Vc = [S.sbuf("Vc%d" % i, [128, NCT, NV], BF16) for i in range(2)]
    Et = S.sbuf("Et", [NSEL, NKT, 128], BF16)
    dmask = S.sbuf("dmask", [128, 8, 512], BF16)
    wmask = S.sbuf("wmask", [128, 12, 512], BF16)
    def load_residents():
        for gi in range(2):
            S.dma("sp", Kc[gi][0:64, :], g.kcT[gi], reads=["kcT"], writes=[Kc[gi]])
            S.dma("sp", Kc[gi][64:68, :], I["kaug_c"], writes=[Kc[gi]])
            S.dma("act", Vc[gi][:], g.vcaug[gi], reads=["vcaug"], writes=[Vc[gi]])
        load_slot(0)
        S.dma("act", wmask[:], I["wmask"], writes=[wmask])
        for gi in range(2):
            S.dma("act", Kwin[gi][0:64, :], g.kTs[6 + gi][:, 0:SEQ], reads=["kTs"], writes=[Kwin[gi]])
            S.dma("act", Kwin[gi][64:68, :], I["kaug"], writes=[Kwin[gi]])
        for k0 in range(0, NKT, 16):
            k1 = min(NKT, k0 + 16)
            S.dma("sp", V4[:, k0:k1], g.vtm[k0:k1].rearrange("k p t e -> p k t e"), reads=["vtm"], writes=[(V4, k0 // 16)])
        S.dma("sp", dmask[:], I["dmask"], writes=[dmask])
        for gi in range(2):
            S.dma("sp", Kslc[gi][0:64, :], g.kTs[4 + gi][:, 0:SEQ], reads=["kTs"], writes=[Kslc[gi]])
            S.dma("sp", Kslc[gi][64:68, :], I["kaug"], writes=[Kslc[gi]])
        S.dma("act", Et[:], I["E"], writes=[Et])

    Qa = [S.sbuf("Qa%d" % i, [68, 8, 512], BF16) for i in range(2)]
    cm = [S.sbuf("cm%d" % i, [128, NCT, 512], BF16) for i in range(2)]
    bon = [S.sbuf("bon%d" % i, [128, 4, NSEL], F32) for i in range(2)]
    gt = [S.sbuf("gt%d" % i, [128, 4, 24], F32) for i in range(2)]
    y = S.sbuf("ynsa", [128, 4, 512], F32)
    imp = [S.sbuf("imp%d" % i, [128, 4, NSEL], F32) for i in range(2)]
    selT = S.sbuf("selT", [NSEL, 2, 512], BF16)
    impb = [S.sbuf("impb%d" % i, [128, NSEL], F32) for i in range(8)]
    wk = [S.sbuf("wk%d" % i, [128, NSEL], F32) for i in range(8)]
    m8 = [S.sbuf("m8_%d" % i, [128, 16], F32) for i in range(8)]
    sn = [S.sbuf("sn%d" % i, [128, NSEL], BF16) for i in range(8)]
    ybf = S.sbuf("ybf", [128, 4, 512], BF16)
    yT = S.sbuf("yT", [128, 4, 512], BF16)
    fin = Fin(S, "a")
    st = Stream(S, g, "a")
    hd = [0]

    def load_slot(P):
        b = P % 2
        S.dma("sp", Qa[b][0:64], g.qnsaT[P], reads=["qnsaT"], writes=[Qa[b]])
        S.dma("sp", Qa[b][64:68], I["qaug"][P], writes=[Qa[b]])
        S.dma("act", cm[b][:], I["cmask"][P], writes=[cm[b]])
        S.dma("act", bon[b][:], I["bonus"][P], writes=[bon[b]])
        S.dma("act", gt[b][:], g.gnsa[P], reads=["gnsa"], writes=[gt[b]])

    load_residents()
    for P in range(NP):
        b = P % 2
        if P + 1 < NP:
            load_slot(P + 1)
        Q, cmk, bn, gates = Qa[b], cm[b], bon[b], gt[b]
        fin.gt = gates
        ncmp = min(NCT, ((2 * P + 2) * 32 + 127) // 128)
        csets = []
        for si in range(2):
            b0, b1 = g.pG[2 + 2 * si], g.pG[3 + 2 * si]
            A = b0[:].rearrange("p (c e) -> p c e", e=256)
            Bk = b1[:].rearrange("p (c e) -> p c e", e=256)
            csets.append(((b0, b1), (A, Bk)))
        for gi in range(2):
            for hh in range(4):
                head = 4 * gi + hh
                (bks, vws) = csets[head % 2]
                cacc = [(bks[c // 2], vws[c // 2][:, c % 2, 0:NV], c % 2 == 0) for c in range(4)]
                groups = [list(range(t0, min(ncmp, t0 + 2))) for t0 in range(0, ncmp, 2)]

                def after(gi=gi, hh=hh, head=head, gates=gates, bks=bks, vws=vws):
                    accs = [(bks[c // 2], vws[c // 2][:, c % 2, :]) for c in range(4)]

                    def extra(c, rs_ap):
                        if hh == 0:
                            S.op("dve", lambda e: e.tensor_scalar(imp[gi][:, c, :], accs[c][1][:, 65:NV], rs_ap, None, ALU.mult),
                                 reads=[accs[c][0], (fin.sc, "rs")], writes=[(imp[gi], c)])
                        else:
                            S.op("dve", lambda e: e.scalar_tensor_tensor(imp[gi][:, c, :], accs[c][1][:, 65:NV], rs_ap, imp[gi][:, c, :], ALU.mult, ALU.add),
                                 reads=[accs[c][0], (fin.sc, "rs"), (imp[gi], c)], writes=[(imp[gi], c)])

                    fin.gt = gates
                    fin.run(accs, gates[:, :, head * 3 + 0], y, head * 64, True, extra)

                for gidx, tl in enumerate(groups):
                    st.add(score=[(Kc[gi][0:68, it * 128:(it + 1) * 128], Q[0:68, head, :], [Kc[gi], Q], None) for it in tl],
                           mask=(cmk[:, tl[0]:tl[-1] + 1, :], cmk), d=0, defer=1,
                           pv=[(Vc[gi][:, it, :], [Vc[gi]]) for it in tl],
                           acc=cacc, first=(gidx == 0), last=(gidx == len(groups) - 1),
                           after=(after if gidx == len(groups) - 1 else None))
        st.run()
        chains = [(gi, c) for gi in range(2) for c in range(4)]
        for i, (gi, c) in enumerate(chains):
            S.op("dve", lambda e, i=i, c=c, gi=gi, bn=bn: e.tensor_tensor(impb[i][:], imp[gi][:, c, :], bn[:, c, :], ALU.add), reads=[(imp[gi], c), bn], writes=[impb[i]])
        for i in range(8):
            S.op("dve", lambda e, i=i: e.max(m8[i][:, 0:8], impb[i][:]), reads=[impb[i]], writes=[(m8[i], 0)])
        for i in range(8):
            S.op("dve", lambda e, i=i: e.match_replace(wk[i][:], m8[i][:, 0:8], impb[i][:], -3.0e38), reads=[impb[i], (m8[i], 0)], writes=[wk[i]])
        for i in range(8):
            S.op("dve", lambda e, i=i: e.max(m8[i][:, 8:16], wk[i][:]), reads=[wk[i]], writes=[(m8[i], 1)])
        for i in range(8):
            S.op("dve", lambda e, i=i: e.tensor_scalar(sn[i][:], impb[i][:], m8[i][:, 15:16], NEGM, ALU.is_lt, ALU.mult), reads=[impb[i], (m8[i], 1)], writes=[sn[i]])
        for i, (gi, c) in enumerate(chains):
            ptg = g.pTb[gi]
            S.op("pe", lambda e, i=i, c=c, ptg=ptg: e.transpose(ptg[0:NSEL, c * 128:(c + 1) * 128], sn[i][:], g.ident_bf[:]), reads=[sn[i], g.ident_bf], writes=[ptg])
        for gi in range(2):
            ptg = g.pTb[gi]
            if gi == 0:
                S.op("act", lambda e, gi=gi, ptg=ptg: e.activation(selT[:, gi, :], ptg[0:NSEL, 0:512], AF.Copy), reads=[ptg], writes=[(selT, gi)])
            else:
                S.op("dve", lambda e, gi=gi, ptg=ptg: e.tensor_copy(selT[:, gi, :], ptg[0:NSEL, 0:512]), reads=[ptg], writes=[(selT, gi)])
        for gi in range(2):
            for hh in range(4):
                head = 4 * gi + hh
                bank = accb[hd[0] % 2]
                hd[0] += 1
                av = acc_views(bank)

                def after_w(bank=bank, head=head, gates=gates):
                    a4 = bank[:].rearrange("p (c e) -> p c e", e=128)
                    fin.gt = gates
                    fin.run([(bank, a4[:, c, :]) for c in range(4)], gates[:, :, head * 3 + 2], y, head * 64, False)

                kts = [kt for kt in range((2 * P - 1) * 4, (2 * P + 2) * 4) if kt >= 0]
                groups = [kts[i:i + 2] for i in range(0, len(kts), 2)]
                for gidx, tl in enumerate(groups):
                    r = tl[0] - (2 * P - 1) * 4
                    st.add(score=[(Kwin[gi][0:68, kt * 128:(kt + 1) * 128], Q[0:68, head, :], [Kwin[gi], Q], None) for kt in tl],
                           mask=(wmask[:, r:r + 2, :], wmask),
                           pv=[(V4[:, kt, 2 + gi, :], [(V4, kt // 16)]) for kt in tl],
                           acc=av, first=(gidx == 0), last=(gidx == len(groups) - 1),
                           after=(after_w if gidx == len(groups) - 1 else None))
        nkt = (2 * P + 2) * 4
        for gi in range(2):
            for hh in range(4):
                head = 4 * gi + hh
                bank = accb[hd[0] % 2]
                hd[0] += 1
                av = acc_views(bank)

                def after_s(bank=bank, head=head, gates=gates):
                    a4 = bank[:].rearrange("p (c e) -> p c e", e=128)
                    fin.gt = gates
                    fin.run([(bank, a4[:, c, :]) for c in range(4)], gates[:, :, head * 3 + 1], y, head * 64, False)

                dmax = int(np.ceil(ALIBI_CUT * 2.0 ** (head + 1)))
                kt_min = max(0, (2 * P * 512 - dmax) // 128) // 2 * 2
                kt_min = min(kt_min, nkt - 8)
                groups = [list(range(t0, t0 + 2)) for t0 in range(kt_min, nkt, 2)]
                for gidx, tl in enumerate(groups):
                    r = tl[0] - (nkt - 8)
                    st.add(score=[(Kslc[gi][0:68, kt * 128:(kt + 1) * 128], Q[0:68, head, :], [Kslc[gi], Q],
                                   (Et[:, kt, :], selT[:, gi, :], [Et, (selT, gi)])) for kt in tl],
                           mask=((dmask[:, r:r + 2, :], dmask) if r >= 0 else None),
                           pv=[(V4[:, kt, gi, :], [(V4, kt // 16)]) for kt in tl],
                           acc=av, first=(gidx == 0), last=(gidx == len(groups) - 1),
                           after=(after_s if gidx == len(groups) - 1 else None))
        st.run()
        S.op("act", lambda e: e.activation(ybf[:], y[:], AF.Copy), reads=[y], writes=[ybf])
        transpose_to(S, g, ybf, yT, nk=4)
        S.dma("sp", g.ynsaT[P], yT[:], reads=[yT], writes=["ynsaT"])
        if g.debug:
            S.dma("sp", g.dbg_y[P], y[:], reads=[y], writes=["dbg_y"])
            S.dma("sp", g.dbg_sel[P], selT[:], reads=[selT], writes=["dbg_sel"])


def phase_2b(S, g, I, hs):
    SEQ, NP, NKT = g.SEQ, g.NP, g.NKT
    scale = 96.0 ** -0.5
    accb = [g.pG[4], g.pG[5]]
    Kmla = S.sbuf("Kmla", [96, 4, SEQ], BF16)
    Vm = S.sbuf("Vm", [128, NKT, 4, 65], BF16)
    dmask = S.sbuf("dmaskb", [128, 8, 512], BF16)
    def load_k(j):
        S.dma("act", Kmla[0:64, j, :], g.knopeT[4 * hs + j], reads=["knopeT"], writes=[(Kmla, j)])
        S.dma("act", Kmla[64:96, j, :], g.krotT, reads=["krotT"], writes=[(Kmla, j)])

    load_k(0)
    S.dma("act", dmask[:], I["dmask"], writes=[dmask])
    for k0 in range(0, NKT, 16):
        k1 = min(NKT, k0 + 16)
        S.dma("sp", Vm[:, k0:k1], g.vmla[k0:k1, :, 4 * hs:4 * hs + 4, :].rearrange("k p h e -> p k h e"), reads=["vmla"], writes=[(Vm, k0 // 16)])
    for j in range(1, 4):
        load_k(j)
    Qm = [S.sbuf("Qm%d" % i, [96, 4, 512], BF16) for i in range(2)]
    y = [S.sbuf("ymla%d" % i, [128, 4, 256], F32) for i in range(2)]
    ybf = S.sbuf("ybfb", [128, 4, 256], BF16)
    yT = S.sbuf("yTb", [128, 2, 512], BF16)
    fin = Fin(S, "b%d" % hs)
    st = Stream(S, g, "b%d" % hs)
    hd = [0]
    S.dma("sp", Qm[0][:], g.qmlaT[0][:, 4 * hs:4 * hs + 4, :], reads=["qmlaT"], writes=[Qm[0]])
    for P in range(NP):
        b = P % 2
        if P + 1 < NP:
            S.dma("sp", Qm[1 - b][:], g.qmlaT[P + 1][:, 4 * hs:4 * hs + 4, :], reads=["qmlaT"], writes=[Qm[1 - b]])
        Q, yy = Qm[b], y[b]
        nkt = (2 * P + 2) * 4
        for j in range(4):
            bank = accb[hd[0] % 2]
            hd[0] += 1
            av = acc_views(bank)

            def after(bank=bank, j=j, yy=yy):
                a4 = bank[:].rearrange("p (c e) -> p c e", e=128)
                fin.run([(bank, a4[:, c, :]) for c in range(4)], None, yy, j * 64, True)

            groups = [list(range(t0, t0 + 2)) for t0 in range(0, nkt, 2)]
            for gidx, tl in enumerate(groups):
                r = tl[0] - (nkt - 8)
                st.add(score=[(Kmla[0:96, j, kt * 128:(kt + 1) * 128], Q[0:96, j, :], [(Kmla, j), Q], None) for kt in tl],
                       mask=((dmask[:, r:r + 2, :], dmask) if r >= 0 else None), scale=scale,
                       pv=[(Vm[:, kt, j, :], [(Vm, kt // 16)]) for kt in tl],
                       acc=av, first=(gidx == 0), last=(gidx == len(groups) - 1),
                       after=(after if gidx == len(groups) - 1 else None))
        st.run()
        S.op("act", lambda e, yy=yy: e.activation(ybf[:], yy[:], AF.Copy), reads=[yy], writes=[ybf])
        transpose_to(S, g, ybf, yT, nk=2)
        S.dma("sp", g.ymlaT[P][:, 2 * hs:2 * hs + 2, :], yT[:], reads=[yT], writes=["ymlaT"])


def phase_3a(S, g, I):
    NP = g.NP
    rot = Rot(g.pG)
    Wpn = S.sbuf("Wpn", [128, 4, D], BF16)
    Wpm = S.sbuf("Wpm", [128, 4, D], BF16)
    Wo = S.sbuf("Wo", [128, 8, D], BF16)
    Wgm = S.sbuf("Wgm", [128, 8, 2048], BF16)
    gpost = S.sbuf("gpostM", [128, D], F32)
    cols = S.sbuf("cols3a", [128, 16], F32)
    S.dma("sp", cols[:, 0:8], I["mix_precol"], writes=[(cols, 0)])
    S.dma("sp", cols[:, 12:14], I["hm"], writes=[(cols, 3)])
    S.dma("sp", gpost[:], I["mix_postrow"], writes=[gpost])
    m0 = S.mark()
    st = [S.sbuf("st3a%d" % i, [128, 2048], F32) for i in range(3)]
    i = 0

    def load_plain(dst, src, nk):
        nonlocal i
        sv = src.rearrange("(k p) n -> p k n", p=128)
        for k0 in range(0, nk, 2):
            s_ = st[i % 3]
            S.dma(("sp", "act")[i % 2], s_[:].rearrange("p (j n) -> p j n", j=2), sv[:, k0:k0 + 2, :], writes=[s_])
            cast_copy(S, i, dst[:, k0:k0 + 2, :], s_[:].rearrange("p (j n) -> p j n", j=2), [s_], [dst])
            i += 1

    load_plain(Wpn, I["w_proj_nsa"], 4)
    load_plain(Wpm, I["w_proj_mla"], 4)
    hOs = [S.sbuf("hO3%d" % i, [128, 8, 512], BF16) for i in range(2)]
    ynTs = [S.sbuf("ynT%d" % i, [128, 4, 512], BF16) for i in range(2)]
    ymTs = [S.sbuf("ymT%d" % i, [128, 4, 512], BF16) for i in range(2)]
    xAs = [S.sbuf("x1A%d" % i, [128, 4, D], F32) for i in range(2)]
    xBs = [S.sbuf("x1B%d" % i, [128, 4, D], F32) for i in range(2)]
    mg = S.sbuf("mg", [128, 8, 512], BF16)
    ga = S.sbuf("ga", [128, 512], F32)
    gb = S.sbuf("gb", [128, 512], F32)
    t1 = S.sbuf("t13", [128, 512], F32)
    t2 = S.sbuf("t23", [128, 512], F32)
    junk = S.sbuf("junk3", [128, 512], BF16)
    B = {"st": S.sbuf("stat3", [128, 32], F32)}
    stt = B["st"]

    def load_slot(P):
        b = P % 2
        S.dma("sp", hOs[b][:], g.hmTown[P], reads=["hmTown"], writes=[hOs[b]])
        S.dma("sp", ynTs[b][:], g.ynsaT[P], reads=["ynsaT"], writes=[ynTs[b]])
        S.dma("sp", ymTs[b][:], g.ymlaT[P], reads=["ymlaT"], writes=[ymTs[b]])
        S.dma("sp", xAs[b][:], g.x1s[2 * P], reads=["x1s"], writes=[xAs[b]])
        S.dma("sp", xBs[b][:], g.x1s[2 * P + 1], reads=["x1s"], writes=[xBs[b]])

    load_slot(0)
    for k in range(8):
        s_ = st[i % 3]
        S.dma(("sp", "act")[i % 2], s_[:], I["w_in_p"][k * 128:(k + 1) * 128, OGM:OGM + 2048], writes=[s_])
        cast_copy(S, i, Wgm[:, k, :], s_[:], [s_, (cols, 0)], [Wgm], scale=cols[:, k:k + 1])
        i += 1
    load_plain(Wo, I["w_out"], 8)
    for P in range(NP):
        if P + 1 < NP:
            load_slot(P + 1)
        hO, ynT, ymT, xA, xB = hOs[P % 2], ynTs[P % 2], ymTs[P % 2], xAs[P % 2], xBs[P % 2]
        for c in range(4):
            S.op("act", lambda e, c=c, xA=xA: e.activation(xA[:, c, :], xA[:, c, :], AF.Copy, scale=cols[:, 12:13]), reads=[(xA, c), (cols, 3)], writes=[(xA, c)])
            S.op("dve", lambda e, c=c, xA=xA, xB=xB: e.scalar_tensor_tensor(xA[:, c, :], xB[:, c, :], cols[:, 13:14], xA[:, c, :], ALU.mult, ALU.add),
                 reads=[(xB, c), (xA, c), (cols, 3)], writes=[(xA, c)])
        for m in range(8):
            ms = slice(m * 128, (m + 1) * 128)
            pn, pm, pa, pb = rot(), rot(), rot(), rot()
            for f in range(4):
                S.op("pe", lambda e, f=f, pn=pn, ms=ms, ynT=ynT: e.matmul(pn[:], Wpn[:, f, ms], ynT[:, f, :], start=(f == 0), stop=(f == 3)), reads=[Wpn, ynT], writes=[pn])
            for f in range(4):
                S.op("pe", lambda e, f=f, pm=pm, ms=ms, ymT=ymT: e.matmul(pm[:], Wpm[:, f, ms], ymT[:, f, :], start=(f == 0), stop=(f == 3)), reads=[Wpm, ymT], writes=[pm])
            for k in range(8):
                S.op("pe", lambda e, k=k, pa=pa, m=m, hO=hO: e.matmul(pa[:], Wgm[:, k, m * 128:(m + 1) * 128], hO[:, k, :], start=(k == 0), stop=(k == 7)), reads=[Wgm, hO], writes=[pa])
            for k in range(8):
                S.op("pe", lambda e, k=k, pb=pb, m=m, hO=hO: e.matmul(pb[:], Wgm[:, k, 1024 + m * 128:1024 + (m + 1) * 128], hO[:, k, :], start=(k == 0), stop=(k == 7)), reads=[Wgm, hO], writes=[pb])
            S.op("act", lambda e, pa=pa: e.activation(ga[:], pa[:], AF.Sigmoid), reads=[pa], writes=[ga])
            S.op("act", lambda e, pb=pb: e.activation(gb[:], pb[:], AF.Sigmoid), reads=[pb], writes=[gb])
            S.op("dve", lambda e, pn=pn: e.tensor_tensor(t1[:], ga[:], pn[:], ALU.mult), reads=[ga, pn], writes=[t1])
            S.op("dve", lambda e, pm=pm: e.tensor_tensor(t2[:], gb[:], pm[:], ALU.mult), reads=[gb, pm], writes=[t2])
            S.op("dve", lambda e, m=m: e.tensor_tensor(mg[:, m, :], t1[:], t2[:], ALU.add), reads=[t1, t2], writes=[(mg, m)])
        for c in range(4):
            po = [rot(), rot()]
            for nh in range(2):
                for m in range(8):
                    S.op("pe", lambda e, c=c, nh=nh, m=m, po=po: e.matmul(po[nh][:], mg[:, m, c * 128:(c + 1) * 128], Wo[:, m, nh * 512:(nh + 1) * 512], start=(m == 0), stop=(m == 7)),
                         reads=[(mg, m), Wo], writes=[po[nh]])
                S.op("act", lambda e, nh=nh, po=po: e.activation(junk[:], po[nh][:], AF.Square, accum_out=stt[:, 4 + nh:5 + nh]), reads=[po[nh]], writes=[junk, (stt, "ss2")])
            S.op("dve", lambda e: e.tensor_tensor(stt[:, 6:7], stt[:, 4:5], stt[:, 5:6], ALU.add), reads=[(stt, "ss2")], writes=[(stt, "ss2s")])
            rms_rstd(S, g, B, stt[:, 6:7], 1, stt[:, 12:13], [(stt, "ss2s")], D)
            for nh in range(2):
                S.op("dve", lambda e, nh=nh, po=po: e.scalar_tensor_tensor(t1[:], po[nh][:], stt[:, 12:13], gpost[:, nh * 512:(nh + 1) * 512], ALU.mult, ALU.mult),
                     reads=[po[nh], (stt, "rstd"), gpost], writes=[t1])
                S.op("dve", lambda e, c=c, nh=nh, xA=xA: e.tensor_tensor(xA[:, c, nh * 512:(nh + 1) * 512], t1[:], xA[:, c, nh * 512:(nh + 1) * 512], ALU.add),
                     reads=[t1, (xA, c)], writes=[(xA, c)])
        S.dma("sp", g.x2s[P], xA[:], reads=[xA], writes=["x2s"])


def dram_in(nc, name, shape, dt):
    return nc.dram_tensor(name, list(shape), dt, kind="ExternalInput").ap()


def build(SEQ, debug=False, stop_after=None):
    NT = SEQ // 512
    NP = NT // 2
    NKT = SEQ // 128
    NSEL = SEQ // 64
    NC = SEQ // 16
    NCT = max(1, NC // 128)
    nc = bass.Bass("TRN2", target_bir_lowering=False)
    S = Sched(nc)
    g = G()
    g.nc, g.S = nc, S
    g.SEQ, g.NT, g.NP, g.NKT, g.NSEL, g.NC, g.NCT = SEQ, NT, NP, NKT, NSEL, NC, NCT
    I = {}

    def inp(name, shape, dt=F32):
        I[name] = dram_in(nc, name, shape, dt)
        return I[name]

    nc.in_names = I

    def scratch(name, shape, dt):
        kind = "ExternalOutput" if debug else "Internal"
        return nc.dram_tensor(name, list(shape), dt, kind=kind).ap()

    inp("x", [SEQ, D])
    for f in ("ff1", "ff2"):
        inp(f + "_wg", [D, DFF]); inp(f + "_wu", [D, DFF]); inp(f + "_wd", [DFF, D])
        inp(f + "_precol", [128, 8]); inp(f + "_postrow", [128, D])
    inp("mix_precol", [128, 8]); inp("mix_postrow", [128, D])
    inp("ident_bf", [128, 128], BF16); inp("ident_f", [128, 128])
    out = nc.dram_tensor("out", [NP * 512, D], F32, kind="ExternalOutput").ap()

    inp("w_in_p", [D, 3768]); inp("mla_w_uq", [256, 768]); inp("mla_w_ukv_p", [128, 1024])
    inp("q_norm_col", [128, 2]); inp("kv_norm_col", [128, 1]); inp("hm", [128, 2])
    inp("cosk", [32, SEQ]); inp("sink", [32, SEQ]); inp("cosq", [32, NP * 512]); inp("sinq", [32, NP * 512])

    g.x1s = scratch("x1s", [NT, 128, 4, D], F32)
    g.hmTs = scratch("hmTs", [NT, 128, 8, 512], BF16)
    g.hmTown = scratch("hmTown", [NP, 128, 8, 512], BF16)
    g.kTs = scratch("kTs", [8, 64, SEQ + 16], BF16)
    g.vtm = scratch("vtm", [NKT, 128, 4, 65], BF16)
    g.knopeT = scratch("knopeT", [8, 64, SEQ], BF16)
    g.krotT = scratch("krotT", [32, SEQ], BF16)
    g.vmla = scratch("vmla", [NKT, 128, 8, 65], BF16)
    g.qnsaT = scratch("qnsaT", [NP, 64, 8, 512], BF16)
    g.qmlaT = scratch("qmlaT", [NP, 96, 8, 512], BF16)
    g.gnsa = scratch("gnsa", [NP, 128, 4, 24], F32)
    for kvn in ("k", "v"):
        inp("cmp_w1_" + kvn, [2048, 256]); inp("cmp_w2_" + kvn, [256, 64]); inp("cmp_posT_" + kvn, [64, 32])
    inp("ovl", [128, NCT, NSEL], BF16)
    inp("kaug", [4, SEQ], BF16); inp("kaug_c", [4, NCT * 128], BF16); inp("qaug", [NP, 4, 8, 512], BF16)
    inp("dmask", [128, 8, 512], BF16); inp("wmask", [128, 12, 512], BF16); inp("cmask", [NP, 128, NCT, 512], BF16)
    inp("bonus", [NP, 128, 4, NSEL]); inp("E", [NSEL, NKT, 128], BF16)
    inp("w_proj_nsa", [512, D]); inp("w_proj_mla", [512, D]); inp("w_out", [D, D])
    g.x2s = scratch("x2s", [NP, 128, 4, D], F32)
    g.ynsaT = scratch("ynsaT", [NP, 128, 4, 512], BF16)
    g.ymlaT = scratch("ymlaT", [NP, 128, 4, 512], BF16)
    g.debug = debug
    if debug:
        g.dbg_y = scratch("dbg_y", [NP, 128, 4, 512], F32)
        g.dbg_sel = scratch("dbg_sel", [NP, NSEL, 2, 512], BF16)
    g.kcT = scratch("kcT", [2, 64, NCT * 128], BF16)
    g.vcaug = scratch("vcaug", [2, 128, NCT, 65 + NSEL], BF16)

    g.pD = [S.psum("pD%d" % i, [128, 1024], F32) for i in range(3)]
    g.pG = [T("pG%d" % i, g.pD[i // 2].h[:, (i % 2) * 512:(i % 2 + 1) * 512]) for i in range(6)]
    g.pTb = [S.psum("pTb%d" % i, [128, 1024], BF16) for i in range(2)]

    g.ident_bf = S.sbuf("ident_bf", [128, 128], BF16)
    g.ident_f = S.sbuf("ident_f", [128, 128], F32)
    S.dma("sp", g.ident_bf[:], I["ident_bf"], writes=[g.ident_bf])
    S.dma("sp", g.ident_f[:], I["ident_f"], writes=[g.ident_f])
    base = S.mark()

    W = load_ffn_weights(S, g, I["ff1_wg"], I["ff1_wu"], I["ff1_wd"], I["ff1_precol"], I["ff1_postrow"], "f1")
    B = alloc_ffn_work(S, "f1")
    xt = S.sbuf("xt", [128, 4, D], F32)
    xv = I["x"].rearrange("(t c p) d -> t p c d", c=4, p=128)
    st1 = B["st"]
    for c in range(4):
        S.dma("sp", xt[:, c, :], xv[0][:, c, :], writes=[(xt, c)])
    for t in range(NT):
        def hmix_tr(c):
            pt = g.pTb[c % 2]
            for k in range(8):
                S.op("pe", lambda e, k=k: e.transpose(pt[:, k * 128:(k + 1) * 128], B["xn"][:, c, k * 128:(k + 1) * 128], g.ident_bf[:]),
                     reads=[(B["xn"], c), g.ident_bf], writes=[pt])
            dst = B["xnT"][:, :, c * 128:(c + 1) * 128]
            srcp = pt[:, 0:1024].rearrange("p (k n) -> p k n", n=128)
            if c % 2 == 0:
                S.op("dve", lambda e: e.tensor_copy(dst, srcp), reads=[pt], writes=[B["xnT"]])
            else:
                S.op("act", lambda e: e.activation(dst, srcp, AF.Copy), reads=[pt], writes=[B["xnT"]])

        def after_chunk(c, t=t):
            S.dma("sp", g.x1s[t][:, c, :], xt[:, c, :], reads=[(xt, c)], writes=["x1s"])
            S.op("act", lambda e: e.activation(B["junk"][:], xt[:, c, :], AF.Square, accum_out=st1[:, 20 + c:21 + c]),
                 reads=[(xt, c)], writes=[B["junk"], (st1, "hss%d" % c)])
            S.op("dve", lambda e: e.tensor_scalar(st1[:, 28:29], st1[:, 20 + c:21 + c], 1.0 / D, EPS, ALU.mult, ALU.add), reads=[(st1, "hss%d" % c)], writes=[(st1, "hms")])
            S.op("act", lambda e: e.activation(st1[:, 29:30], st1[:, 28:29], AF.Sqrt), reads=[(st1, "hms")], writes=[(st1, "hsd")])
            S.op("dve", lambda e: e.reciprocal(st1[:, 30:31], st1[:, 29:30]), reads=[(st1, "hsd")], writes=[(st1, "hrs")])
            if c % 2 == 0:
                S.op("act", lambda e: e.activation(B["xn"][:, c, :], xt[:, c, :], AF.Copy, scale=st1[:, 30:31]), reads=[(xt, c), (st1, "hrs")], writes=[(B["xn"], c)])
            else:
                S.op("dve", lambda e: e.tensor_scalar(B["xn"][:, c, :], xt[:, c, :], st1[:, 30:31], None, ALU.mult), reads=[(xt, c), (st1, "hrs")], writes=[(B["xn"], c)])
            if t + 1 < NT:
                S.dma("sp", xt[:, c, :], xv[t + 1][:, c, :], writes=[(xt, c)])
            if c > 0:
                hmix_tr(c - 1)

        wd_hook = (lambda j: W["load_wd_pair"](j) if j % 2 == 0 else None) if t == 0 else None
        ffn_tile(S, g, W, B, xt, after_chunk, gu_hook=wd_hook)
        hmix_tr(3)
        S.dma("act", g.hmTs[t], B["xnT"][:], reads=[B["xnT"]], writes=["hmTs"])
    S.release(base)
    if stop_after == "1a":
        S.finish()
        return nc

    phase_1b(S, g, I)
    S.release(base)
    if stop_after == "1b":
        S.finish()
        return nc

    phase_1c(S, g, I)
    S.release(base)
    if stop_after == "1c":
        S.finish()
        return nc

    phase_2a(S, g, I)
    S.release(base)
    if stop_after == "2a":
        S.finish()
        return nc

    for hs in range(2):
        phase_2b(S, g, I, hs)
        S.release(base)
    if stop_after == "2b":
        S.finish()
        return nc

    phase_3a(S, g, I)
    S.release(base)
    if stop_after == "3a":
        S.finish()
        return nc

    W2 = load_ffn_weights(S, g, I["ff2_wg"], I["ff2_wu"], I["ff2_wd"], I["ff2_precol"], I["ff2_postrow"], "f2")
    B2 = alloc_ffn_work(S, "f2")
    xt2 = S.sbuf("xt2", [128, 4, D], F32)
    ov = out.rearrange("(t c p) d -> t p c d", c=4, p=128)
    for c in range(4):
        S.dma("sp", xt2[:, c, :], g.x2s[0][:, c, :], reads=["x2s"], writes=[(xt2, c)])
    for P in range(NP):
        def after_chunk2(c, P=P):
            S.dma("sp", ov[P][:, c, :], xt2[:, c, :], reads=[(xt2, c)], writes=["out"])
            if P + 1 < NP:
                S.dma("sp", xt2[:, c, :], g.x2s[P + 1][:, c, :], reads=["x2s"], writes=[(xt2, c)])

        wd_hook2 = (lambda j: W2["load_wd_pair"](j) if j % 2 == 0 else None) if P == 0 else None
        ffn_tile(S, g, W2, B2, xt2, after_chunk2, gu_hook=wd_hook2)

    S.finish()
    return nc


BFNP = ml_dtypes.bfloat16


def host_tables(SEQ, half):
    NT = SEQ // 512
    NP = NT // 2
    NKT = SEQ // 128
    NSEL = SEQ // 64
    NC = SEQ // 16
    NCT = max(1, NC // 128)
    f32 = np.float32
    Tb = {}
    pos = np.arange(SEQ)
    qpos = np.concatenate([np.arange((2 * P + half) * 512, (2 * P + half + 1) * 512) for P in range(NP)])
    slopes = (2.0 ** -(np.arange(8) + 1.0)).astype(f32)
    freqs = (10000.0 ** (-np.arange(16, dtype=f32) / 16)).astype(f32)

    def cs(p):
        ang = (p.astype(f32)[None, :] * freqs[:, None]).astype(f32)
        c, s = np.cos(ang).astype(f32), np.sin(ang).astype(f32)
        return np.concatenate([c, c], 0), np.concatenate([s, s], 0)

    Tb["cosk"], Tb["sink"] = cs(pos)
    Tb["cosq"], Tb["sinq"] = cs(qpos)

    def aug_k(p):
        return np.stack([np.ones_like(p), np.ones_like(p), 128 * (p // 128), p % 128]).astype(f32)

    Tb["kaug"] = aug_k(pos).astype(BFNP)
    cend = 16 * np.arange(NCT * 128) + 31
    Tb["kaug_c"] = aug_k(cend).astype(BFNP)
    qa = np.zeros((NP, 4, 8, 512), f32)
    for P in range(NP):
        t = qpos[P * 512:(P + 1) * 512]
        for h in range(8):
            s = slopes[h]
            qa[P, 0, h] = -s * (128 * (t // 128))
            qa[P, 1, h] = -s * (t % 128)
            qa[P, 2, h] = s
            qa[P, 3, h] = s
    Tb["qaug"] = qa.astype(BFNP)
    j = np.arange(128)[:, None, None]
    i = np.arange(512)[None, None, :]
    r = np.arange(8)[None, :, None]
    Tb["dmask"] = np.where(128 * r + j > i + 512 * half, NEGM, 0.0).astype(BFNP)
    r = np.arange(12)[None, :, None]
    dist = (i + 512 * half) - (128 * r + j - 512)
    Tb["wmask"] = np.where((dist >= 0) & (dist < 512), 0.0, NEGM).astype(BFNP)
    cm = np.zeros((NP, 128, NCT, 512), f32)
    ce = cend.reshape(NCT, 128).T
    for P in range(NP):
        t = qpos[P * 512:(P + 1) * 512]
        cm[P] = np.where(ce[:, :, None] > t[None, None, :], NEGM, 0.0)
    Tb["cmask"] = cm.astype(BFNP)
    bo = np.zeros((NP, 128, 4, NSEL), f32)
    blk = np.arange(NSEL)[None, None, :]
    for P in range(NP):
        t = qpos[P * 512:(P + 1) * 512].reshape(4, 128).T
        cur = (t // 64)[:, :, None]
        forced = (blk == 0) | (blk == cur) | (blk == cur - 1)
        bo[P] = np.where(blk <= cur, np.where(forced, 1e4, 0.0), -1e30)
    Tb["bonus"] = bo
    E = np.zeros((NSEL, NKT, 128), f32)
    for kt in range(NKT):
        E[2 * kt, kt, 0:64] = 1.0
        E[2 * kt + 1, kt, 64:128] = 1.0
    Tb["E"] = E.astype(BFNP)
    n_c = (SEQ - 32) // 16 + 1
    c0 = np.arange(NCT * 128) * 16
    s0 = np.arange(NSEL) * 64
    ov = np.clip(np.minimum(c0[:, None] + 32, s0[None, :] + 64) - np.maximum(c0[:, None], s0[None, :]), 0, None) / 32.0
    ov[n_c:] = 0.0
    Tb["ovl"] = np.ascontiguousarray(ov.reshape(NCT, 128, NSEL).transpose(1, 0, 2)).astype(BFNP)
    Tb["hm"] = np.tile(np.array([[1.0 - half, float(half)]], f32), (128, 1))
    Tb["ident_bf"] = np.eye(128).astype(BFNP)
    Tb["ident_f"] = np.eye(128, dtype=f32)
    return Tb


W_IN_PERM = None


def w_in_perm():
    kv0 = 512
    def six(s):
        return list(range(kv0 + s * 128, kv0 + (s + 1) * 128))
    perm = list(range(0, 512)) + six(0) + six(1) + six(2) + six(4) + six(3) + six(5)
    perm += list(range(1560, 1688)) + list(range(1688, 1720)) + list(range(1304, 1560)) + list(range(1280, 1304))
    perm += list(range(1720, 3768))
    assert len(perm) == 3768 and len(set(perm)) == 3768
    return np.array(perm)


def host_weights(inp):
    f32 = np.float32
    col = lambda v, k: np.ascontiguousarray(np.asarray(v, f32).reshape(k, 128).T)
    row = lambda v: np.ascontiguousarray(np.broadcast_to(np.asarray(v, f32), (128, D)))
    Wm = {}
    for f in ("ff1", "ff2"):
        Wm[f + "_wg"] = np.asarray(inp[f + "_w_gate"], f32)
        Wm[f + "_wu"] = np.asarray(inp[f + "_w_up"], f32)
        Wm[f + "_wd"] = np.asarray(inp[f + "_w_down"], f32)
        Wm[f + "_precol"] = col(inp[f + "_pre_g"], 8)
        Wm[f + "_postrow"] = row(inp[f + "_post_g"])
    Wm["mix_precol"] = col(inp["mix_pre_g"], 8)
    Wm["mix_postrow"] = row(inp["mix_post_g"])
    Wm["w_in_p"] = np.ascontiguousarray(np.asarray(inp["w_in"], f32)[:, w_in_perm()])
    Wm["mla_w_uq"] = np.asarray(inp["mla_w_uq"], f32)
    ukv = np.asarray(inp["mla_w_ukv"], f32).reshape(128, 8, 2, 64)
    Wm["mla_w_ukv_p"] = np.ascontiguousarray(np.concatenate([ukv[:, :, 0, :].reshape(128, 512), ukv[:, :, 1, :].reshape(128, 512)], 1))
    Wm["q_norm_col"] = col(inp["mla_q_norm_g"], 2)
    Wm["kv_norm_col"] = col(inp["mla_kv_norm_g"], 1)
    for kvn in ("k", "v"):
        Wm["cmp_w1_" + kvn] = np.asarray(inp["cmp_w1_" + kvn], f32)
        Wm["cmp_w2_" + kvn] = np.asarray(inp["cmp_w2_" + kvn], f32)
        Wm["cmp_posT_" + kvn] = np.ascontiguousarray(np.asarray(inp["cmp_pos_" + kvn], f32).T)
    Wm["w_proj_nsa"] = np.asarray(inp["w_proj_nsa"], f32)
    Wm["w_proj_mla"] = np.asarray(inp["w_proj_mla"], f32)
    Wm["w_out"] = np.asarray(inp["w_out"], f32)
    return Wm


_CACHE = {}


def kernel(**inputs):
    x = np.asarray(inputs["x"], np.float32)
    Bn, SEQ, _ = x.shape
    assert 2 * Bn == 8
    if SEQ not in _CACHE:
        _CACHE[SEQ] = build(SEQ)
    nc = _CACHE[SEQ]
    Wm = host_weights(inputs)
    tabs = [host_tables(SEQ, 0), host_tables(SEQ, 1)]
    in_maps = []
    for core in range(8):
        b, half = core // 2, core % 2
        m = dict(Wm)
        m.update(tabs[half])
        m["x"] = np.ascontiguousarray(x[b])
        in_maps.append({k: v for k, v in m.items() if k in nc.in_names})
    res = run_bass_kernel_spmd(nc, in_maps, core_ids=list(range(8)))
    NP = SEQ // 1024
    out = np.empty((Bn, SEQ, D), np.float32)
    for core in range(8):
        b, half = core // 2, core % 2
        o = res.results[core]["out"]
        for P in range(NP):
            t = 2 * P + half
            out[b, t * 512:(t + 1) * 512] = o[P * 512:(P + 1) * 512]
    return out
```

```python
import numpy as np
import ml_dtypes
import concourse.bass as bass
import concourse.mybir as mybir
from concourse.bass_utils import run_bass_kernel_spmd

F32 = mybir.dt.float32
BF16 = mybir.dt.bfloat16
ALU = mybir.AluOpType
AF = mybir.ActivationFunctionType
AX = mybir.AxisListType


class T:
    def __init__(self, name, h):
        self.name = name
        self.h = h

    def __getitem__(self, idx):
        return self.h[idx]


class Sched:
    ENGS = ("pe", "act", "dve", "pool", "sp")

    def __init__(self, nc, n_dma_sems=32):
        self.nc = nc
        self.sem = {e: nc.alloc_semaphore("sem_" + e) for e in self.ENGS}
        self.cnt = {e: 0 for e in self.ENGS}
        self.prog = {e: [] for e in self.ENGS}
        self.seen = {e: {} for e in self.ENGS}
        self.dsem = [nc.alloc_semaphore("dsem%d" % i) for i in range(n_dma_sems)]
        self.dcnt = [0] * n_dma_sems
        self.dnext = 0
        self.state = {}
        self.n_ops = 0

    SB_BASE = 16512
    SB_END = 229376 - 64

    def sbuf(self, name, shape, dtype):
        esz = {F32: 4, BF16: 2}.get(dtype, 4)
        n = 1
        for d in shape[1:]:
            n *= d
        nbytes = (n * esz + 63) // 64 * 64
        off = getattr(self, "_sb", self.SB_BASE)
        assert off + nbytes <= self.SB_END, ("SBUF overflow", name, off, nbytes)
        self._sb = off + nbytes
        self._uid = getattr(self, "_uid", 0) + 1
        uname = "%s_%d" % (name, self._uid)
        return T(uname, self.nc.alloc_sbuf_tensor_at(uname, list(shape), dtype, offset=off))

    def mark(self):
        return getattr(self, "_sb", self.SB_BASE)

    def release(self, mark):
        self.barrier()
        self._sb = mark

    def barrier(self):
        targets = []
        for k, c in enumerate(self.dcnt):
            if c > 0:
                targets.append((("d", k), c))
        for e in self.ENGS:
            if self.cnt[e] > 0:
                targets.append((("e", e), self.cnt[e]))
        for e in self.ENGS:
            waits = []
            for sk, val in targets:
                if sk == ("e", e):
                    continue
                if self.seen[e].get(sk, 0) >= val:
                    continue
                self.seen[e][sk] = val
                waits.append((sk, val))
            if waits:
                self.prog[e].append((waits, None, None))

    def psum(self, name, shape, dtype=F32):
        return T(name, self.nc.alloc_psum_tensor(name, list(shape), dtype))

    def _key(self, b):
        if isinstance(b, tuple):
            t, slot = b
        else:
            t, slot = b, None
        name = t.name if isinstance(t, T) else t
        return name, slot

    def _states(self, b, create=True):
        name, slot = self._key(b)
        d = self.state.setdefault(name, {})
        if None not in d:
            d[None] = {"w": None, "r": {}}
        if slot is None:
            return list(d.values())
        if slot not in d:
            d[slot] = {"w": d[None]["w"], "r": dict(d[None]["r"])}
        return [d[slot]]

    def _deps(self, reads, writes):
        deps = {}

        def add(sk, val):
            if val > deps.get(sk, 0):
                deps[sk] = val

        for b in reads:
            for st in self._states(b):
                if st["w"] is not None:
                    add(*st["w"])
        for b in writes:
            for st in self._states(b):
                if st["w"] is not None:
                    add(*st["w"])
                for sk, val in st["r"].items():
                    add(sk, val)
        return deps

    def _update(self, reads, writes, sk, val):
        for b in reads:
            for st in self._states(b):
                if val > st["r"].get(sk, 0):
                    st["r"][sk] = val
        for b in writes:
            for st in self._states(b):
                st["w"] = (sk, val)
                st["r"] = {}

    def _waits(self, eng, deps):
        waits = []
        for sk, val in deps.items():
            if sk == ("e", "pe") and eng == "pe":
                continue
            if self.seen[eng].get(sk, 0) >= val:
                continue
            self.seen[eng][sk] = val
            waits.append((sk, val))
        return waits

    def op(self, eng, fn, reads=(), writes=()):
        deps = self._deps(reads, writes)
        waits = self._waits(eng, deps)
        self.cnt[eng] += 1
        val = self.cnt[eng]
        self.prog[eng].append((waits, fn, ("e", eng)))
        self._update(reads, writes, ("e", eng), val)
        self.n_ops += 1

    def dma(self, eng, out, in_, reads=(), writes=()):
        k = self.dnext
        self.dnext = (self.dnext + 1) % len(self.dsem)
        deps = self._deps(reads, writes)
        if self.dcnt[k] > 0:
            sk = ("d", k)
            if self.dcnt[k] > deps.get(sk, 0):
                deps[sk] = self.dcnt[k]
        waits = self._waits(eng, deps)
        self.dcnt[k] += 16
        val = self.dcnt[k]
        self.prog[eng].append((waits, lambda e, o=out, i=in_: e.dma_start(out=o, in_=i), ("d", k)))
        self._update(reads, writes, ("d", k), val)
        self.n_ops += 1

    def _semh(self, sk):
        return self.sem[sk[1]] if sk[0] == "e" else self.dsem[sk[1]]

    def finish(self):
        final = []
        for k, c in enumerate(self.dcnt):
            if c > 0:
                final.append((("d", k), c))
        for e in self.ENGS:
            if e != "sp" and self.cnt[e] > 0:
                final.append((("e", e), self.cnt[e]))
        nc = self.nc
        prog = self.prog
        semh = self._semh

        def emit(engname, e):
            for waits, fn, inc in prog[engname]:
                for sk, val in waits:
                    e.wait_ge(semh(sk), val)
                if fn is None:
                    continue
                ins = fn(e)
                if inc[0] == "e":
                    ins.then_inc(semh(inc), 1)
                else:
                    ins.then_inc(semh(inc), 16)
            if engname == "sp":
                for sk, val in final:
                    e.wait_ge(semh(sk), val)

        with nc.Block() as block:
            @block.sync
            def _(e):
                emit("sp", e)

            if prog["pe"]:
                @block.tensor
                def _(e):
                    emit("pe", e)

            if prog["act"]:
                @block.scalar
                def _(e):
                    emit("act", e)

            if prog["dve"]:
                @block.vector
                def _(e):
                    emit("dve", e)

            if prog["pool"]:
                @block.gpsimd
                def _(e):
                    emit("pool", e)


D = 1024
DFF = 2816
NJ = DFF // 128
EPS = 1e-6
NEGM = -1.0e4
ALIBI_CUT = 164.0
OQ = 0
OKC = 512
OVC = 640
OKS = 768
OKW = 896
OVS = 1024
OVW = 1152
OCKV = 1280
OKPE = 1408
OCQ = 1440
OGN = 1696
OGM = 1720
WIN_A = 1720


class G:
    pass


def bc(ap_col):
    return ap_col


def load_ffn_weights(S, g, wg_d, wu_d, wd_d, precol_d, postrow_d, tag):
    W = {}
    W["wg"] = S.sbuf(tag + "wg", [128, 8, DFF], BF16)
    W["wu"] = S.sbuf(tag + "wu", [128, 8, DFF], BF16)
    W["wd"] = S.sbuf(tag + "wd", [128, NJ, D], BF16)
    W["ghalf"] = S.sbuf(tag + "ghalf", [128, D], F32)
    W["precol"] = S.sbuf(tag + "precol", [128, 8], F32)
    m = S.mark()
    st = [S.sbuf(tag + "st%d" % i, [128, DFF], F32) for i in range(5)]
    S.dma("sp", W["precol"][:], precol_d, writes=[W["precol"]])
    S.dma("sp", st[2][:, 0:D], postrow_d, writes=[st[2]])
    S.op("dve", lambda e: e.tensor_scalar(W["ghalf"][:], st[2][:, 0:D], 0.5, None, ALU.mult), reads=[st[2]], writes=[W["ghalf"]])
    i = 0
    for name, src in (("wg", wg_d), ("wu", wu_d)):
        for k in range(8):
            s_ = st[i % 5]
            S.dma(("sp", "act")[i % 2], s_[:], src[k * 128:(k + 1) * 128, :], writes=[s_])
            dst = W[name]
            if i % 2 == 0:
                S.op("act", lambda e, dst=dst, s_=s_, k=k: e.activation(dst[:, k, :], s_[:], AF.Copy, scale=W["precol"][:, k:k + 1]),
                     reads=[s_, W["precol"]], writes=[(dst, k)])
            else:
                S.op("dve", lambda e, dst=dst, s_=s_, k=k: e.tensor_scalar(dst[:, k, :], s_[:], W["precol"][:, k:k + 1], None, ALU.mult),
                     reads=[s_, W["precol"]], writes=[(dst, k)])
            i += 1
    S.release(m)
    wdst = S.sbuf(tag + "wdst", [128, 2 * D], F32)
    wdv = wd_d.rearrange("(j p) n -> p j n", p=128)

    def load_wd_pair(j0):
        S.dma("sp", wdst[:].rearrange("p (j n) -> p j n", j=2), wdv[:, j0:j0 + 2, :], writes=[wdst])
        if (j0 // 2) % 2 == 0:
            S.op("act", lambda e: e.activation(W["wd"][:, j0:j0 + 2, :], wdst[:].rearrange("p (j n) -> p j n", j=2), AF.Copy),
                 reads=[wdst], writes=[(W["wd"], j0), (W["wd"], j0 + 1)])
        else:
            S.op("dve", lambda e: e.tensor_copy(W["wd"][:, j0:j0 + 2, :], wdst[:].rearrange("p (j n) -> p j n", j=2)),
                 reads=[wdst], writes=[(W["wd"], j0), (W["wd"], j0 + 1)])

    W["load_wd_pair"] = load_wd_pair
    return W


def alloc_ffn_work(S, tag):
    B = {}
    B["xn"] = S.sbuf(tag + "xn", [128, 4, D], BF16)
    B["xnT"] = S.sbuf(tag + "xnT", [128, 8, 512], BF16)
    B["hT"] = S.sbuf(tag + "hT", [128, NJ, 512], BF16)
    B["sg"] = [S.sbuf(tag + "sg%d" % i, [128, 512], F32) for i in range(2)]
    B["tmp"] = S.sbuf(tag + "tmp", [128, 512], F32)
    B["junk"] = S.sbuf(tag + "junk", [128, D], BF16)
    B["st"] = S.sbuf(tag + "stat", [128, 32], F32)
    return B


def rms_rstd(S, g, B, ss_ap, n, out_ap, rd, nfeat):
    st = B["st"]
    S.op("dve", lambda e: e.tensor_scalar(st[:, 16:16 + n], ss_ap, 1.0 / nfeat, EPS, ALU.mult, ALU.add), reads=rd, writes=[(st, "ms")])
    S.op("act", lambda e: e.activation(st[:, 24:24 + n], st[:, 16:16 + n], AF.Sqrt), reads=[(st, "ms")], writes=[(st, "sd")])
    S.op("dve", lambda e: e.reciprocal(out_ap, st[:, 24:24 + n]), reads=[(st, "sd")], writes=[(st, "rstd")])


def norm_transpose(S, g, B, xt):
    st, xn, xnT, junk = B["st"], B["xn"], B["xnT"], B["junk"]
    for c in range(4):
        if c % 2 == 0:
            S.op("act", lambda e, c=c: e.activation(junk[:], xt[:, c, :], AF.Square, accum_out=st[:, c:c + 1]),
                 reads=[(xt, c)], writes=[junk, (st, "ss%d" % c)])
        else:
            S.op("dve", lambda e, c=c: e.scalar_tensor_tensor(xn[:, c, :], xt[:, c, :], 1.0, xt[:, c, :], ALU.mult, ALU.mult, accum_out=st[:, c:c + 1]),
                 reads=[(xt, c)], writes=[(xn, c), (st, "ss%d" % c)])
    rms_rstd(S, g, B, st[:, 0:4], 4, st[:, 8:12], [(st, "ss%d" % c) for c in range(4)], D)
    for c in range(4):
        if c % 2 == 0:
            S.op("act", lambda e, c=c: e.activation(xn[:, c, :], xt[:, c, :], AF.Copy, scale=st[:, 8 + c:9 + c]),
                 reads=[(xt, c), (st, "rstd")], writes=[(xn, c)])
        else:
            S.op("dve", lambda e, c=c: e.tensor_scalar(xn[:, c, :], xt[:, c, :], st[:, 8 + c:9 + c], None, ALU.mult),
                 reads=[(xt, c), (st, "rstd")], writes=[(xn, c)])
    transpose_to(S, g, xn, xnT)


def transpose_to(S, g, src, dstT, nk=8):
    for k0 in range(0, nk, 2):
        kk = min(2, nk - k0)
        pt = g.pTb[(k0 // 2) % 2]
        for k in range(k0, k0 + kk):
            for c in range(4):
                S.op("pe", lambda e, k=k, c=c, pt=pt, k0=k0: e.transpose(pt[:, (k - k0) * 512 + c * 128:(k - k0) * 512 + (c + 1) * 128], src[:, c, k * 128:(k + 1) * 128], g.ident_bf[:]),
                     reads=[(src, c), g.ident_bf], writes=[pt])
        dst = dstT[:, k0:k0 + kk, :]
        srcp = pt[:, 0:kk * 512].rearrange("p (k n) -> p k n", n=512)
        if (k0 // 2) % 2 == 0:
            S.op("dve", lambda e, dst=dst, srcp=srcp: e.tensor_copy(dst, srcp), reads=[pt], writes=[(dstT, k0), (dstT, k0 + kk - 1)])
        else:
            S.op("act", lambda e, dst=dst, srcp=srcp: e.activation(dst, srcp, AF.Copy), reads=[pt], writes=[(dstT, k0), (dstT, k0 + kk - 1)])


def ffn_tile(S, g, W, B, xt, after_chunk=None, gu_hook=None):
    st, xnT, hT, sg, tmp, junk = B["st"], B["xnT"], B["hT"], B["sg"], B["tmp"], B["junk"]
    norm_transpose(S, g, B, xt)
    P = g.pG
    for j in range(NJ):
        pg = P[(j % 2) * 2]
        pu = P[(j % 2) * 2 + 1]
        for k in range(8):
            S.op("pe", lambda e, j=j, k=k, pg=pg: e.matmul(pg[:], W["wg"][:, k, j * 128:(j + 1) * 128], xnT[:, k, :], start=(k == 0), stop=(k == 7)),
                 reads=[(W["wg"], k), (xnT, k)], writes=[pg])
        for k in range(8):
            S.op("pe", lambda e, j=j, k=k, pu=pu: e.matmul(pu[:], W["wu"][:, k, j * 128:(j + 1) * 128], xnT[:, k, :], start=(k == 0), stop=(k == 7)),
                 reads=[(W["wu"], k), (xnT, k)], writes=[pu])
        s_ = sg[j % 2]
        S.op("act", lambda e, pg=pg, s_=s_: e.activation(s_[:], pg[:], AF.Silu), reads=[pg], writes=[s_])
        S.op("dve", lambda e, pu=pu, s_=s_, j=j: e.tensor_tensor(hT[:, j, :], s_[:], pu[:], ALU.mult), reads=[s_, pu], writes=[(hT, j)])
        if gu_hook is not None:
            gu_hook(j)
    for c in range(4):
        pd = [P[(c % 2) * 2], P[(c % 2) * 2 + 1]]
        for nh in range(2):
            for j in range(NJ):
                S.op("pe", lambda e, c=c, nh=nh, j=j, pd=pd: e.matmul(pd[nh][:], hT[:, j, c * 128:(c + 1) * 128], W["wd"][:, j, nh * 512:(nh + 1) * 512], start=(j == 0), stop=(j == NJ - 1)),
                     reads=[(hT, j), (W["wd"], j)], writes=[pd[nh]])
            S.op("act", lambda e, nh=nh, pd=pd: e.activation(junk[:, 0:512], pd[nh][:], AF.Square, accum_out=st[:, 4 + nh:5 + nh]),
                 reads=[pd[nh]], writes=[junk, (st, "ss2")])
        S.op("dve", lambda e: e.tensor_tensor(st[:, 6:7], st[:, 4:5], st[:, 5:6], ALU.add), reads=[(st, "ss2")], writes=[(st, "ss2s")])
        rms_rstd(S, g, B, st[:, 6:7], 1, st[:, 12:13], [(st, "ss2s")], D)
        for nh in range(2):
            S.op("dve", lambda e, nh=nh, pd=pd: e.scalar_tensor_tensor(tmp[:], pd[nh][:], st[:, 12:13], W["ghalf"][:, nh * 512:(nh + 1) * 512], ALU.mult, ALU.mult),
                 reads=[pd[nh], (st, "rstd"), W["ghalf"]], writes=[tmp])
            S.op("dve", lambda e, c=c, nh=nh: e.tensor_tensor(xt[:, c, nh * 512:(nh + 1) * 512], tmp[:], xt[:, c, nh * 512:(nh + 1) * 512], ALU.add),
                 reads=[tmp, (xt, c)], writes=[(xt, c)])
        if after_chunk is not None:
            after_chunk(c)


class Rot:
    def __init__(self, banks):
        self.banks = banks
        self.i = 0

    def __call__(self):
        b = self.banks[self.i % len(self.banks)]
        self.i += 1
        return b


def cast_copy(S, i, out_ap, in_ap, reads, writes, scale=None):
    if i % 2 == 0:
        if scale is None:
            S.op("act", lambda e: e.activation(out_ap, in_ap, AF.Copy), reads=reads, writes=writes)
        else:
            S.op("act", lambda e: e.activation(out_ap, in_ap, AF.Copy, scale=scale), reads=reads, writes=writes)
    else:
        if scale is None:
            S.op("dve", lambda e: e.tensor_copy(out_ap, in_ap), reads=reads, writes=writes)
        else:
            S.op("dve", lambda e: e.tensor_scalar(out_ap, in_ap, scale, None, ALU.mult), reads=reads, writes=writes)


def phase_1b(S, g, I):
    SEQ, NT, NP = g.SEQ, g.NT, g.NP
    rot = Rot(g.pG)
    WinA = S.sbuf("WinA", [128, 8, WIN_A], BF16)
    Wrot = S.sbuf("Wrot", [128, 8, 32], BF16)
    Wuq = S.sbuf("Wuq", [128, 2, 768], BF16)
    Wuqr = S.sbuf("Wuqr", [128, 2, 768], BF16)
    Wukv = S.sbuf("Wukv", [128, 1024], BF16)
    cols = S.sbuf("cols1b", [128, 16], F32)
    S.dma("sp", cols[:, 0:8], I["mix_precol"], writes=[(cols, 0)])
    S.dma("sp", cols[:, 8:10], I["q_norm_col"], writes=[(cols, 1)])
    S.dma("sp", cols[:, 10:11], I["kv_norm_col"], writes=[(cols, 2)])
    S.dma("sp", cols[:, 12:14], I["hm"], writes=[(cols, 3)])
    m = S.mark()
    st = [S.sbuf("st1b%d" % i, [128, WIN_A], F32) for i in range(2)]
    for k in range(8):
        s_ = st[k % 2]
        S.dma(("sp", "act")[k % 2], s_[:], I["w_in_p"][k * 128:(k + 1) * 128, 0:WIN_A], writes=[s_])
        cast_copy(S, k, WinA[:, k, :], s_[:], [s_, (cols, 0)], [(WinA, k)], scale=cols[:, k:k + 1])
    S.op("act", lambda e: e.activation(Wrot[:, :, 0:16], WinA[:, :, OKPE + 16:OKPE + 32], AF.Copy, scale=-1.0), reads=[WinA], writes=[(Wrot, 0)])
    S.op("dve", lambda e: e.tensor_copy(Wrot[:, :, 16:32], WinA[:, :, OKPE:OKPE + 16]), reads=[WinA], writes=[(Wrot, 1)])
    for k2 in range(2):
        s_ = st[k2 % 2]
        S.dma("sp", s_[:, 0:768], I["mla_w_uq"][k2 * 128:(k2 + 1) * 128, :], writes=[s_])
        cast_copy(S, k2, Wuq[:, k2, :], s_[:, 0:768], [s_, (cols, 1)], [(Wuq, k2)], scale=cols[:, 8 + k2:9 + k2])
    S.op("pool", lambda e: e.memset(Wuqr[:], 0.0), writes=[Wuqr])
    Wuq4 = Wuq[:].rearrange("p k (h e) -> p k h e", e=96)
    Wuqr4 = Wuqr[:].rearrange("p k (h e) -> p k h e", e=96)
    for k2 in range(2):
        S.op("act", lambda e, k2=k2: e.activation(Wuqr4[:, k2, :, 64:80], Wuq4[:, k2, :, 80:96], AF.Copy, scale=-1.0), reads=[Wuq], writes=[Wuqr])
        S.op("dve", lambda e, k2=k2: e.tensor_copy(Wuqr4[:, k2, :, 80:96], Wuq4[:, k2, :, 64:80]), reads=[Wuq], writes=[Wuqr])
    s_ = st[0]
    S.dma("sp", s_[:, 0:1024], I["mla_w_ukv_p"], writes=[s_])
    cast_copy(S, 1, Wukv[:], s_[:, 0:1024], [s_, (cols, 2)], [Wukv], scale=cols[:, 10:11])
    hA = S.sbuf("hA", [128, 8, 512], BF16)
    hB = S.sbuf("hB", [128, 8, 512], BF16)
    hO = S.sbuf("hO", [128, 8, 512], BF16)
    kst = S.sbuf("kst", [64, 8, 512], BF16)
    vst = S.sbuf("vst", [128, 4, 4, 65], BF16)
    kn = S.sbuf("kn", [128, 4, 128], BF16)
    knT = S.sbuf("knT", [128, 1, 512], BF16)
    knst = S.sbuf("knst", [64, 8, 512], BF16)
    vmst = S.sbuf("vmst", [128, 4, 8, 65], BF16)
    krst = S.sbuf("krst", [32, 512], BF16)
    t1 = S.sbuf("t1", [96, 512], F32)
    t2 = S.sbuf("t2", [96, 512], F32)
    ck = S.sbuf("ck", [32, 512], F32)
    sk = S.sbuf("sk", [32, 512], F32)
    cq = S.sbuf("cq", [96, 512], F32)
    sq = S.sbuf("sq", [96, 512], F32)
    qst = S.sbuf("qst", [64, 8, 512], BF16)
    gst = S.sbuf("gst", [128, 4, 24], F32)
    qn = S.sbuf("qn", [128, 4, 256], BF16)
    qnT = S.sbuf("qnT", [128, 2, 512], BF16)
    qmst = S.sbuf("qmst", [96, 8, 512], BF16)
    junk = S.sbuf("junk1b", [128, 256], BF16)
    B = {"st": S.sbuf("stat1b", [128, 32], F32)}
    stt = B["st"]
    zt = S.sbuf("zt", [64, 8, 16], BF16)
    S.op("pool", lambda e: e.memset(zt[:], 0.0), writes=[zt])
    S.dma("sp", g.kTs.rearrange("t d s -> d t s")[:, :, SEQ:SEQ + 16], zt[:], reads=[zt], writes=["kTs"])
    S.op("pool", lambda e: e.memset(vst[:], 1.0), writes=[vst])
    S.op("pool", lambda e: e.memset(vmst[:], 1.0), writes=[vmst])
    cnt = [0]

    def cc(out_ap, in_ap, reads, writes, scale=None):
        cast_copy(S, cnt[0], out_ap, in_ap, reads, writes, scale)
        cnt[0] += 1

    def kside(T, h, hook=None):
        cs = slice(T * 512, (T + 1) * 512)
        for ti in range(8):
            off = OKC + ti * 64
            ps = rot()
            for k in range(8):
                S.op("pe", lambda e, k=k, ps=ps, off=off: e.matmul(ps[0:64, :], WinA[:, k, off:off + 64], h[:, k, :], start=(k == 0), stop=(k == 7)),
                     reads=[(WinA, k), h], writes=[ps])
            cc(kst[:, ti, :], ps[0:64, :], [ps], [(kst, ti)])
            if hook is not None:
                hook(ti)
        S.dma("sp", g.kTs.rearrange("t d s -> d t s")[:, :, cs], kst[:], reads=[kst], writes=["kTs"])
        for c in range(4):
            ps = rot()
            for k in range(8):
                S.op("pe", lambda e, k=k, c=c, ps=ps: e.matmul(ps[:, 0:384], h[:, k, c * 128:(c + 1) * 128], WinA[:, k, OVS:OVS + 384], start=(k == 0), stop=(k == 7)),
                     reads=[(WinA, k), h], writes=[ps])
            cc(vst[:, c, :, 0:64], ps[:, 0:256].rearrange("p (t e) -> p t e", e=64), [ps], [(vst, c)])
            S.op("act", lambda e, c=c, ps=ps: e.activation(junk[:, 0:128], ps[:, 256:384], AF.Square, accum_out=stt[:, c:c + 1]),
                 reads=[ps], writes=[junk, (stt, "ss")])
            rms_rstd(S, g, B, stt[:, c:c + 1], 1, stt[:, 8 + c:9 + c], [(stt, "ss")], 128)
            S.op("dve", lambda e, c=c, ps=ps: e.tensor_scalar(kn[:, c, :], ps[:, 256:384], stt[:, 8 + c:9 + c], None, ALU.mult),
                 reads=[ps, (stt, "rstd")], writes=[(kn, c)])
        S.dma("act", g.vtm[T * 4:(T + 1) * 4].rearrange("k p t e -> p k t e"), vst[:], reads=[vst], writes=["vtm"])
        transpose_to(S, g, kn, knT, nk=1)
        for hh in range(8):
            ps = rot()
            S.op("pe", lambda e, hh=hh, ps=ps: e.matmul(ps[0:64, :], Wukv[:, hh * 64:(hh + 1) * 64], knT[:, 0, :], start=True, stop=True),
                 reads=[Wukv, knT], writes=[ps])
            cc(knst[:, hh, :], ps[0:64, :], [ps], [(knst, hh)])
        S.dma("sp", g.knopeT.rearrange("h d s -> d h s")[:, :, cs], knst[:], reads=[knst], writes=["knopeT"])
        for c in range(4):
            ps = rot()
            S.op("pe", lambda e, c=c, ps=ps: e.matmul(ps[:], knT[:, 0, c * 128:(c + 1) * 128], Wukv[:, 512:1024], start=True, stop=True),
                 reads=[Wukv, knT], writes=[ps])
            cc(vmst[:, c, :, 0:64], ps[:].rearrange("p (h e) -> p h e", e=64), [ps], [(vmst, c)])
        S.dma("act", g.vmla[T * 4:(T + 1) * 4].rearrange("k p h e -> p k h e"), vmst[:], reads=[vmst], writes=["vmla"])
        S.dma("sp", ck[:], I["cosk"][:, cs], writes=[ck])
        S.dma("sp", sk[:], I["sink"][:, cs], writes=[sk])
        psa = rot()
        psb = rot()
        for k in range(8):
            S.op("pe", lambda e, k=k, psa=psa: e.matmul(psa[0:32, :], WinA[:, k, OKPE:OKPE + 32], h[:, k, :], start=(k == 0), stop=(k == 7)),
                 reads=[(WinA, k), h], writes=[psa])
        for k in range(8):
            S.op("pe", lambda e, k=k, psb=psb: e.matmul(psb[0:32, :], Wrot[:, k, :], h[:, k, :], start=(k == 0), stop=(k == 7)),
                 reads=[Wrot, h], writes=[psb])
        S.op("dve", lambda e, psa=psa: e.tensor_tensor(t1[0:32, :], psa[0:32, :], ck[:], ALU.mult), reads=[psa, ck], writes=[t1])
        S.op("dve", lambda e, psb=psb: e.tensor_tensor(t2[0:32, :], psb[0:32, :], sk[:], ALU.mult), reads=[psb, sk], writes=[t2])
        S.op("pool", lambda e: e.tensor_tensor(krst[:], t1[0:32, :], t2[0:32, :], ALU.add), reads=[t1, t2], writes=[krst])
        S.dma("sp", g.krotT[:, cs], krst[:], reads=[krst], writes=["krotT"])

    def qside(P, h):
        for hh in range(8):
            ps = rot()
            for k in range(8):
                S.op("pe", lambda e, k=k, ps=ps, hh=hh: e.matmul(ps[0:64, :], WinA[:, k, OQ + hh * 64:OQ + (hh + 1) * 64], h[:, k, :], start=(k == 0), stop=(k == 7)),
                     reads=[(WinA, k), h], writes=[ps])
            cc(qst[:, hh, :], ps[0:64, :], [ps], [(qst, hh)], scale=0.125)
        S.dma("sp", g.qnsaT[P], qst[:], reads=[qst], writes=["qnsaT"])
        for c in range(4):
            ps = rot()
            for k in range(8):
                S.op("pe", lambda e, k=k, c=c, ps=ps: e.matmul(ps[:, 0:280], h[:, k, c * 128:(c + 1) * 128], WinA[:, k, OCQ:OCQ + 280], start=(k == 0), stop=(k == 7)),
                     reads=[(WinA, k), h], writes=[ps])
            S.op("act", lambda e, c=c, ps=ps: e.activation(gst[:, c, :], ps[:, 256:280], AF.Sigmoid), reads=[ps], writes=[(gst, c)])
            S.op("act", lambda e, c=c, ps=ps: e.activation(junk[:, 0:256], ps[:, 0:256], AF.Square, accum_out=stt[:, c:c + 1]),
                 reads=[ps], writes=[junk, (stt, "ss")])
            rms_rstd(S, g, B, stt[:, c:c + 1], 1, stt[:, 8 + c:9 + c], [(stt, "ss")], 256)
            S.op("dve", lambda e, c=c, ps=ps: e.tensor_scalar(qn[:, c, :], ps[:, 0:256], stt[:, 8 + c:9 + c], None, ALU.mult),
                 reads=[ps, (stt, "rstd")], writes=[(qn, c)])
        S.dma("act", g.gnsa[P], gst[:], reads=[gst], writes=["gnsa"])
        transpose_to(S, g, qn, qnT, nk=2)
        qs = slice(P * 512, (P + 1) * 512)
        S.dma("sp", cq[64:96, :], I["cosq"][:, qs], writes=[cq])
        S.dma("sp", sq[64:96, :], I["sinq"][:, qs], writes=[sq])
        for hh in range(8):
            psa = rot()
            psb = rot()
            for k2 in range(2):
                S.op("pe", lambda e, k2=k2, psa=psa, hh=hh: e.matmul(psa[0:96, :], Wuq[:, k2, hh * 96:(hh + 1) * 96], qnT[:, k2, :], start=(k2 == 0), stop=(k2 == 1)),
                     reads=[Wuq, qnT], writes=[psa])
            for k2 in range(2):
                S.op("pe", lambda e, k2=k2, psb=psb, hh=hh: e.matmul(psb[0:96, :], Wuqr[:, k2, hh * 96:(hh + 1) * 96], qnT[:, k2, :], start=(k2 == 0), stop=(k2 == 1)),
                     reads=[Wuqr, qnT], writes=[psb])
            cc(qmst[0:64, hh, :], psa[0:64, :], [psa], [(qmst, hh)])
            S.op("dve", lambda e, psa=psa: e.tensor_tensor(t1[64:96, :], psa[64:96, :], cq[64:96, :], ALU.mult), reads=[psa, cq], writes=[t1])
            S.op("dve", lambda e, psb=psb: e.tensor_tensor(t2[64:96, :], psb[64:96, :], sq[64:96, :], ALU.mult), reads=[psb, sq], writes=[t2])
            S.op("pool", lambda e, hh=hh: e.tensor_tensor(qmst[64:96, hh, :], t1[64:96, :], t2[64:96, :], ALU.add), reads=[t1, t2], writes=[(qmst, hh)])
        S.dma("sp", g.qmlaT[P], qmst[:], reads=[qmst], writes=["qmlaT"])

    hAs = [hA, S.sbuf("hA2", [128, 8, 512], BF16)]
    hBs = [hB, S.sbuf("hB2", [128, 8, 512], BF16)]

    def load_pair(P):
        S.dma("sp", hAs[P % 2][:], g.hmTs[2 * P], reads=["hmTs"], writes=[hAs[P % 2]])
        S.dma("sp", hBs[P % 2][:], g.hmTs[2 * P + 1], reads=["hmTs"], writes=[hBs[P % 2]])

    load_pair(0)
    for P in range(NP):
        if P + 1 < NP:
            load_pair(P + 1)
        a_, b_ = hAs[P % 2], hBs[P % 2]

        def sel_k(k, a_=a_, b_=b_):
            S.op("act", lambda e: e.activation(hO[:, k, :], a_[:, k, :], AF.Copy, scale=cols[:, 12:13]), reads=[a_, (cols, 3)], writes=[(hO, k)])
            S.op("dve", lambda e: e.scalar_tensor_tensor(hO[:, k, :], b_[:, k, :], cols[:, 13:14], hO[:, k, :], ALU.mult, ALU.add),
                 reads=[b_, (hO, k), (cols, 3)], writes=[(hO, k)])

        kside(2 * P, a_, hook=sel_k)
        S.dma("act", g.hmTown[P], hO[:], reads=[hO], writes=["hmTown"])
        kside(2 * P + 1, b_)
        qside(P, hO)


def phase_1c(S, g, I):
    SEQ, NCT, NSEL = g.SEQ, g.NCT, g.NSEL
    NCP = NCT * 128
    rot = Rot(g.pG)
    w1b = [S.sbuf("w1b%d" % i, [64, 32, 256], BF16) for i in range(2)]
    w2b = [S.sbuf("w2b%d" % i, [128, 2, 64], BF16) for i in range(2)]
    posT = [S.sbuf("posT%d" % i, [64, 32], BF16) for i in range(2)]
    bias = [S.sbuf("cbias%d" % i, [128, 2], F32) for i in range(2)]
    srcT = [S.sbuf("csrcT%d" % i, [64, SEQ + 16], BF16) for i in range(2)]
    hid = S.sbuf("chid", [128, 2, NCP], BF16)
    u = S.sbuf("cu", [128, 512], F32)
    u2 = S.sbuf("cu2", [128, 512], F32)
    zz = S.sbuf("czz", [128, 512], F32)
    sgm = S.sbuf("csg", [128, 512], F32)
    kcst = S.sbuf("kcst", [64, NCP], BF16)
    vcst = S.sbuf("vcst", [128, NCT, 65 + NSEL], BF16)
    w2s = [S.sbuf("w2s%d" % i, [128, 2, 64], F32) for i in range(2)]
    pss = [S.sbuf("pss%d" % i, [64, 32], F32) for i in range(2)]
    stg = [S.sbuf("c1st%d" % i, [64, 16, 256], F32) for i in range(2)]
    S.op("pool", lambda e: e.memset(vcst[:], 1.0), writes=[vcst])
    S.dma("sp", vcst[:, :, 65:65 + NSEL], I["ovl"], reads=[], writes=[vcst])
    CH = min(512, NCP)
    order = [(0, 0), (0, 1), (1, 0), (1, 1)]
    S.dma("sp", srcT[0][:], g.kTs[0], reads=["kTs"], writes=[srcT[0]])
    for kvi, kvn in enumerate(("k", "v")):
        w1v = I["cmp_w1_" + kvn].rearrange("(l d) m -> d l m", d=64)
        for hh in range(2):
            S.dma(("sp", "act")[hh], stg[hh][:], w1v[:, hh * 16:(hh + 1) * 16, :], writes=[stg[hh]])
            cast_copy(S, hh, w1b[kvi][:, hh * 16:(hh + 1) * 16, :], stg[hh][:], [stg[hh]], [(w1b[kvi], hh)])
        S.dma("sp", w2s[kvi][:], I["cmp_w2_" + kvn].rearrange("(k p) d -> p k d", p=128), writes=[w2s[kvi]])
        cast_copy(S, 1, w2b[kvi][:], w2s[kvi][:], [w2s[kvi]], [w2b[kvi]])
        S.dma("sp", pss[kvi][:], I["cmp_posT_" + kvn], writes=[pss[kvi]])
        cast_copy(S, 1, posT[kvi][:], pss[kvi][:], [pss[kvi]], [posT[kvi]])
    for kvi in range(2):
        for mc in range(2):
            pb = rot()
            for l in range(32):
                S.op("pe", lambda e, l=l, mc=mc, pb=pb, kvi=kvi: e.matmul(pb[:, 0:1], w1b[kvi][:, l, mc * 128:(mc + 1) * 128], posT[kvi][:, l:l + 1], start=(l == 0), stop=(l == 31)),
                     reads=[w1b[kvi], posT[kvi]], writes=[pb])
            S.op("dve", lambda e, mc=mc, pb=pb, kvi=kvi: e.tensor_copy(bias[kvi][:, mc:mc + 1], pb[:, 0:1]), reads=[pb], writes=[(bias[kvi], mc)])
    for oi, (kvi, gi) in enumerate(order):
        src = srcT[oi % 2]
        if oi + 1 < len(order):
            nk, ng = order[oi + 1]
            S.dma("sp", srcT[(oi + 1) % 2][:], g.kTs[2 * nk + ng], reads=["kTs"], writes=[srcT[(oi + 1) % 2]])
        W1, W2, bs = w1b[kvi], w2b[kvi], bias[kvi]
        for mc in range(2):
            for b0 in range(0, NCP, CH):
                ps = rot()
                for l in range(32):
                    S.op("pe", lambda e, l=l, mc=mc, ps=ps, b0=b0, W1=W1, src=src: e.matmul(ps[:, 0:CH], W1[:, l, mc * 128:(mc + 1) * 128], src[:, l + 16 * b0: l + 16 * (b0 + CH - 1) + 1: 16], start=(l == 0), stop=(l == 31)),
                         reads=[W1, src], writes=[ps])
                S.op("dve", lambda e, mc=mc, ps=ps, bs=bs: e.tensor_scalar(u[:, 0:CH], ps[:, 0:CH], bs[:, mc:mc + 1], None, ALU.add), reads=[ps, (bs, mc)], writes=[u])
                S.op("act", lambda e: e.activation(u2[:, 0:CH], u[:, 0:CH], AF.Square), reads=[u], writes=[u2])
                S.op("dve", lambda e: e.tensor_scalar(u2[:, 0:CH], u2[:, 0:CH], 0.044715, 1.0, ALU.mult, ALU.add), reads=[u2], writes=[u2])
                S.op("dve", lambda e: e.tensor_tensor(zz[:, 0:CH], u[:, 0:CH], u2[:, 0:CH], ALU.mult), reads=[u, u2], writes=[zz])
                S.op("act", lambda e: e.activation(sgm[:, 0:CH], zz[:, 0:CH], AF.Sigmoid, scale=1.5957691216057308), reads=[zz], writes=[sgm])
                S.op("dve", lambda e, mc=mc, b0=b0: e.tensor_tensor(hid[:, mc, b0:b0 + CH], u[:, 0:CH], sgm[:, 0:CH], ALU.mult), reads=[u, sgm], writes=[(hid, mc)])
        if kvi == 0:
            for b0 in range(0, NCP, CH):
                ps = rot()
                for mc in range(2):
                    S.op("pe", lambda e, mc=mc, ps=ps, b0=b0, W2=W2: e.matmul(ps[0:64, 0:CH], W2[:, mc, :], hid[:, mc, b0:b0 + CH], start=(mc == 0), stop=(mc == 1)),
                         reads=[W2, hid], writes=[ps])
                cast_copy(S, 0, kcst[:, b0:b0 + CH], ps[0:64, 0:CH], [ps], [kcst])
            S.dma("sp", g.kcT[gi], kcst[:], reads=[kcst], writes=["kcT"])
        else:
            for it in range(NCT):
                ps = rot()
                for mc in range(2):
                    S.op("pe", lambda e, mc=mc, ps=ps, it=it, W2=W2: e.matmul(ps[:, 0:64], hid[:, mc, it * 128:(it + 1) * 128], W2[:, mc, :], start=(mc == 0), stop=(mc == 1)),
                         reads=[W2, hid], writes=[ps])
                cast_copy(S, it, vcst[:, it, 0:64], ps[:, 0:64], [ps], [vcst])
            S.dma("sp", g.vcaug[gi], vcst[:], reads=[vcst], writes=["vcaug"])


class Fin:
    def __init__(self, S, tag):
        self.S = S
        self.sc = S.sbuf(tag + "finsc", [128, 16], F32)

    def run(self, accs, gate_ap, y, col0, first, extra=None):
        S, sc = self.S, self.sc
        banks = []
        for b, _ in accs:
            if b not in banks:
                banks.append(b)
        for c in range(4):
            S.op("dve", lambda e, c=c: e.tensor_scalar(sc[:, c:c + 1], accs[c][1][:, 64:65], 1e-30, None, ALU.max),
                 reads=[accs[c][0]], writes=[(sc, "mx")])
        S.op("dve", lambda e: e.reciprocal(sc[:, 4:8], sc[:, 0:4]), reads=[(sc, "mx")], writes=[(sc, "rs")])
        if gate_ap is not None:
            S.op("dve", lambda e: e.tensor_tensor(sc[:, 8:12], sc[:, 4:8], gate_ap, ALU.mult), reads=[(sc, "rs"), self.gt], writes=[(sc, "cg")])
            off = 8
        else:
            off = 4
        for c in range(4):
            if first:
                S.op("dve", lambda e, c=c: e.tensor_scalar(y[:, c, col0:col0 + 64], accs[c][1][:, 0:64], sc[:, off + c:off + c + 1], None, ALU.mult),
                     reads=[accs[c][0], (sc, "cg"), (sc, "rs")], writes=[(y, c)])
            else:
                S.op("dve", lambda e, c=c: e.scalar_tensor_tensor(y[:, c, col0:col0 + 64], accs[c][1][:, 0:64], sc[:, off + c:off + c + 1], y[:, c, col0:col0 + 64], ALU.mult, ALU.add),
                     reads=[accs[c][0], (sc, "cg"), (sc, "rs"), (y, c)], writes=[(y, c)])
            if extra is not None:
                extra(c, sc[:, 4 + c:5 + c])


class Stream:
    def __init__(self, S, g, tag):
        self.S, self.g = S, g
        self.pT = [S.sbuf(tag + "pT%d" % i, [128, 2, 512], BF16) for i in range(3)]
        self.n = 0
        self.ntm = 0
        self.items = []

    def add(self, **kw):
        self.items.append(kw)

    def _score(self, j):
        S, g = self.S, self.g
        it = self.items[j]
        d = it.get("d", it["slot"] % 2)
        mask = it.get("mask")
        for ti, (lhsT, rhs, reads, extra) in enumerate(it["score"]):
            bank = g.pG[2 * d + ti]
            mms = [(lhsT, rhs, reads)]
            if extra is not None:
                mms.append(extra)
            if mask is not None:
                mms.append((g.ident_bf[:], mask[0][:, ti, :], [g.ident_bf, mask[1]]))
            for mi, (l_, r_, rd_) in enumerate(mms):
                S.op("pe", lambda e, bank=bank, l_=l_, r_=r_, mi=mi, n=len(mms): e.matmul(bank[:], l_, r_, start=(mi == 0), stop=(mi == n - 1)),
                     reads=rd_, writes=[bank])

    def _exp(self, j):
        S, g = self.S, self.g
        it = self.items[j]
        d = it.get("d", it["slot"] % 2)
        nt = len(it["score"])
        pt = self.pT[it["slot"] % 3]
        banks = [g.pG[2 * d + ti] for ti in range(nt)]
        src = g.pD[d].h[:, 0:nt * 512].rearrange("p (t n) -> p t n", n=512)
        scale = it.get("scale", 1.0)
        S.op("act", lambda e: e.activation(pt[:, 0:nt, :], src, AF.Exp, scale=scale), reads=banks, writes=[pt])
        it["pt"] = pt

    def _pv(self, j):
        S = self.S
        it = self.items[j]
        pt = it["pt"]
        nt = len(it["score"])
        for ti in range(nt):
            v_ap, v_reads = it["pv"][ti]
            first = it["first"] and ti == 0
            last = it["last"] and ti == nt - 1
            for c in range(4):
                bank, out_ap, lead = it["acc"][c]
                S.op("pe", lambda e, out_ap=out_ap, pt=pt, ti=ti, c=c, v_ap=v_ap, first=first, last=last, lead=lead:
                     e.matmul(out_ap, pt[:, ti, c * 128:(c + 1) * 128], v_ap, start=(first and lead), stop=last, skip_group_check=True),
                     reads=[pt] + v_reads, writes=[bank])

    def run(self):
        items = self.items
        n = len(items)
        for j, it in enumerate(items):
            it["slot"] = self.n + j
        pending = []
        if n:
            self._score(0)
        fixed = any("d" in it for it in items)
        for j in range(n):
            if fixed:
                self._exp(j)
                if j + 1 < n:
                    self._score(j + 1)
            else:
                if j + 1 < n:
                    self._score(j + 1)
                self._exp(j)
            self._pv(j)
            pending = [(d - 1, f) for d, f in pending]
            for d, f in pending:
                if d <= 0:
                    f()
            pending = [(d, f) for d, f in pending if d > 0]
            if items[j].get("after") is not None:
                pending.append((items[j].get("defer", 2), items[j]["after"]))
        for d, f in pending:
            f()
        self.n += n
        self.items = []


def acc_views(bank, ncols=65):
    a4 = bank[:].rearrange("p (c e) -> p c e", e=128)
    return [(bank, a4[:, c, 0:ncols], c == 0) for c in range(4)]


def phase_2a(S, g, I):
    SEQ, NP, NKT, NSEL, NCT = g.SEQ, g.NP, g.NKT, g.NSEL, g.NCT
    NCP = NCT * 128
    NV = 65 + NSEL
    accb = [g.pG[4], g.pG[5]]
    Kslc = [S.sbuf("Kslc%d" % i, [68, SEQ], BF16) for i in range(2)]
    Kwin = [S.sbuf("Kwin%d" % i, [68, SEQ], BF16) for i in range(2)]
    V4 = S.sbuf("V4", [128, NKT, 4, 65], BF16)
    Kc = [S.sbuf("Kc%d" % i, [68, NCP], BF16) for i in range(2)]
    Vc = [S.sbuf("Vc%d" % i, [128, NCT, NV], BF16) for i in range(2)]
    Et = S.sbuf("Et", [NSEL, NKT, 128], BF16)
    dmask = S.sbuf("dmask", [128, 8, 512], BF16)
    wmask = S.sbuf("wmask", [128, 12, 512], BF16)
    def load_residents():
        for gi in range(2):
            S.dma("sp", Kc[gi][0:64, :], g.kcT[gi], reads=["kcT"], writes=[Kc[gi]])
            S.dma("sp", Kc[gi][64:68, :], I["kaug_c"], writes=[Kc[gi]])
            S.dma("act", Vc[gi][:], g.vcaug[gi], reads=["vcaug"], writes=[Vc[gi]])
        load_slot(0)
        S.dma("act", wmask[:], I["wmask"], writes=[wmask])
        for gi in range(2):
            S.dma("act", Kwin[gi][0:64, :], g.kTs[6 + gi][:, 0:SEQ], reads=["kTs"], writes=[Kwin[gi]])
            S.dma("act", Kwin[gi][64:68, :], I["kaug"], writes=[Kwin[gi]])
        for k0 in range(0, NKT, 16):
            k1 = min(NKT, k0 + 16)
            S.dma("sp", V4[:, k0:k1], g.vtm[k0:k1].rearrange("k p t e -> p k t e"), reads=["vtm"], writes=[(V4, k0 // 16)])
        S.dma("sp", dmask[:], I["dmask"], writes=[dmask])
        for gi in range(2):
            S.dma("sp", Kslc[gi][0:64, :], g.kTs[4 + gi][:, 0:SEQ], reads=["kTs"], writes=[Kslc[gi]])
            S.dma("sp", Kslc[gi][64:68, :], I["kaug"], writes=[Kslc[gi]])
        S.dma("act", Et[:], I["E"], writes=[Et])

    Qa = [S.sbuf("Qa%d" % i, [68, 8, 512], BF16) for i in range(2)]
    cm = [S.sbuf("cm%d" % i, [128, NCT, 512], BF16) for i in range(2)]
    bon = [S.sbuf("bon%d" % i, [128, 4, NSEL], F32) for i in range(2)]
    gt = [S.sbuf("gt%d" % i, [128, 4, 24], F32) for i in range(2)]
    y = S.sbuf("ynsa", [128, 4, 512], F32)
    imp = [S.sbuf("imp%d" % i, [128, 4, NSEL], F32) for i in range(2)]
    selT = S.sbuf("selT", [NSEL, 2, 512], BF16)
    impb = [S.sbuf("impb%d" % i, [128, NSEL], F32) for i in range(8)]
    wk = [S.sbuf("wk%d" % i, [128, NSEL], F32) for i in range(8)]
    m8 = [S.sbuf("m8_%d" % i, [128, 16], F32) for i in range(8)]
    sn = [S.sbuf("sn%d" % i, [128, NSEL], BF16) for i in range(8)]
    ybf = S.sbuf("ybf", [128, 4, 512], BF16)
    yT = S.sbuf("yT", [128, 4, 512], BF16)
    fin = Fin(S, "a")
    st = Stream(S, g, "a")
    hd = [0]

    def load_slot(P):
        b = P % 2
        S.dma("sp", Qa[b][0:64], g.qnsaT[P], reads=["qnsaT"], writes=[Qa[b]])
        S.dma("sp", Qa[b][64:68], I["qaug"][P], writes=[Qa[b]])
        S.dma("act", cm[b][:], I["cmask"][P], writes=[cm[b]])
        S.dma("act", bon[b][:], I["bonus"][P], writes=[bon[b]])
        S.dma("act", gt[b][:], g.gnsa[P], reads=["gnsa"], writes=[gt[b]])

    load_residents()
    for P in range(NP):
        b = P % 2
        if P + 1 < NP:
            load_slot(P + 1)
        Q, cmk, bn, gates = Qa[b], cm[b], bon[b], gt[b]
        fin.gt = gates
        ncmp = min(NCT, ((2 * P + 2) * 32 + 127) // 128)
        csets = []
        for si in range(2):
            b0, b1 = g.pG[2 + 2 * si], g.pG[3 + 2 * si]
            A = b0[:].rearrange("p (c e) -> p c e", e=256)
            Bk = b1[:].rearrange("p (c e) -> p c e", e=256)
            csets.append(((b0, b1), (A, Bk)))
        for gi in range(2):
            for hh in range(4):
                head = 4 * gi + hh
                (bks, vws) = csets[head % 2]
                cacc = [(bks[c // 2], vws[c // 2][:, c % 2, 0:NV], c % 2 == 0) for c in range(4)]
                groups = [list(range(t0, min(ncmp, t0 + 2))) for t0 in range(0, ncmp, 2)]

                def after(gi=gi, hh=hh, head=head, gates=gates, bks=bks, vws=vws):
                    accs = [(bks[c // 2], vws[c // 2][:, c % 2, :]) for c in range(4)]

                    def extra(c, rs_ap):
                        if hh == 0:
                            S.op("dve", lambda e: e.tensor_scalar(imp[gi][:, c, :], accs[c][1][:, 65:NV], rs_ap, None, ALU.mult),
                                 reads=[accs[c][0], (fin.sc, "rs")], writes=[(imp[gi], c)])
                        else:
                            S.op("dve", lambda e: e.scalar_tensor_tensor(imp[gi][:, c, :], accs[c][1][:, 65:NV], rs_ap, imp[gi][:, c, :], ALU.mult, ALU.add),
                                 reads=[accs[c][0], (fin.sc, "rs"), (imp[gi], c)], writes=[(imp[gi], c)])

                    fin.gt = gates
                    fin.run(accs, gates[:, :, head * 3 + 0], y, head * 64, True, extra)

                for gidx, tl in enumerate(groups):
                    st.add(score=[(Kc[gi][0:68, it * 128:(it + 1) * 128], Q[0:68, head, :], [Kc[gi], Q], None) for it in tl],
                           mask=(cmk[:, tl[0]:tl[-1] + 1, :], cmk), d=0, defer=1,
                           pv=[(Vc[gi][:, it, :], [Vc[gi]]) for it in tl],
                           acc=cacc, first=(gidx == 0), last=(gidx == len(groups) - 1),
                           after=(after if gidx == len(groups) - 1 else None))
        st.run()
        chains = [(gi, c) for gi in range(2) for c in range(4)]
        for i, (gi, c) in enumerate(chains):
            S.op("dve", lambda e, i=i, c=c, gi=gi, bn=bn: e.tensor_tensor(impb[i][:], imp[gi][:, c, :], bn[:, c, :], ALU.add), reads=[(imp[gi], c), bn], writes=[impb[i]])
        for i in range(8):
            S.op("dve", lambda e, i=i: e.max(m8[i][:, 0:8], impb[i][:]), reads=[impb[i]], writes=[(m8[i], 0)])
        for i in range(8):
            S.op("dve", lambda e, i=i: e.match_replace(wk[i][:], m8[i][:, 0:8], impb[i][:], -3.0e38), reads=[impb[i], (m8[i], 0)], writes=[wk[i]])
        for i in range(8):
            S.op("dve", lambda e, i=i: e.max(m8[i][:, 8:16], wk[i][:]), reads=[wk[i]], writes=[(m8[i], 1)])
        for i in range(8):
            S.op("dve", lambda e, i=i: e.tensor_scalar(sn[i][:], impb[i][:], m8[i][:, 15:16], NEGM, ALU.is_lt, ALU.mult), reads=[impb[i], (m8[i], 1)], writes=[sn[i]])
        for i, (gi, c) in enumerate(chains):
            ptg = g.pTb[gi]
            S.op("pe", lambda e, i=i, c=c, ptg=ptg: e.transpose(ptg[0:NSEL, c * 128:(c + 1) * 128], sn[i][:], g.ident_bf[:]), reads=[sn[i], g.ident_bf], writes=[ptg])
        for gi in range(2):
            ptg = g.pTb[gi]
            if gi == 0:
                S.op("act", lambda e, gi=gi, ptg=ptg: e.activation(selT[:, gi, :], ptg[0:NSEL, 0:512], AF.Copy), reads=[ptg], writes=[(selT, gi)])
            else:
                S.op("dve", lambda e, gi=gi, ptg=ptg: e.tensor_copy(selT[:, gi, :], ptg[0:NSEL, 0:512]), reads=[ptg], writes=[(selT, gi)])
        for gi in range(2):
            for hh in range(4):
                head = 4 * gi + hh
                bank = accb[hd[0] % 2]
                hd[0] += 1
                av = acc_views(bank)

                def after_w(bank=bank, head=head, gates=gates):
                    a4 = bank[:].rearrange("p (c e) -> p c e", e=128)
                    fin.gt = gates
                    fin.run([(bank, a4[:, c, :]) for c in range(4)], gates[:, :, head * 3 + 2], y, head * 64, False)

                kts = [kt for kt in range((2 * P - 1) * 4, (2 * P + 2) * 4) if kt >= 0]
                groups = [kts[i:i + 2] for i in range(0, len(kts), 2)]
                for gidx, tl in enumerate(groups):
                    r = tl[0] - (2 * P - 1) * 4
                    st.add(score=[(Kwin[gi][0:68, kt * 128:(kt + 1) * 128], Q[0:68, head, :], [Kwin[gi], Q], None) for kt in tl],
                           mask=(wmask[:, r:r + 2, :], wmask),
                           pv=[(V4[:, kt, 2 + gi, :], [(V4, kt // 16)]) for kt in tl],
                           acc=av, first=(gidx == 0), last=(gidx == len(groups) - 1),
                           after=(after_w if gidx == len(groups) - 1 else None))
        nkt = (2 * P + 2) * 4
        for gi in range(2):
            for hh in range(4):
                head = 4 * gi + hh
                bank = accb[hd[0] % 2]
                hd[0] += 1
                av = acc_views(bank)

                def after_s(bank=bank, head=head, gates=gates):
                    a4 = bank[:].rearrange("p (c e) -> p c e", e=128)
                    fin.gt = gates
                    fin.run([(bank, a4[:, c, :]) for c in range(4)], gates[:, :, head * 3 + 1], y, head * 64, False)

                dmax = int(np.ceil(ALIBI_CUT * 2.0 ** (head + 1)))
                kt_min = max(0, (2 * P * 512 - dmax) // 128) // 2 * 2
                kt_min = min(kt_min, nkt - 8)
                groups = [list(range(t0, t0 + 2)) for t0 in range(kt_min, nkt, 2)]
                for gidx, tl in enumerate(groups):
                    r = tl[0] - (nkt - 8)
                    st.add(score=[(Kslc[gi][0:68, kt * 128:(kt + 1) * 128], Q[0:68, head, :], [Kslc[gi], Q],
                                   (Et[:, kt, :], selT[:, gi, :], [Et, (selT, gi)])) for kt in tl],
                           mask=((dmask[:, r:r + 2, :], dmask) if r >= 0 else None),
                           pv=[(V4[:, kt, gi, :], [(V4, kt // 16)]) for kt in tl],
                           acc=av, first=(gidx == 0), last=(gidx == len(groups) - 1),
                           after=(after_s if gidx == len(groups) - 1 else None))
        st.run()
        S.op("act", lambda e: e.activation(ybf[:], y[:], AF.Copy), reads=[y], writes=[ybf])
        transpose_to(S, g, ybf, yT, nk=4)
        S.dma("sp", g.ynsaT[P], yT[:], reads=[yT], writes=["ynsaT"])
        if g.debug:
            S.dma("sp", g.dbg_y[P], y[:], reads=[y], writes=["dbg_y"])
            S.dma("sp", g.dbg_sel[P], selT[:], reads=[selT], writes=["dbg_sel"])


def phase_2b(S, g, I, hs):
    SEQ, NP, NKT = g.SEQ, g.NP, g.NKT
    scale = 96.0 ** -0.5
    accb = [g.pG[4], g.pG[5]]
    Kmla = S.sbuf("Kmla", [96, 4, SEQ], BF16)
    Vm = S.sbuf("Vm", [128, NKT, 4, 65], BF16)
    dmask = S.sbuf("dmaskb", [128, 8, 512], BF16)
    def load_k(j):
        S.dma("act", Kmla[0:64, j, :], g.knopeT[4 * hs + j], reads=["knopeT"], writes=[(Kmla, j)])
        S.dma("act", Kmla[64:96, j, :], g.krotT, reads=["krotT"], writes=[(Kmla, j)])

    load_k(0)
    S.dma("act", dmask[:], I["dmask"], writes=[dmask])
    for k0 in range(0, NKT, 16):
        k1 = min(NKT, k0 + 16)
        S.dma("sp", Vm[:, k0:k1], g.vmla[k0:k1, :, 4 * hs:4 * hs + 4, :].rearrange("k p h e -> p k h e"), reads=["vmla"], writes=[(Vm, k0 // 16)])
    for j in range(1, 4):
        load_k(j)
    Qm = [S.sbuf("Qm%d" % i, [96, 4, 512], BF16) for i in range(2)]
    y = [S.sbuf("ymla%d" % i, [128, 4, 256], F32) for i in range(2)]
    ybf = S.sbuf("ybfb", [128, 4, 256], BF16)
    yT = S.sbuf("yTb", [128, 2, 512], BF16)
    fin = Fin(S, "b%d" % hs)
    st = Stream(S, g, "b%d" % hs)
    hd = [0]
    S.dma("sp", Qm[0][:], g.qmlaT[0][:, 4 * hs:4 * hs + 4, :], reads=["qmlaT"], writes=[Qm[0]])
    for P in range(NP):
        b = P % 2
        if P + 1 < NP:
            S.dma("sp", Qm[1 - b][:], g.qmlaT[P + 1][:, 4 * hs:4 * hs + 4, :], reads=["qmlaT"], writes=[Qm[1 - b]])
        Q, yy = Qm[b], y[b]
        nkt = (2 * P + 2) * 4
        for j in range(4):
            bank = accb[hd[0] % 2]
            hd[0] += 1
            av = acc_views(bank)

            def after(bank=bank, j=j, yy=yy):
                a4 = bank[:].rearrange("p (c e) -> p c e", e=128)
                fin.run([(bank, a4[:, c, :]) for c in range(4)], None, yy, j * 64, True)

            groups = [list(range(t0, t0 + 2)) for t0 in range(0, nkt, 2)]
            for gidx, tl in enumerate(groups):
                r = tl[0] - (nkt - 8)
                st.add(score=[(Kmla[0:96, j, kt * 128:(kt + 1) * 128], Q[0:96, j, :], [(Kmla, j), Q], None) for kt in tl],
                       mask=((dmask[:, r:r + 2, :], dmask) if r >= 0 else None), scale=scale,
                       pv=[(Vm[:, kt, j, :], [(Vm, kt // 16)]) for kt in tl],
                       acc=av, first=(gidx == 0), last=(gidx == len(groups) - 1),
                       after=(after if gidx == len(groups) - 1 else None))
        st.run()
        S.op("act", lambda e, yy=yy: e.activation(ybf[:], yy[:], AF.Copy), reads=[yy], writes=[ybf])
        transpose_to(S, g, ybf, yT, nk=2)
        S.dma("sp", g.ymlaT[P][:, 2 * hs:2 * hs + 2, :], yT[:], reads=[yT], writes=["ymlaT"])


def phase_3a(S, g, I):
    NP = g.NP
    rot = Rot(g.pG)
    Wpn = S.sbuf("Wpn", [128, 4, D], BF16)
    Wpm = S.sbuf("Wpm", [128, 4, D], BF16)
    Wo = S.sbuf("Wo", [128, 8, D], BF16)
    Wgm = S.sbuf("Wgm", [128, 8, 2048], BF16)
    gpost = S.sbuf("gpostM", [128, D], F32)
    cols = S.sbuf("cols3a", [128, 16], F32)
    S.dma("sp", cols[:, 0:8], I["mix_precol"], writes=[(cols, 0)])
    S.dma("sp", cols[:, 12:14], I["hm"], writes=[(cols, 3)])
    S.dma("sp", gpost[:], I["mix_postrow"], writes=[gpost])
    m0 = S.mark()
    st = [S.sbuf("st3a%d" % i, [128, 2048], F32) for i in range(3)]
    i = 0

    def load_plain(dst, src, nk):
        nonlocal i
        sv = src.rearrange("(k p) n -> p k n", p=128)
        for k0 in range(0, nk, 2):
            s_ = st[i % 3]
            S.dma(("sp", "act")[i % 2], s_[:].rearrange("p (j n) -> p j n", j=2), sv[:, k0:k0 + 2, :], writes=[s_])
            cast_copy(S, i, dst[:, k0:k0 + 2, :], s_[:].rearrange("p (j n) -> p j n", j=2), [s_], [dst])
            i += 1

    load_plain(Wpn, I["w_proj_nsa"], 4)
    load_plain(Wpm, I["w_proj_mla"], 4)
    for k in range(8):
        s_ = st[i % 3]
        S.dma(("sp", "act")[i % 2], s_[:], I["w_in_p"][k * 128:(k + 1) * 128, OGM:OGM + 2048], writes=[s_])
        cast_copy(S, i, Wgm[:, k, :], s_[:], [s_, (cols, 0)], [Wgm], scale=cols[:, k:k + 1])
        i += 1
    load_plain(Wo, I["w_out"], 8)
    hOs = [S.sbuf("hO3%d" % i, [128, 8, 512], BF16) for i in range(2)]
    ynTs = [S.sbuf("ynT%d" % i, [128, 4, 512], BF16) for i in range(2)]
    ymTs = [S.sbuf("ymT%d" % i, [128, 4, 512], BF16) for i in range(2)]
    xAs = [S.sbuf("x1A%d" % i, [128, 4, D], F32) for i in range(2)]
    xBs = [S.sbuf("x1B%d" % i, [128, 4, D], F32) for i in range(2)]
    mg = S.sbuf("mg", [128, 8, 512], BF16)
    ga = S.sbuf("ga", [128, 512], F32)
    gb = S.sbuf("gb", [128, 512], F32)
    t1 = S.sbuf("t13", [128, 512], F32)
    t2 = S.sbuf("t23", [128, 512], F32)
    junk = S.sbuf("junk3", [128, 512], BF16)
    B = {"st": S.sbuf("stat3", [128, 32], F32)}
    stt = B["st"]

    def load_slot(P):
        b = P % 2
        S.dma("sp", hOs[b][:], g.hmTown[P], reads=["hmTown"], writes=[hOs[b]])
        S.dma("sp", ynTs[b][:], g.ynsaT[P], reads=["ynsaT"], writes=[ynTs[b]])
        S.dma("sp", ymTs[b][:], g.ymlaT[P], reads=["ymlaT"], writes=[ymTs[b]])
        S.dma("sp", xAs[b][:], g.x1s[2 * P], reads=["x1s"], writes=[xAs[b]])
        S.dma("sp", xBs[b][:], g.x1s[2 * P + 1], reads=["x1s"], writes=[xBs[b]])

    load_slot(0)
    for P in range(NP):
        if P + 1 < NP:
            load_slot(P + 1)
        hO, ynT, ymT, xA, xB = hOs[P % 2], ynTs[P % 2], ymTs[P % 2], xAs[P % 2], xBs[P % 2]
        for c in range(4):
            S.op("act", lambda e, c=c, xA=xA: e.activation(xA[:, c, :], xA[:, c, :], AF.Copy, scale=cols[:, 12:13]), reads=[(xA, c), (cols, 3)], writes=[(xA, c)])
            S.op("dve", lambda e, c=c, xA=xA, xB=xB: e.scalar_tensor_tensor(xA[:, c, :], xB[:, c, :], cols[:, 13:14], xA[:, c, :], ALU.mult, ALU.add),
                 reads=[(xB, c), (xA, c), (cols, 3)], writes=[(xA, c)])
        for m in range(8):
            ms = slice(m * 128, (m + 1) * 128)
            pn, pm, pa, pb = rot(), rot(), rot(), rot()
            for f in range(4):
                S.op("pe", lambda e, f=f, pn=pn, ms=ms, ynT=ynT: e.matmul(pn[:], Wpn[:, f, ms], ynT[:, f, :], start=(f == 0), stop=(f == 3)), reads=[Wpn, ynT], writes=[pn])
            for f in range(4):
                S.op("pe", lambda e, f=f, pm=pm, ms=ms, ymT=ymT: e.matmul(pm[:], Wpm[:, f, ms], ymT[:, f, :], start=(f == 0), stop=(f == 3)), reads=[Wpm, ymT], writes=[pm])
            for k in range(8):
                S.op("pe", lambda e, k=k, pa=pa, m=m, hO=hO: e.matmul(pa[:], Wgm[:, k, m * 128:(m + 1) * 128], hO[:, k, :], start=(k == 0), stop=(k == 7)), reads=[Wgm, hO], writes=[pa])
            for k in range(8):
                S.op("pe", lambda e, k=k, pb=pb, m=m, hO=hO: e.matmul(pb[:], Wgm[:, k, 1024 + m * 128:1024 + (m + 1) * 128], hO[:, k, :], start=(k == 0), stop=(k == 7)), reads=[Wgm, hO], writes=[pb])
            S.op("act", lambda e, pa=pa: e.activation(ga[:], pa[:], AF.Sigmoid), reads=[pa], writes=[ga])
            S.op("act", lambda e, pb=pb: e.activation(gb[:], pb[:], AF.Sigmoid), reads=[pb], writes=[gb])
            S.op("dve", lambda e, pn=pn: e.tensor_tensor(t1[:], ga[:], pn[:], ALU.mult), reads=[ga, pn], writes=[t1])
            S.op("dve", lambda e, pm=pm: e.tensor_tensor(t2[:], gb[:], pm[:], ALU.mult), reads=[gb, pm], writes=[t2])
            S.op("dve", lambda e, m=m: e.tensor_tensor(mg[:, m, :], t1[:], t2[:], ALU.add), reads=[t1, t2], writes=[(mg, m)])
        for c in range(4):
            po = [rot(), rot()]
            for nh in range(2):
                for m in range(8):
                    S.op("pe", lambda e, c=c, nh=nh, m=m, po=po: e.matmul(po[nh][:], mg[:, m, c * 128:(c + 1) * 128], Wo[:, m, nh * 512:(nh + 1) * 512], start=(m == 0), stop=(m == 7)),
                         reads=[(mg, m), Wo], writes=[po[nh]])
                S.op("act", lambda e, nh=nh, po=po: e.activation(junk[:], po[nh][:], AF.Square, accum_out=stt[:, 4 + nh:5 + nh]), reads=[po[nh]], writes=[junk, (stt, "ss2")])
            S.op("dve", lambda e: e.tensor_tensor(stt[:, 6:7], stt[:, 4:5], stt[:, 5:6], ALU.add), reads=[(stt, "ss2")], writes=[(stt, "ss2s")])
            rms_rstd(S, g, B, stt[:, 6:7], 1, stt[:, 12:13], [(stt, "ss2s")], D)
            for nh in range(2):
                S.op("dve", lambda e, nh=nh, po=po: e.scalar_tensor_tensor(t1[:], po[nh][:], stt[:, 12:13], gpost[:, nh * 512:(nh + 1) * 512], ALU.mult, ALU.mult),
                     reads=[po[nh], (stt, "rstd"), gpost], writes=[t1])
                S.op("dve", lambda e, c=c, nh=nh, xA=xA: e.tensor_tensor(xA[:, c, nh * 512:(nh + 1) * 512], t1[:], xA[:, c, nh * 512:(nh + 1) * 512], ALU.add),
                     reads=[t1, (xA, c)], writes=[(xA, c)])
        S.dma("sp", g.x2s[P], xA[:], reads=[xA], writes=["x2s"])


def dram_in(nc, name, shape, dt):
    return nc.dram_tensor(name, list(shape), dt, kind="ExternalInput").ap()


def build(SEQ, debug=False, stop_after=None):
    NT = SEQ // 512
    NP = NT // 2
    NKT = SEQ // 128
    NSEL = SEQ // 64
    NC = SEQ // 16
    NCT = max(1, NC // 128)
    nc = bass.Bass("TRN2", target_bir_lowering=False)
    S = Sched(nc)
    g = G()
    g.nc, g.S = nc, S
    g.SEQ, g.NT, g.NP, g.NKT, g.NSEL, g.NC, g.NCT = SEQ, NT, NP, NKT, NSEL, NC, NCT
    I = {}

    def inp(name, shape, dt=F32):
        I[name] = dram_in(nc, name, shape, dt)
        return I[name]

    nc.in_names = I

    def scratch(name, shape, dt):
        kind = "ExternalOutput" if debug else "Internal"
        return nc.dram_tensor(name, list(shape), dt, kind=kind).ap()

    inp("x", [SEQ, D])
    for f in ("ff1", "ff2"):
        inp(f + "_wg", [D, DFF]); inp(f + "_wu", [D, DFF]); inp(f + "_wd", [DFF, D])
        inp(f + "_precol", [128, 8]); inp(f + "_postrow", [128, D])
    inp("mix_precol", [128, 8]); inp("mix_postrow", [128, D])
    inp("ident_bf", [128, 128], BF16); inp("ident_f", [128, 128])
    out = nc.dram_tensor("out", [NP * 512, D], F32, kind="ExternalOutput").ap()

    inp("w_in_p", [D, 3768]); inp("mla_w_uq", [256, 768]); inp("mla_w_ukv_p", [128, 1024])
    inp("q_norm_col", [128, 2]); inp("kv_norm_col", [128, 1]); inp("hm", [128, 2])
    inp("cosk", [32, SEQ]); inp("sink", [32, SEQ]); inp("cosq", [32, NP * 512]); inp("sinq", [32, NP * 512])

    g.x1s = scratch("x1s", [NT, 128, 4, D], F32)
    g.hmTs = scratch("hmTs", [NT, 128, 8, 512], BF16)
    g.hmTown = scratch("hmTown", [NP, 128, 8, 512], BF16)
    g.kTs = scratch("kTs", [8, 64, SEQ + 16], BF16)
    g.vtm = scratch("vtm", [NKT, 128, 4, 65], BF16)
    g.knopeT = scratch("knopeT", [8, 64, SEQ], BF16)
    g.krotT = scratch("krotT", [32, SEQ], BF16)
    g.vmla = scratch("vmla", [NKT, 128, 8, 65], BF16)
    g.qnsaT = scratch("qnsaT", [NP, 64, 8, 512], BF16)
    g.qmlaT = scratch("qmlaT", [NP, 96, 8, 512], BF16)
    g.gnsa = scratch("gnsa", [NP, 128, 4, 24], F32)
    for kvn in ("k", "v"):
        inp("cmp_w1_" + kvn, [2048, 256]); inp("cmp_w2_" + kvn, [256, 64]); inp("cmp_posT_" + kvn, [64, 32])
    inp("ovl", [128, NCT, NSEL], BF16)
    inp("kaug", [4, SEQ], BF16); inp("kaug_c", [4, NCT * 128], BF16); inp("qaug", [NP, 4, 8, 512], BF16)
    inp("dmask", [128, 8, 512], BF16); inp("wmask", [128, 12, 512], BF16); inp("cmask", [NP, 128, NCT, 512], BF16)
    inp("bonus", [NP, 128, 4, NSEL]); inp("E", [NSEL, NKT, 128], BF16)
    inp("w_proj_nsa", [512, D]); inp("w_proj_mla", [512, D]); inp("w_out", [D, D])
    g.x2s = scratch("x2s", [NP, 128, 4, D], F32)
    g.ynsaT = scratch("ynsaT", [NP, 128, 4, 512], BF16)
    g.ymlaT = scratch("ymlaT", [NP, 128, 4, 512], BF16)
    g.debug = debug
    if debug:
        g.dbg_y = scratch("dbg_y", [NP, 128, 4, 512], F32)
        g.dbg_sel = scratch("dbg_sel", [NP, NSEL, 2, 512], BF16)
    g.kcT = scratch("kcT", [2, 64, NCT * 128], BF16)
    g.vcaug = scratch("vcaug", [2, 128, NCT, 65 + NSEL], BF16)

    g.pD = [S.psum("pD%d" % i, [128, 1024], F32) for i in range(3)]
    g.pG = [T("pG%d" % i, g.pD[i // 2].h[:, (i % 2) * 512:(i % 2 + 1) * 512]) for i in range(6)]
    g.pTb = [S.psum("pTb%d" % i, [128, 1024], BF16) for i in range(2)]

    g.ident_bf = S.sbuf("ident_bf", [128, 128], BF16)
    g.ident_f = S.sbuf("ident_f", [128, 128], F32)
    S.dma("sp", g.ident_bf[:], I["ident_bf"], writes=[g.ident_bf])
    S.dma("sp", g.ident_f[:], I["ident_f"], writes=[g.ident_f])
    base = S.mark()

    W = load_ffn_weights(S, g, I["ff1_wg"], I["ff1_wu"], I["ff1_wd"], I["ff1_precol"], I["ff1_postrow"], "f1")
    B = alloc_ffn_work(S, "f1")
    xt = S.sbuf("xt", [128, 4, D], F32)
    xv = I["x"].rearrange("(t c p) d -> t p c d", c=4, p=128)
    st1 = B["st"]
    for c in range(4):
        S.dma("sp", xt[:, c, :], xv[0][:, c, :], writes=[(xt, c)])
    for t in range(NT):
        def hmix_tr(c):
            pt = g.pTb[c % 2]
            for k in range(8):
                S.op("pe", lambda e, k=k: e.transpose(pt[:, k * 128:(k + 1) * 128], B["xn"][:, c, k * 128:(k + 1) * 128], g.ident_bf[:]),
                     reads=[(B["xn"], c), g.ident_bf], writes=[pt])
            dst = B["xnT"][:, :, c * 128:(c + 1) * 128]
            srcp = pt[:, 0:1024].rearrange("p (k n) -> p k n", n=128)
            if c % 2 == 0:
                S.op("dve", lambda e: e.tensor_copy(dst, srcp), reads=[pt], writes=[B["xnT"]])
            else:
                S.op("act", lambda e: e.activation(dst, srcp, AF.Copy), reads=[pt], writes=[B["xnT"]])

        def after_chunk(c, t=t):
            S.dma("sp", g.x1s[t][:, c, :], xt[:, c, :], reads=[(xt, c)], writes=["x1s"])
            S.op("act", lambda e: e.activation(B["junk"][:], xt[:, c, :], AF.Square, accum_out=st1[:, 20 + c:21 + c]),
                 reads=[(xt, c)], writes=[B["junk"], (st1, "hss%d" % c)])
            S.op("dve", lambda e: e.tensor_scalar(st1[:, 28:29], st1[:, 20 + c:21 + c], 1.0 / D, EPS, ALU.mult, ALU.add), reads=[(st1, "hss%d" % c)], writes=[(st1, "hms")])
            S.op("act", lambda e: e.activation(st1[:, 29:30], st1[:, 28:29], AF.Sqrt), reads=[(st1, "hms")], writes=[(st1, "hsd")])
            S.op("dve", lambda e: e.reciprocal(st1[:, 30:31], st1[:, 29:30]), reads=[(st1, "hsd")], writes=[(st1, "hrs")])
            if c % 2 == 0:
                S.op("act", lambda e: e.activation(B["xn"][:, c, :], xt[:, c, :], AF.Copy, scale=st1[:, 30:31]), reads=[(xt, c), (st1, "hrs")], writes=[(B["xn"], c)])
            else:
                S.op("dve", lambda e: e.tensor_scalar(B["xn"][:, c, :], xt[:, c, :], st1[:, 30:31], None, ALU.mult), reads=[(xt, c), (st1, "hrs")], writes=[(B["xn"], c)])
            if t + 1 < NT:
                S.dma("sp", xt[:, c, :], xv[t + 1][:, c, :], writes=[(xt, c)])
            if c > 0:
                hmix_tr(c - 1)

        wd_hook = (lambda j: W["load_wd_pair"](j) if j % 2 == 0 else None) if t == 0 else None
        ffn_tile(S, g, W, B, xt, after_chunk, gu_hook=wd_hook)
        hmix_tr(3)
        S.dma("act", g.hmTs[t], B["xnT"][:], reads=[B["xnT"]], writes=["hmTs"])
    S.release(base)
    if stop_after == "1a":
        S.finish()
        return nc

    phase_1b(S, g, I)
    S.release(base)
    if stop_after == "1b":
        S.finish()
        return nc

    phase_1c(S, g, I)
    S.release(base)
    if stop_after == "1c":
        S.finish()
        return nc

    phase_2a(S, g, I)
    S.release(base)
    if stop_after == "2a":
        S.finish()
        return nc

    for hs in range(2):
        phase_2b(S, g, I, hs)
        S.release(base)
    if stop_after == "2b":
        S.finish()
        return nc

    phase_3a(S, g, I)
    S.release(base)
    if stop_after == "3a":
        S.finish()
        return nc

    W2 = load_ffn_weights(S, g, I["ff2_wg"], I["ff2_wu"], I["ff2_wd"], I["ff2_precol"], I["ff2_postrow"], "f2")
    B2 = alloc_ffn_work(S, "f2")
    xt2 = S.sbuf("xt2", [128, 4, D], F32)
    ov = out.rearrange("(t c p) d -> t p c d", c=4, p=128)
    for c in range(4):
        S.dma("sp", xt2[:, c, :], g.x2s[0][:, c, :], reads=["x2s"], writes=[(xt2, c)])
    for P in range(NP):
        def after_chunk2(c, P=P):
            S.dma("sp", ov[P][:, c, :], xt2[:, c, :], reads=[(xt2, c)], writes=["out"])
            if P + 1 < NP:
                S.dma("sp", xt2[:, c, :], g.x2s[P + 1][:, c, :], reads=["x2s"], writes=[(xt2, c)])

        wd_hook2 = (lambda j: W2["load_wd_pair"](j) if j % 2 == 0 else None) if P == 0 else None
        ffn_tile(S, g, W2, B2, xt2, after_chunk2, gu_hook=wd_hook2)

    S.finish()
    return nc


BFNP = ml_dtypes.bfloat16


def host_tables(SEQ, half):
    NT = SEQ // 512
    NP = NT // 2
    NKT = SEQ // 128
    NSEL = SEQ // 64
    NC = SEQ // 16
    NCT = max(1, NC // 128)
    f32 = np.float32
    Tb = {}
    pos = np.arange(SEQ)
    qpos = np.concatenate([np.arange((2 * P + half) * 512, (2 * P + half + 1) * 512) for P in range(NP)])
    slopes = (2.0 ** -(np.arange(8) + 1.0)).astype(f32)
    freqs = (10000.0 ** (-np.arange(16, dtype=f32) / 16)).astype(f32)

    def cs(p):
        ang = (p.astype(f32)[None, :] * freqs[:, None]).astype(f32)
        c, s = np.cos(ang).astype(f32), np.sin(ang).astype(f32)
        return np.concatenate([c, c], 0), np.concatenate([s, s], 0)

    Tb["cosk"], Tb["sink"] = cs(pos)
    Tb["cosq"], Tb["sinq"] = cs(qpos)

    def aug_k(p):
        return np.stack([np.ones_like(p), np.ones_like(p), 128 * (p // 128), p % 128]).astype(f32)

    Tb["kaug"] = aug_k(pos).astype(BFNP)
    cend = 16 * np.arange(NCT * 128) + 31
    Tb["kaug_c"] = aug_k(cend).astype(BFNP)
    qa = np.zeros((NP, 4, 8, 512), f32)
    for P in range(NP):
        t = qpos[P * 512:(P + 1) * 512]
        for h in range(8):
            s = slopes[h]
            qa[P, 0, h] = -s * (128 * (t // 128))
            qa[P, 1, h] = -s * (t % 128)
            qa[P, 2, h] = s
            qa[P, 3, h] = s
    Tb["qaug"] = qa.astype(BFNP)
    j = np.arange(128)[:, None, None]
    i = np.arange(512)[None, None, :]
    r = np.arange(8)[None, :, None]
    Tb["dmask"] = np.where(128 * r + j > i + 512 * half, NEGM, 0.0).astype(BFNP)
    r = np.arange(12)[None, :, None]
    dist = (i + 512 * half) - (128 * r + j - 512)
    Tb["wmask"] = np.where((dist >= 0) & (dist < 512), 0.0, NEGM).astype(BFNP)
    cm = np.zeros((NP, 128, NCT, 512), f32)
    ce = cend.reshape(NCT, 128).T
    for P in range(NP):
        t = qpos[P * 512:(P + 1) * 512]
        cm[P] = np.where(ce[:, :, None] > t[None, None, :], NEGM, 0.0)
    Tb["cmask"] = cm.astype(BFNP)
    bo = np.zeros((NP, 128, 4, NSEL), f32)
    blk = np.arange(NSEL)[None, None, :]
    for P in range(NP):
        t = qpos[P * 512:(P + 1) * 512].reshape(4, 128).T
        cur = (t // 64)[:, :, None]
        forced = (blk == 0) | (blk == cur) | (blk == cur - 1)
        bo[P] = np.where(blk <= cur, np.where(forced, 1e4, 0.0), -1e30)
    Tb["bonus"] = bo
    E = np.zeros((NSEL, NKT, 128), f32)
    for kt in range(NKT):
        E[2 * kt, kt, 0:64] = 1.0
        E[2 * kt + 1, kt, 64:128] = 1.0
    Tb["E"] = E.astype(BFNP)
    n_c = (SEQ - 32) // 16 + 1
    c0 = np.arange(NCT * 128) * 16
    s0 = np.arange(NSEL) * 64
    ov = np.clip(np.minimum(c0[:, None] + 32, s0[None, :] + 64) - np.maximum(c0[:, None], s0[None, :]), 0, None) / 32.0
    ov[n_c:] = 0.0
    Tb["ovl"] = np.ascontiguousarray(ov.reshape(NCT, 128, NSEL).transpose(1, 0, 2)).astype(BFNP)
    Tb["hm"] = np.tile(np.array([[1.0 - half, float(half)]], f32), (128, 1))
    Tb["ident_bf"] = np.eye(128).astype(BFNP)
    Tb["ident_f"] = np.eye(128, dtype=f32)
    return Tb


W_IN_PERM = None


def w_in_perm():
    kv0 = 512
    def six(s):
        return list(range(kv0 + s * 128, kv0 + (s + 1) * 128))
    perm = list(range(0, 512)) + six(0) + six(1) + six(2) + six(4) + six(3) + six(5)
    perm += list(range(1560, 1688)) + list(range(1688, 1720)) + list(range(1304, 1560)) + list(range(1280, 1304))
    perm += list(range(1720, 3768))
    assert len(perm) == 3768 and len(set(perm)) == 3768
    return np.array(perm)


def host_weights(inp):
    f32 = np.float32
    col = lambda v, k: np.ascontiguousarray(np.asarray(v, f32).reshape(k, 128).T)
    row = lambda v: np.ascontiguousarray(np.broadcast_to(np.asarray(v, f32), (128, D)))
    Wm = {}
    for f in ("ff1", "ff2"):
        Wm[f + "_wg"] = np.asarray(inp[f + "_w_gate"], f32)
        Wm[f + "_wu"] = np.asarray(inp[f + "_w_up"], f32)
        Wm[f + "_wd"] = np.asarray(inp[f + "_w_down"], f32)
        Wm[f + "_precol"] = col(inp[f + "_pre_g"], 8)
        Wm[f + "_postrow"] = row(inp[f + "_post_g"])
    Wm["mix_precol"] = col(inp["mix_pre_g"], 8)
    Wm["mix_postrow"] = row(inp["mix_post_g"])
    Wm["w_in_p"] = np.ascontiguousarray(np.asarray(inp["w_in"], f32)[:, w_in_perm()])
    Wm["mla_w_uq"] = np.asarray(inp["mla_w_uq"], f32)
    ukv = np.asarray(inp["mla_w_ukv"], f32).reshape(128, 8, 2, 64)
    Wm["mla_w_ukv_p"] = np.ascontiguousarray(np.concatenate([ukv[:, :, 0, :].reshape(128, 512), ukv[:, :, 1, :].reshape(128, 512)], 1))
    Wm["q_norm_col"] = col(inp["mla_q_norm_g"], 2)
    Wm["kv_norm_col"] = col(inp["mla_kv_norm_g"], 1)
    for kvn in ("k", "v"):
        Wm["cmp_w1_" + kvn] = np.asarray(inp["cmp_w1_" + kvn], f32)
        Wm["cmp_w2_" + kvn] = np.asarray(inp["cmp_w2_" + kvn], f32)
        Wm["cmp_posT_" + kvn] = np.ascontiguousarray(np.asarray(inp["cmp_pos_" + kvn], f32).T)
    Wm["w_proj_nsa"] = np.asarray(inp["w_proj_nsa"], f32)
    Wm["w_proj_mla"] = np.asarray(inp["w_proj_mla"], f32)
    Wm["w_out"] = np.asarray(inp["w_out"], f32)
    return Wm


_CACHE = {}


def kernel(**inputs):
    x = np.asarray(inputs["x"], np.float32)
    Bn, SEQ, _ = x.shape
    assert 2 * Bn == 8
    if SEQ not in _CACHE:
        _CACHE[SEQ] = build(SEQ)
    nc = _CACHE[SEQ]
    Wm = host_weights(inputs)
    tabs = [host_tables(SEQ, 0), host_tables(SEQ, 1)]
    in_maps = []
    for core in range(8):
        b, half = core // 2, core % 2
        m = dict(Wm)
        m.update(tabs[half])
        m["x"] = np.ascontiguousarray(x[b])
        in_maps.append({k: v for k, v in m.items() if k in nc.in_names})
    res = run_bass_kernel_spmd(nc, in_maps, core_ids=list(range(8)))
    NP = SEQ // 1024
    out = np.empty((Bn, SEQ, D), np.float32)
    for core in range(8):
        b, half = core // 2, core % 2
        o = res.results[core]["out"]
        for P in range(NP):
            t = 2 * P + half
            out[b, t * 512:(t + 1) * 512] = o[P * 512:(P + 1) * 512]
    return out
```

```python
import numpy as np
import ml_dtypes
import concourse.bass as bass
import concourse.mybir as mybir
from concourse.bass_utils import run_bass_kernel_spmd

F32 = mybir.dt.float32
BF16 = mybir.dt.bfloat16
ALU = mybir.AluOpType
AF = mybir.ActivationFunctionType
AX = mybir.AxisListType


class T:
    def __init__(self, name, h):
        self.name = name
        self.h = h

    def __getitem__(self, idx):
        return self.h[idx]


class Sched:
    ENGS = ("pe", "act", "dve", "pool", "sp")

    def __init__(self, nc, n_dma_sems=32):
        self.nc = nc
        self.sem = {e: nc.alloc_semaphore("sem_" + e) for e in self.ENGS}
        self.cnt = {e: 0 for e in self.ENGS}
        self.prog = {e: [] for e in self.ENGS}
        self.seen = {e: {} for e in self.ENGS}
        self.dsem = [nc.alloc_semaphore("dsem%d" % i) for i in range(n_dma_sems)]
        self.dcnt = [0] * n_dma_sems
        self.dnext = 0
        self.state = {}
        self.n_ops = 0

    SB_BASE = 16512
    SB_END = 229376 - 64

    def sbuf(self, name, shape, dtype):
        esz = {F32: 4, BF16: 2}.get(dtype, 4)
        n = 1
        for d in shape[1:]:
            n *= d
        nbytes = (n * esz + 63) // 64 * 64
        off = getattr(self, "_sb", self.SB_BASE)
        assert off + nbytes <= self.SB_END, ("SBUF overflow", name, off, nbytes)
        self._sb = off + nbytes
        self._uid = getattr(self, "_uid", 0) + 1
        uname = "%s_%d" % (name, self._uid)
        return T(uname, self.nc.alloc_sbuf_tensor_at(uname, list(shape), dtype, offset=off))

    def mark(self):
        return getattr(self, "_sb", self.SB_BASE)

    def release(self, mark):
        self.barrier()
        self._sb = mark

    def barrier(self):
        targets = []
        for k, c in enumerate(self.dcnt):
            if c > 0:
                targets.append((("d", k), c))
        for e in self.ENGS:
            if self.cnt[e] > 0:
                targets.append((("e", e), self.cnt[e]))
        for e in self.ENGS:
            waits = []
            for sk, val in targets:
                if sk == ("e", e):
                    continue
                if self.seen[e].get(sk, 0) >= val:
                    continue
                self.seen[e][sk] = val
                waits.append((sk, val))
            if waits:
                self.prog[e].append((waits, None, None))

    def psum(self, name, shape, dtype=F32):
        return T(name, self.nc.alloc_psum_tensor(name, list(shape), dtype))

    def _key(self, b):
        if isinstance(b, tuple):
            t, slot = b
        else:
            t, slot = b, None
        name = t.name if isinstance(t, T) else t
        return name, slot

    def _states(self, b, create=True):
        name, slot = self._key(b)
        d = self.state.setdefault(name, {})
        if None not in d:
            d[None] = {"w": None, "r": {}}
        if slot is None:
            return list(d.values())
        if slot not in d:
            d[slot] = {"w": d[None]["w"], "r": dict(d[None]["r"])}
        return [d[slot]]

    def _deps(self, reads, writes):
        deps = {}

        def add(sk, val):
            if val > deps.get(sk, 0):
                deps[sk] = val

        for b in reads:
            for st in self._states(b):
                if st["w"] is not None:
                    add(*st["w"])
        for b in writes:
            for st in self._states(b):
                if st["w"] is not None:
                    add(*st["w"])
                for sk, val in st["r"].items():
                    add(sk, val)
        return deps

    def _update(self, reads, writes, sk, val):
        for b in reads:
            for st in self._states(b):
                if val > st["r"].get(sk, 0):
                    st["r"][sk] = val
        for b in writes:
            for st in self._states(b):
                st["w"] = (sk, val)
                st["r"] = {}

    def _waits(self, eng, deps):
        waits = []
        for sk, val in deps.items():
            if sk == ("e", "pe") and eng == "pe":
                continue
            if self.seen[eng].get(sk, 0) >= val:
                continue
            self.seen[eng][sk] = val
            waits.append((sk, val))
        return waits

    def op(self, eng, fn, reads=(), writes=()):
        deps = self._deps(reads, writes)
        waits = self._waits(eng, deps)
        self.cnt[eng] += 1
        val = self.cnt[eng]
        self.prog[eng].append((waits, fn, ("e", eng)))
        self._update(reads, writes, ("e", eng), val)
        self.n_ops += 1

    def dma(self, eng, out, in_, reads=(), writes=()):
        k = self.dnext
        self.dnext = (self.dnext + 1) % len(self.dsem)
        deps = self._deps(reads, writes)
        if self.dcnt[k] > 0:
            sk = ("d", k)
            if self.dcnt[k] > deps.get(sk, 0):
                deps[sk] = self.dcnt[k]
        waits = self._waits(eng, deps)
        self.dcnt[k] += 16
        val = self.dcnt[k]
        self.prog[eng].append((waits, lambda e, o=out, i=in_: e.dma_start(out=o, in_=i), ("d", k)))
        self._update(reads, writes, ("d", k), val)
        self.n_ops += 1

    def _semh(self, sk):
        return self.sem[sk[1]] if sk[0] == "e" else self.dsem[sk[1]]

    def finish(self):
        final = []
        for k, c in enumerate(self.dcnt):
            if c > 0:
                final.append((("d", k), c))
        for e in self.ENGS:
            if e != "sp" and self.cnt[e] > 0:
                final.append((("e", e), self.cnt[e]))
        nc = self.nc
        prog = self.prog
        semh = self._semh

        def emit(engname, e):
            for waits, fn, inc in prog[engname]:
                for sk, val in waits:
                    e.wait_ge(semh(sk), val)
                if fn is None:
                    continue
                ins = fn(e)
                if inc[0] == "e":
                    ins.then_inc(semh(inc), 1)
                else:
                    ins.then_inc(semh(inc), 16)
            if engname == "sp":
                for sk, val in final:
                    e.wait_ge(semh(sk), val)

        with nc.Block() as block:
            @block.sync
            def _(e):
                emit("sp", e)

            if prog["pe"]:
                @block.tensor
                def _(e):
                    emit("pe", e)

            if prog["act"]:
                @block.scalar
                def _(e):
                    emit("act", e)

            if prog["dve"]:
                @block.vector
                def _(e):
                    emit("dve", e)

            if prog["pool"]:
                @block.gpsimd
                def _(e):
                    emit("pool", e)


D = 1024
DFF = 2816
NJ = DFF // 128
EPS = 1e-6
NEGM = -1.0e4
ALIBI_CUT = 164.0
OQ = 0
OKC = 512
OVC = 640
OKS = 768
OKW = 896
OVS = 1024
OVW = 1152
OCKV = 1280
OKPE = 1408
OCQ = 1440
OGN = 1696
OGM = 1720
WIN_A = 1720


class G:
    pass


def bc(ap_col):
    return ap_col


def load_ffn_weights(S, g, wg_d, wu_d, wd_d, precol_d, postrow_d, tag):
    W = {}
    W["wg"] = S.sbuf(tag + "wg", [128, 8, DFF], BF16)
    W["wu"] = S.sbuf(tag + "wu", [128, 8, DFF], BF16)
    W["wd"] = S.sbuf(tag + "wd", [128, NJ, D], BF16)
    W["ghalf"] = S.sbuf(tag + "ghalf", [128, D], F32)
    W["precol"] = S.sbuf(tag + "precol", [128, 8], F32)
    m = S.mark()
    st = [S.sbuf(tag + "st%d" % i, [128, DFF], F32) for i in range(5)]
    S.dma("sp", W["precol"][:], precol_d, writes=[W["precol"]])
    S.dma("sp", st[2][:, 0:D], postrow_d, writes=[st[2]])
    S.op("dve", lambda e: e.tensor_scalar(W["ghalf"][:], st[2][:, 0:D], 0.5, None, ALU.mult), reads=[st[2]], writes=[W["ghalf"]])
    i = 0
    for name, src in (("wg", wg_d), ("wu", wu_d)):
        for k in range(8):
            s_ = st[i % 5]
            S.dma(("sp", "act")[i % 2], s_[:], src[k * 128:(k + 1) * 128, :], writes=[s_])
            dst = W[name]
            if i % 2 == 0:
                S.op("act", lambda e, dst=dst, s_=s_, k=k: e.activation(dst[:, k, :], s_[:], AF.Copy, scale=W["precol"][:, k:k + 1]),
                     reads=[s_, W["precol"]], writes=[(dst, k)])
            else:
                S.op("dve", lambda e, dst=dst, s_=s_, k=k: e.tensor_scalar(dst[:, k, :], s_[:], W["precol"][:, k:k + 1], None, ALU.mult),
                     reads=[s_, W["precol"]], writes=[(dst, k)])
            i += 1
    S.release(m)
    wdst = S.sbuf(tag + "wdst", [128, 2 * D], F32)
    wdv = wd_d.rearrange("(j p) n -> p j n", p=128)

    def load_wd_pair(j0):
        S.dma("sp", wdst[:].rearrange("p (j n) -> p j n", j=2), wdv[:, j0:j0 + 2, :], writes=[wdst])
        if (j0 // 2) % 2 == 0:
            S.op("act", lambda e: e.activation(W["wd"][:, j0:j0 + 2, :], wdst[:].rearrange("p (j n) -> p j n", j=2), AF.Copy),
                 reads=[wdst], writes=[(W["wd"], j0), (W["wd"], j0 + 1)])
        else:
            S.op("dve", lambda e: e.tensor_copy(W["wd"][:, j0:j0 + 2, :], wdst[:].rearrange("p (j n) -> p j n", j=2)),
                 reads=[wdst], writes=[(W["wd"], j0), (W["wd"], j0 + 1)])

    W["load_wd_pair"] = load_wd_pair
    return W


def alloc_ffn_work(S, tag):
    B = {}
    B["xn"] = S.sbuf(tag + "xn", [128, 4, D], BF16)
    B["xnT"] = S.sbuf(tag + "xnT", [128, 8, 512], BF16)
    B["hT"] = S.sbuf(tag + "hT", [128, NJ, 512], BF16)
    B["sg"] = [S.sbuf(tag + "sg%d" % i, [128, 512], F32) for i in range(2)]
    B["tmp"] = S.sbuf(tag + "tmp", [128, 512], F32)
    B["junk"] = S.sbuf(tag + "junk", [128, D], BF16)
    B["st"] = S.sbuf(tag + "stat", [128, 32], F32)
    return B


def rms_rstd(S, g, B, ss_ap, n, out_ap, rd, nfeat):
    st = B["st"]
    S.op("dve", lambda e: e.tensor_scalar(st[:, 16:16 + n], ss_ap, 1.0 / nfeat, EPS, ALU.mult, ALU.add), reads=rd, writes=[(st, "ms")])
    S.op("act", lambda e: e.activation(st[:, 24:24 + n], st[:, 16:16 + n], AF.Sqrt), reads=[(st, "ms")], writes=[(st, "sd")])
    S.op("dve", lambda e: e.reciprocal(out_ap, st[:, 24:24 + n]), reads=[(st, "sd")], writes=[(st, "rstd")])


def norm_transpose(S, g, B, xt):
    st, xn, xnT, junk = B["st"], B["xn"], B["xnT"], B["junk"]
    for c in range(4):
        if c % 2 == 0:
            S.op("act", lambda e, c=c: e.activation(junk[:], xt[:, c, :], AF.Square, accum_out=st[:, c:c + 1]),
                 reads=[(xt, c)], writes=[junk, (st, "ss%d" % c)])
        else:
            S.op("dve", lambda e, c=c: e.scalar_tensor_tensor(xn[:, c, :], xt[:, c, :], 1.0, xt[:, c, :], ALU.mult, ALU.mult, accum_out=st[:, c:c + 1]),
                 reads=[(xt, c)], writes=[(xn, c), (st, "ss%d" % c)])
    rms_rstd(S, g, B, st[:, 0:4], 4, st[:, 8:12], [(st, "ss%d" % c) for c in range(4)], D)
    for c in range(4):
        if c % 2 == 0:
            S.op("act", lambda e, c=c: e.activation(xn[:, c, :], xt[:, c, :], AF.Copy, scale=st[:, 8 + c:9 + c]),
                 reads=[(xt, c), (st, "rstd")], writes=[(xn, c)])
        else:
            S.op("dve", lambda e, c=c: e.tensor_scalar(xn[:, c, :], xt[:, c, :], st[:, 8 + c:9 + c], None, ALU.mult),
                 reads=[(xt, c), (st, "rstd")], writes=[(xn, c)])
    transpose_to(S, g, xn, xnT)


def transpose_to(S, g, src, dstT, nk=8):
    for k0 in range(0, nk, 2):
        kk = min(2, nk - k0)
        pt = g.pTb[(k0 // 2) % 2]
        for k in range(k0, k0 + kk):
            for c in range(4):
                S.op("pe", lambda e, k=k, c=c, pt=pt, k0=k0: e.transpose(pt[:, (k - k0) * 512 + c * 128:(k - k0) * 512 + (c + 1) * 128], src[:, c, k * 128:(k + 1) * 128], g.ident_bf[:]),
                     reads=[(src, c), g.ident_bf], writes=[pt])
        dst = dstT[:, k0:k0 + kk, :]
        srcp = pt[:, 0:kk * 512].rearrange("p (k n) -> p k n", n=512)
        if (k0 // 2) % 2 == 0:
            S.op("dve", lambda e, dst=dst, srcp=srcp: e.tensor_copy(dst, srcp), reads=[pt], writes=[(dstT, k0), (dstT, k0 + kk - 1)])
        else:
            S.op("act", lambda e, dst=dst, srcp=srcp: e.activation(dst, srcp, AF.Copy), reads=[pt], writes=[(dstT, k0), (dstT, k0 + kk - 1)])


def ffn_tile(S, g, W, B, xt, after_chunk=None, gu_hook=None):
    st, xnT, hT, sg, tmp, junk = B["st"], B["xnT"], B["hT"], B["sg"], B["tmp"], B["junk"]
    norm_transpose(S, g, B, xt)
    P = g.pG
    for j in range(NJ):
        pg = P[(j % 2) * 2]
        pu = P[(j % 2) * 2 + 1]
        for k in range(8):
            S.op("pe", lambda e, j=j, k=k, pg=pg: e.matmul(pg[:], W["wg"][:, k, j * 128:(j + 1) * 128], xnT[:, k, :], start=(k == 0), stop=(k == 7)),
                 reads=[(W["wg"], k), (xnT, k)], writes=[pg])
        for k in range(8):
            S.op("pe", lambda e, j=j, k=k, pu=pu: e.matmul(pu[:], W["wu"][:, k, j * 128:(j + 1) * 128], xnT[:, k, :], start=(k == 0), stop=(k == 7)),
                 reads=[(W["wu"], k), (xnT, k)], writes=[pu])
        s_ = sg[j % 2]
        S.op("act", lambda e, pg=pg, s_=s_: e.activation(s_[:], pg[:], AF.Silu), reads=[pg], writes=[s_])
        S.op("dve", lambda e, pu=pu, s_=s_, j=j: e.tensor_tensor(hT[:, j, :], s_[:], pu[:], ALU.mult), reads=[s_, pu], writes=[(hT, j)])
        if gu_hook is not None:
            gu_hook(j)
    for c in range(4):
        pd = [P[(c % 2) * 2], P[(c % 2) * 2 + 1]]
        for nh in range(2):
            for j in range(NJ):
                S.op("pe", lambda e, c=c, nh=nh, j=j, pd=pd: e.matmul(pd[nh][:], hT[:, j, c * 128:(c + 1) * 128], W["wd"][:, j, nh * 512:(nh + 1) * 512], start=(j == 0), stop=(j == NJ - 1)),
                     reads=[(hT, j), (W["wd"], j)], writes=[pd[nh]])
            S.op("act", lambda e, nh=nh, pd=pd: e.activation(junk[:, 0:512], pd[nh][:], AF.Square, accum_out=st[:, 4 + nh:5 + nh]),
                 reads=[pd[nh]], writes=[junk, (st, "ss2")])
        S.op("dve", lambda e: e.tensor_tensor(st[:, 6:7], st[:, 4:5], st[:, 5:6], ALU.add), reads=[(st, "ss2")], writes=[(st, "ss2s")])
        rms_rstd(S, g, B, st[:, 6:7], 1, st[:, 12:13], [(st, "ss2s")], D)
        for nh in range(2):
            S.op("dve", lambda e, nh=nh, pd=pd: e.scalar_tensor_tensor(tmp[:], pd[nh][:], st[:, 12:13], W["ghalf"][:, nh * 512:(nh + 1) * 512], ALU.mult, ALU.mult),
                 reads=[pd[nh], (st, "rstd"), W["ghalf"]], writes=[tmp])
            S.op("dve", lambda e, c=c, nh=nh: e.tensor_tensor(xt[:, c, nh * 512:(nh + 1) * 512], tmp[:], xt[:, c, nh * 512:(nh + 1) * 512], ALU.add),
                 reads=[tmp, (xt, c)], writes=[(xt, c)])
        if after_chunk is not None:
            after_chunk(c)


class Rot:
    def __init__(self, banks):
        self.banks = banks
        self.i = 0

    def __call__(self):
        b = self.banks[self.i % len(self.banks)]
        self.i += 1
        return b


def cast_copy(S, i, out_ap, in_ap, reads, writes, scale=None):
    if i % 2 == 0:
        if scale is None:
            S.op("act", lambda e: e.activation(out_ap, in_ap, AF.Copy), reads=reads, writes=writes)
        else:
            S.op("act", lambda e: e.activation(out_ap, in_ap, AF.Copy, scale=scale), reads=reads, writes=writes)
    else:
        if scale is None:
            S.op("dve", lambda e: e.tensor_copy(out_ap, in_ap), reads=reads, writes=writes)
        else:
            S.op("dve", lambda e: e.tensor_scalar(out_ap, in_ap, scale, None, ALU.mult), reads=reads, writes=writes)


def phase_1b(S, g, I):
    SEQ, NT, NP = g.SEQ, g.NT, g.NP
    rot = Rot(g.pG)
    WinA = S.sbuf("WinA", [128, 8, WIN_A], BF16)
    Wrot = S.sbuf("Wrot", [128, 8, 32], BF16)
    Wuq = S.sbuf("Wuq", [128, 2, 768], BF16)
    Wuqr = S.sbuf("Wuqr", [128, 2, 768], BF16)
    Wukv = S.sbuf("Wukv", [128, 1024], BF16)
    cols = S.sbuf("cols1b", [128, 16], F32)
    S.dma("sp", cols[:, 0:8], I["mix_precol"], writes=[(cols, 0)])
    S.dma("sp", cols[:, 8:10], I["q_norm_col"], writes=[(cols, 1)])
    S.dma("sp", cols[:, 10:11], I["kv_norm_col"], writes=[(cols, 2)])
    S.dma("sp", cols[:, 12:14], I["hm"], writes=[(cols, 3)])
    m = S.mark()
    st = [S.sbuf("st1b%d" % i, [128, WIN_A], F32) for i in range(2)]
    for k in range(8):
        s_ = st[k % 2]
        S.dma(("sp", "act")[k % 2], s_[:], I["w_in_p"][k * 128:(k + 1) * 128, 0:WIN_A], writes=[s_])
        cast_copy(S, k, WinA[:, k, :], s_[:], [s_, (cols, 0)], [(WinA, k)], scale=cols[:, k:k + 1])
    S.op("act", lambda e: e.activation(Wrot[:, :, 0:16], WinA[:, :, OKPE + 16:OKPE + 32], AF.Copy, scale=-1.0), reads=[WinA], writes=[(Wrot, 0)])
    S.op("dve", lambda e: e.tensor_copy(Wrot[:, :, 16:32], WinA[:, :, OKPE:OKPE + 16]), reads=[WinA], writes=[(Wrot, 1)])
    for k2 in range(2):
        s_ = st[k2 % 2]
        S.dma("sp", s_[:, 0:768], I["mla_w_uq"][k2 * 128:(k2 + 1) * 128, :], writes=[s_])
        cast_copy(S, k2, Wuq[:, k2, :], s_[:, 0:768], [s_, (cols, 1)], [(Wuq, k2)], scale=cols[:, 8 + k2:9 + k2])
    S.op("pool", lambda e: e.memset(Wuqr[:], 0.0), writes=[Wuqr])
    Wuq4 = Wuq[:].rearrange("p k (h e) -> p k h e", e=96)
    Wuqr4 = Wuqr[:].rearrange("p k (h e) -> p k h e", e=96)
    for k2 in range(2):
        S.op("act", lambda e, k2=k2: e.activation(Wuqr4[:, k2, :, 64:80], Wuq4[:, k2, :, 80:96], AF.Copy, scale=-1.0), reads=[Wuq], writes=[Wuqr])
        S.op("dve", lambda e, k2=k2: e.tensor_copy(Wuqr4[:, k2, :, 80:96], Wuq4[:, k2, :, 64:80]), reads=[Wuq], writes=[Wuqr])
    s_ = st[0]
    S.dma("sp", s_[:, 0:1024], I["mla_w_ukv_p"], writes=[s_])
    cast_copy(S, 1, Wukv[:], s_[:, 0:1024], [s_, (cols, 2)], [Wukv], scale=cols[:, 10:11])
    hA = S.sbuf("hA", [128, 8, 512], BF16)
    hB = S.sbuf("hB", [128, 8, 512], BF16)
    hO = S.sbuf("hO", [128, 8, 512], BF16)
    kst = S.sbuf("kst", [64, 8, 512], BF16)
    vst = S.sbuf("vst", [128, 4, 4, 65], BF16)
    kn = S.sbuf("kn", [128, 4, 128], BF16)
    knT = S.sbuf("knT", [128, 1, 512], BF16)
    knst = S.sbuf("knst", [64, 8, 512], BF16)
    vmst = S.sbuf("vmst", [128, 4, 8, 65], BF16)
    krst = S.sbuf("krst", [32, 512], BF16)
    t1 = S.sbuf("t1", [96, 512], F32)
    t2 = S.sbuf("t2", [96, 512], F32)
    ck = S.sbuf("ck", [32, 512], F32)
    sk = S.sbuf("sk", [32, 512], F32)
    cq = S.sbuf("cq", [96, 512], F32)
    sq = S.sbuf("sq", [96, 512], F32)
    qst = S.sbuf("qst", [64, 8, 512], BF16)
    gst = S.sbuf("gst", [128, 4, 24], F32)
    qn = S.sbuf("qn", [128, 4, 256], BF16)
    qnT = S.sbuf("qnT", [128, 2, 512], BF16)
    qmst = S.sbuf("qmst", [96, 8, 512], BF16)
    junk = S.sbuf("junk1b", [128, 256], BF16)
    B = {"st": S.sbuf("stat1b", [128, 32], F32)}
    stt = B["st"]
    zt = S.sbuf("zt", [64, 8, 16], BF16)
    S.op("pool", lambda e: e.memset(zt[:], 0.0), writes=[zt])
    S.dma("sp", g.kTs.rearrange("t d s -> d t s")[:, :, SEQ:SEQ + 16], zt[:], reads=[zt], writes=["kTs"])
    S.op("pool", lambda e: e.memset(vst[:], 1.0), writes=[vst])
    S.op("pool", lambda e: e.memset(vmst[:], 1.0), writes=[vmst])
    cnt = [0]

    def cc(out_ap, in_ap, reads, writes, scale=None):
        cast_copy(S, cnt[0], out_ap, in_ap, reads, writes, scale)
        cnt[0] += 1

    def kside(T, h, hook=None):
        cs = slice(T * 512, (T + 1) * 512)
        for ti in range(8):
            off = OKC + ti * 64
            ps = rot()
            for k in range(8):
                S.op("pe", lambda e, k=k, ps=ps, off=off: e.matmul(ps[0:64, :], WinA[:, k, off:off + 64], h[:, k, :], start=(k == 0), stop=(k == 7)),
                     reads=[(WinA, k), h], writes=[ps])
            cc(kst[:, ti, :], ps[0:64, :], [ps], [(kst, ti)])
            if hook is not None:
                hook(ti)
        S.dma("sp", g.kTs.rearrange("t d s -> d t s")[:, :, cs], kst[:], reads=[kst], writes=["kTs"])
        for c in range(4):
            ps = rot()
            for k in range(8):
                S.op("pe", lambda e, k=k, c=c, ps=ps: e.matmul(ps[:, 0:384], h[:, k, c * 128:(c + 1) * 128], WinA[:, k, OVS:OVS + 384], start=(k == 0), stop=(k == 7)),
                     reads=[(WinA, k), h], writes=[ps])
            cc(vst[:, c, :, 0:64], ps[:, 0:256].rearrange("p (t e) -> p t e", e=64), [ps], [(vst, c)])
            S.op("act", lambda e, c=c, ps=ps: e.activation(junk[:, 0:128], ps[:, 256:384], AF.Square, accum_out=stt[:, c:c + 1]),
                 reads=[ps], writes=[junk, (stt, "ss")])
            rms_rstd(S, g, B, stt[:, c:c + 1], 1, stt[:, 8 + c:9 + c], [(stt, "ss")], 128)
            S.op("dve", lambda e, c=c, ps=ps: e.tensor_scalar(kn[:, c, :], ps[:, 256:384], stt[:, 8 + c:9 + c], None, ALU.mult),
                 reads=[ps, (stt, "rstd")], writes=[(kn, c)])
        S.dma("act", g.vtm[T * 4:(T + 1) * 4].rearrange("k p t e -> p k t e"), vst[:], reads=[vst], writes=["vtm"])
        transpose_to(S, g, kn, knT, nk=1)
        for hh in range(8):
            ps = rot()
            S.op("pe", lambda e, hh=hh, ps=ps: e.matmul(ps[0:64, :], Wukv[:, hh * 64:(hh + 1) * 64], knT[:, 0, :], start=True, stop=True),
                 reads=[Wukv, knT], writes=[ps])
            cc(knst[:, hh, :], ps[0:64, :], [ps], [(knst, hh)])
        S.dma("sp", g.knopeT.rearrange("h d s -> d h s")[:, :, cs], knst[:], reads=[knst], writes=["knopeT"])
        for c in range(4):
            ps = rot()
            S.op("pe", lambda e, c=c, ps=ps: e.matmul(ps[:], knT[:, 0, c * 128:(c + 1) * 128], Wukv[:, 512:1024], start=True, stop=True),
                 reads=[Wukv, knT], writes=[ps])
            cc(vmst[:, c, :, 0:64], ps[:].rearrange("p (h e) -> p h e", e=64), [ps], [(vmst, c)])
        S.dma("act", g.vmla[T * 4:(T + 1) * 4].rearrange("k p h e -> p k h e"), vmst[:], reads=[vmst], writes=["vmla"])
        S.dma("sp", ck[:], I["cosk"][:, cs], writes=[ck])
        S.dma("sp", sk[:], I["sink"][:, cs], writes=[sk])
        psa = rot()
        psb = rot()
        for k in range(8):
            S.op("pe", lambda e, k=k, psa=psa: e.matmul(psa[0:32, :], WinA[:, k, OKPE:OKPE + 32], h[:, k, :], start=(k == 0), stop=(k == 7)),
                 reads=[(WinA, k), h], writes=[psa])
        for k in range(8):
            S.op("pe", lambda e, k=k, psb=psb: e.matmul(psb[0:32, :], Wrot[:, k, :], h[:, k, :], start=(k == 0), stop=(k == 7)),
                 reads=[Wrot, h], writes=[psb])
        S.op("dve", lambda e, psa=psa: e.tensor_tensor(t1[0:32, :], psa[0:32, :], ck[:], ALU.mult), reads=[psa, ck], writes=[t1])
        S.op("dve", lambda e, psb=psb: e.tensor_tensor(t2[0:32, :], psb[0:32, :], sk[:], ALU.mult), reads=[psb, sk], writes=[t2])
        S.op("pool", lambda e: e.tensor_tensor(krst[:], t1[0:32, :], t2[0:32, :], ALU.add), reads=[t1, t2], writes=[krst])
        S.dma("sp", g.krotT[:, cs], krst[:], reads=[krst], writes=["krotT"])

    def qside(P, h):
        for hh in range(8):
            ps = rot()
            for k in range(8):
                S.op("pe", lambda e, k=k, ps=ps, hh=hh: e.matmul(ps[0:64, :], WinA[:, k, OQ + hh * 64:OQ + (hh + 1) * 64], h[:, k, :], start=(k == 0), stop=(k == 7)),
                     reads=[(WinA, k), h], writes=[ps])
            cc(qst[:, hh, :], ps[0:64, :], [ps], [(qst, hh)], scale=0.125)
        S.dma("sp", g.qnsaT[P], qst[:], reads=[qst], writes=["qnsaT"])
        for c in range(4):
            ps = rot()
            for k in range(8):
                S.op("pe", lambda e, k=k, c=c, ps=ps: e.matmul(ps[:, 0:280], h[:, k, c * 128:(c + 1) * 128], WinA[:, k, OCQ:OCQ + 280], start=(k == 0), stop=(k == 7)),
                     reads=[(WinA, k), h], writes=[ps])
            S.op("act", lambda e, c=c, ps=ps: e.activation(gst[:, c, :], ps[:, 256:280], AF.Sigmoid), reads=[ps], writes=[(gst, c)])
            S.op("act", lambda e, c=c, ps=ps: e.activation(junk[:, 0:256], ps[:, 0:256], AF.Square, accum_out=stt[:, c:c + 1]),
                 reads=[ps], writes=[junk, (stt, "ss")])
            rms_rstd(S, g, B, stt[:, c:c + 1], 1, stt[:, 8 + c:9 + c], [(stt, "ss")], 256)
            S.op("dve", lambda e, c=c, ps=ps: e.tensor_scalar(qn[:, c, :], ps[:, 0:256], stt[:, 8 + c:9 + c], None, ALU.mult),
                 reads=[ps, (stt, "rstd")], writes=[(qn, c)])
        S.dma("act", g.gnsa[P], gst[:], reads=[gst], writes=["gnsa"])
        transpose_to(S, g, qn, qnT, nk=2)
        qs = slice(P * 512, (P + 1) * 512)
        S.dma("sp", cq[64:96, :], I["cosq"][:, qs], writes=[cq])
        S.dma("sp", sq[64:96, :], I["sinq"][:, qs], writes=[sq])
        for hh in range(8):
            psa = rot()
            psb = rot()
            for k2 in range(2):
                S.op("pe", lambda e, k2=k2, psa=psa, hh=hh: e.matmul(psa[0:96, :], Wuq[:, k2, hh * 96:(hh + 1) * 96], qnT[:, k2, :], start=(k2 == 0), stop=(k2 == 1)),
                     reads=[Wuq, qnT], writes=[psa])
            for k2 in range(2):
                S.op("pe", lambda e, k2=k2, psb=psb, hh=hh: e.matmul(psb[0:96, :], Wuqr[:, k2, hh * 96:(hh + 1) * 96], qnT[:, k2, :], start=(k2 == 0), stop=(k2 == 1)),
                     reads=[Wuqr, qnT], writes=[psb])
            cc(qmst[0:64, hh, :], psa[0:64, :], [psa], [(qmst, hh)])
            S.op("dve", lambda e, psa=psa: e.tensor_tensor(t1[64:96, :], psa[64:96, :], cq[64:96, :], ALU.mult), reads=[psa, cq], writes=[t1])
            S.op("dve", lambda e, psb=psb: e.tensor_tensor(t2[64:96, :], psb[64:96, :], sq[64:96, :], ALU.mult), reads=[psb, sq], writes=[t2])
            S.op("pool", lambda e, hh=hh: e.tensor_tensor(qmst[64:96, hh, :], t1[64:96, :], t2[64:96, :], ALU.add), reads=[t1, t2], writes=[(qmst, hh)])
        S.dma("sp", g.qmlaT[P], qmst[:], reads=[qmst], writes=["qmlaT"])

    hAs = [hA, S.sbuf("hA2", [128, 8, 512], BF16)]
    hBs = [hB, S.sbuf("hB2", [128, 8, 512], BF16)]

    def load_pair(P):
        S.dma("sp", hAs[P % 2][:], g.hmTs[2 * P], reads=["hmTs"], writes=[hAs[P % 2]])
        S.dma("sp", hBs[P % 2][:], g.hmTs[2 * P + 1], reads=["hmTs"], writes=[hBs[P % 2]])

    load_pair(0)
    for P in range(NP):
        if P + 1 < NP:
            load_pair(P + 1)
        a_, b_ = hAs[P % 2], hBs[P % 2]

        def sel_k(k, a_=a_, b_=b_):
            S.op("act", lambda e: e.activation(hO[:, k, :], a_[:, k, :], AF.Copy, scale=cols[:, 12:13]), reads=[a_, (cols, 3)], writes=[(hO, k)])
            S.op("dve", lambda e: e.scalar_tensor_tensor(hO[:, k, :], b_[:, k, :], cols[:, 13:14], hO[:, k, :], ALU.mult, ALU.add),
                 reads=[b_, (hO, k), (cols, 3)], writes=[(hO, k)])

        kside(2 * P, a_, hook=sel_k)
        S.dma("act", g.hmTown[P], hO[:], reads=[hO], writes=["hmTown"])
        kside(2 * P + 1, b_)
        qside(P, hO)


def phase_1c(S, g, I):
    SEQ, NCT, NSEL = g.SEQ, g.NCT, g.NSEL
    NCP = NCT * 128
    rot = Rot(g.pG)
    w1b = [S.sbuf("w1b%d" % i, [64, 32, 256], BF16) for i in range(2)]
    w2b = [S.sbuf("w2b%d" % i, [128, 2, 64], BF16) for i in range(2)]
    posT = [S.sbuf("posT%d" % i, [64, 32], BF16) for i in range(2)]
    bias = [S.sbuf("cbias%d" % i, [128, 2], F32) for i in range(2)]
    srcT = [S.sbuf("csrcT%d" % i, [64, SEQ + 16], BF16) for i in range(2)]
    hid = S.sbuf("chid", [128, 2, NCP], BF16)
    u = S.sbuf("cu", [128, 512], F32)
    u2 = S.sbuf("cu2", [128, 512], F32)
    zz = S.sbuf("czz", [128, 512], F32)
    sgm = S.sbuf("csg", [128, 512], F32)
    kcst = S.sbuf("kcst", [64, NCP], BF16)
    vcst = S.sbuf("vcst", [128, NCT, 65 + NSEL], BF16)
    w2s = [S.sbuf("w2s%d" % i, [128, 2, 64], F32) for i in range(2)]
    pss = [S.sbuf("pss%d" % i, [64, 32], F32) for i in range(2)]
    stg = [S.sbuf("c1st%d" % i, [64, 16, 256], F32) for i in range(2)]
    S.op("pool", lambda e: e.memset(vcst[:], 1.0), writes=[vcst])
    S.dma("sp", vcst[:, :, 65:65 + NSEL], I["ovl"], reads=[], writes=[vcst])
    CH = min(512, NCP)
    order = [(0, 0), (0, 1), (1, 0), (1, 1)]
    S.dma("sp", srcT[0][:], g.kTs[0], reads=["kTs"], writes=[srcT[0]])
    for kvi, kvn in enumerate(("k", "v")):
        w1v = I["cmp_w1_" + kvn].rearrange("(l d) m -> d l m", d=64)
        for hh in range(2):
            S.dma(("sp", "act")[hh], stg[hh][:], w1v[:, hh * 16:(hh + 1) * 16, :], writes=[stg[hh]])
            cast_copy(S, hh, w1b[kvi][:, hh * 16:(hh + 1) * 16, :], stg[hh][:], [stg[hh]], [(w1b[kvi], hh)])
        S.dma("sp", w2s[kvi][:], I["cmp_w2_" + kvn].rearrange("(k p) d -> p k d", p=128), writes=[w2s[kvi]])
        cast_copy(S, 1, w2b[kvi][:], w2s[kvi][:], [w2s[kvi]], [w2b[kvi]])
        S.dma("sp", pss[kvi][:], I["cmp_posT_" + kvn], writes=[pss[kvi]])
        cast_copy(S, 1, posT[kvi][:], pss[kvi][:], [pss[kvi]], [posT[kvi]])
    for kvi in range(2):
        for mc in range(2):
            pb = rot()
            for l in range(32):
                S.op("pe", lambda e, l=l, mc=mc, pb=pb, kvi=kvi: e.matmul(pb[:, 0:1], w1b[kvi][:, l, mc * 128:(mc + 1) * 128], posT[kvi][:, l:l + 1], start=(l == 0), stop=(l == 31)),
                     reads=[w1b[kvi], posT[kvi]], writes=[pb])
            S.op("dve", lambda e, mc=mc, pb=pb, kvi=kvi: e.tensor_copy(bias[kvi][:, mc:mc + 1], pb[:, 0:1]), reads=[pb], writes=[(bias[kvi], mc)])
    for oi, (kvi, gi) in enumerate(order):
        src = srcT[oi % 2]
        if oi + 1 < len(order):
            nk, ng = order[oi + 1]
            S.dma("sp", srcT[(oi + 1) % 2][:], g.kTs[2 * nk + ng], reads=["kTs"], writes=[srcT[(oi + 1) % 2]])
        W1, W2, bs = w1b[kvi], w2b[kvi], bias[kvi]
        for mc in range(2):
            for b0 in range(0, NCP, CH):
                ps = rot()
                for l in range(32):
                    S.op("pe", lambda e, l=l, mc=mc, ps=ps, b0=b0, W1=W1, src=src: e.matmul(ps[:, 0:CH], W1[:, l, mc * 128:(mc + 1) * 128], src[:, l + 16 * b0: l + 16 * (b0 + CH - 1) + 1: 16], start=(l == 0), stop=(l == 31)),
                         reads=[W1, src], writes=[ps])
                S.op("dve", lambda e, mc=mc, ps=ps, bs=bs: e.tensor_scalar(u[:, 0:CH], ps[:, 0:CH], bs[:, mc:mc + 1], None, ALU.add), reads=[ps, (bs, mc)], writes=[u])
                S.op("act", lambda e: e.activation(u2[:, 0:CH], u[:, 0:CH], AF.Square), reads=[u], writes=[u2])
                S.op("dve", lambda e: e.tensor_scalar(u2[:, 0:CH], u2[:, 0:CH], 0.044715, 1.0, ALU.mult, ALU.add), reads=[u2], writes=[u2])
                S.op("dve", lambda e: e.tensor_tensor(zz[:, 0:CH], u[:, 0:CH], u2[:, 0:CH], ALU.mult), reads=[u, u2], writes=[zz])
                S.op("act", lambda e: e.activation(sgm[:, 0:CH], zz[:, 0:CH], AF.Sigmoid, scale=1.5957691216057308), reads=[zz], writes=[sgm])
                S.op("dve", lambda e, mc=mc, b0=b0: e.tensor_tensor(hid[:, mc, b0:b0 + CH], u[:, 0:CH], sgm[:, 0:CH], ALU.mult), reads=[u, sgm], writes=[(hid, mc)])
        if kvi == 0:
            for b0 in range(0, NCP, CH):
                ps = rot()
                for mc in range(2):
                    S.op("pe", lambda e, mc=mc, ps=ps, b0=b0, W2=W2: e.matmul(ps[0:64, 0:CH], W2[:, mc, :], hid[:, mc, b0:b0 + CH], start=(mc == 0), stop=(mc == 1)),
                         reads=[W2, hid], writes=[ps])
                cast_copy(S, 0, kcst[:, b0:b0 + CH], ps[0:64, 0:CH], [ps], [kcst])
            S.dma("sp", g.kcT[gi], kcst[:], reads=[kcst], writes=["kcT"])
        else:
            for it in range(NCT):
                ps = rot()
                for mc in range(2):
                    S.op("pe", lambda e, mc=mc, ps=ps, it=it, W2=W2: e.matmul(ps[:, 0:64], hid[:, mc, it * 128:(it + 1) * 128], W2[:, mc, :], start=(mc == 0), stop=(mc == 1)),
                         reads=[W2, hid], writes=[ps])
                cast_copy(S, it, vcst[:, it, 0:64], ps[:, 0:64], [ps], [vcst])
            S.dma("sp", g.vcaug[gi], vcst[:], reads=[vcst], writes=["vcaug"])


class Fin:
    def __init__(self, S, tag):
        self.S = S
        self.sc = S.sbuf(tag + "finsc", [128, 16], F32)

    def run(self, accs, gate_ap, y, col0, first, extra=None, sum_ap=None):
        S, sc = self.S, self.sc
        banks = []
        for b, _ in accs:
            if b not in banks:
                banks.append(b)
        if sum_ap is not None:
            S.op("dve", lambda e: e.tensor_scalar(sc[:, 0:4], sum_ap, 1e-30, None, ALU.max), reads=[accs[0][0]], writes=[(sc, "mx")])
        else:
            for c in range(4):
                S.op("dve", lambda e, c=c: e.tensor_scalar(sc[:, c:c + 1], accs[c][1][:, 64:65], 1e-30, None, ALU.max),
                     reads=[accs[c][0]], writes=[(sc, "mx")])
        S.op("dve", lambda e: e.reciprocal(sc[:, 4:8], sc[:, 0:4]), reads=[(sc, "mx")], writes=[(sc, "rs")])
        if gate_ap is not None:
            S.op("dve", lambda e: e.tensor_tensor(sc[:, 8:12], sc[:, 4:8], gate_ap, ALU.mult), reads=[(sc, "rs"), self.gt], writes=[(sc, "cg")])
            off = 8
        else:
            off = 4
        for c in range(4):
            if first:
                S.op("dve", lambda e, c=c: e.tensor_scalar(y[:, c, col0:col0 + 64], accs[c][1][:, 0:64], sc[:, off + c:off + c + 1], None, ALU.mult),
                     reads=[accs[c][0], (sc, "cg"), (sc, "rs")], writes=[(y, c)])
            else:
                S.op("dve", lambda e, c=c: e.scalar_tensor_tensor(y[:, c, col0:col0 + 64], accs[c][1][:, 0:64], sc[:, off + c:off + c + 1], y[:, c, col0:col0 + 64], ALU.mult, ALU.add),
                     reads=[accs[c][0], (sc, "cg"), (sc, "rs"), (y, c)], writes=[(y, c)])
            if extra is not None:
                extra(c, sc[:, 4 + c:5 + c])


class Stream:
    def __init__(self, S, g, tag):
        self.S, self.g = S, g
        self.pT = [S.sbuf(tag + "pT%d" % i, [128, 2, 512], BF16) for i in range(3)]
        self.n = 0
        self.ntm = 0
        self.items = []

    def add(self, **kw):
        self.items.append(kw)

    def _score(self, j):
        S, g = self.S, self.g
        it = self.items[j]
        d = it.get("d", it["slot"] % 2)
        mask = it.get("mask")
        for ti, (lhsT, rhs, reads, extra) in enumerate(it["score"]):
            bank = g.pG[2 * d + ti]
            mms = [(lhsT, rhs, reads)]
            if extra is not None:
                mms.append(extra)
            if mask is not None:
                mms.append((g.ident_bf[:], mask[0][:, ti, :], [g.ident_bf, mask[1]]))
            for mi, (l_, r_, rd_) in enumerate(mms):
                S.op("pe", lambda e, bank=bank, l_=l_, r_=r_, mi=mi, n=len(mms): e.matmul(bank[:], l_, r_, start=(mi == 0), stop=(mi == n - 1)),
                     reads=rd_, writes=[bank])

    def _exp(self, j):
        S, g = self.S, self.g
        it = self.items[j]
        d = it.get("d", it["slot"] % 2)
        nt = len(it["score"])
        pt = self.pT[it["slot"] % 3]
        banks = [g.pG[2 * d + ti] for ti in range(nt)]
        src = g.pD[d].h[:, 0:nt * 512].rearrange("p (t n) -> p t n", n=512)
        scale = it.get("scale", 1.0)
        S.op("act", lambda e: e.activation(pt[:, 0:nt, :], src, AF.Exp, scale=scale), reads=banks, writes=[pt])
        it["pt"] = pt

    def _pv(self, j):
        S = self.S
        it = self.items[j]
        pt = it["pt"]
        nt = len(it["score"])
        for ti in range(nt):
            v_ap, v_reads = it["pv"][ti]
            first = it["first"] and ti == 0
            last = it["last"] and ti == nt - 1
            for c in range(4):
                bank, out_ap, lead = it["acc"][c]
                S.op("pe", lambda e, out_ap=out_ap, pt=pt, ti=ti, c=c, v_ap=v_ap, first=first, last=last, lead=lead:
                     e.matmul(out_ap, pt[:, ti, c * 128:(c + 1) * 128], v_ap, start=(first and lead), stop=last, skip_group_check=True),
                     reads=[pt] + v_reads, writes=[bank])

    def run(self):
        items = self.items
        n = len(items)
        for j, it in enumerate(items):
            it["slot"] = self.n + j
        pending = []
        if n:
            self._score(0)
        fixed = any("d" in it for it in items)
        for j in range(n):
            if fixed:
                self._exp(j)
                if j + 1 < n:
                    self._score(j + 1)
            else:
                if j + 1 < n:
                    self._score(j + 1)
                self._exp(j)
            self._pv(j)
            pending = [(d - 1, f) for d, f in pending]
            for d, f in pending:
                if d <= 0:
                    f()
            pending = [(d, f) for d, f in pending if d > 0]
            if items[j].get("after") is not None:
                pending.append((items[j].get("defer", 2), items[j]["after"]))
        for d, f in pending:
            f()
        self.n += n
        self.items = []


def acc_views(bank, ncols=65):
    a4 = bank[:].rearrange("p (c e) -> p c e", e=128)
    return [(bank, a4[:, c, 0:ncols], c == 0) for c in range(4)]


def phase_2a(S, g, I):
    SEQ, NP, NKT, NSEL, NCT = g.SEQ, g.NP, g.NKT, g.NSEL, g.NCT
    NCP = NCT * 128
    NV = 65 + NSEL
    accb = [g.pG[4], g.pG[5]]
    Kslc = [S.sbuf("Kslc%d" % i, [68, SEQ], BF16) for i in range(2)]
    Kwin = [S.sbuf("Kwin%d" % i, [68, SEQ], BF16) for i in range(2)]
    V4 = S.sbuf("V4", [128, NKT, 4, 65], BF16)
    Kc = [S.sbuf("Kc%d" % i, [68, NCP], BF16) for i in range(2)]
    Vc = [S.sbuf("Vc%d" % i, [128, NCT, NV], BF16) for i in range(2)]
    Et = S.sbuf("Et", [NSEL, NKT, 128], BF16)
    dmask = S.sbuf("dmask", [128, 8, 512], BF16)
    wmask = S.sbuf("wmask", [128, 12, 512], BF16)
    def load_residents():
        for gi in range(2):
            S.dma("sp", Kc[gi][0:64, :], g.kcT[gi], reads=["kcT"], writes=[Kc[gi]])
            S.dma("sp", Kc[gi][64:68, :], I["kaug_c"], writes=[Kc[gi]])
            S.dma("act", Vc[gi][:], g.vcaug[gi], reads=["vcaug"], writes=[Vc[gi]])
        load_slot(0)
        S.dma("act", wmask[:], I["wmask"], writes=[wmask])
        for gi in range(2):
            S.dma("act", Kwin[gi][0:64, :], g.kTs[6 + gi][:, 0:SEQ], reads=["kTs"], writes=[Kwin[gi]])
            S.dma("act", Kwin[gi][64:68, :], I["kaug"], writes=[Kwin[gi]])
        for k0 in range(0, NKT, 16):
            k1 = min(NKT, k0 + 16)
            S.dma("sp", V4[:, k0:k1], g.vtm[k0:k1].rearrange("k p t e -> p k t e"), reads=["vtm"], writes=[(V4, k0 // 16)])
        S.dma("sp", dmask[:], I["dmask"], writes=[dmask])
        for gi in range(2):
            S.dma("sp", Kslc[gi][0:64, :], g.kTs[4 + gi][:, 0:SEQ], reads=["kTs"], writes=[Kslc[gi]])
            S.dma("sp", Kslc[gi][64:68, :], I["kaug"], writes=[Kslc[gi]])
        S.dma("act", Et[:], I["E"], writes=[Et])

    Qa = [S.sbuf("Qa%d" % i, [68, 8, 512], BF16) for i in range(2)]
    cm = [S.sbuf("cm%d" % i, [128, NCT, 512], BF16) for i in range(2)]
    bon = [S.sbuf("bon%d" % i, [128, 4, NSEL], F32) for i in range(2)]
    gt = [S.sbuf("gt%d" % i, [128, 4, 24], F32) for i in range(2)]
    y = S.sbuf("ynsa", [128, 4, 512], F32)
    imp = [S.sbuf("imp%d" % i, [128, 4, NSEL], F32) for i in range(2)]
    selT = S.sbuf("selT", [NSEL, 2, 512], BF16)
    impb = [S.sbuf("impb%d" % i, [128, NSEL], F32) for i in range(8)]
    wk = [S.sbuf("wk%d" % i, [128, NSEL], F32) for i in range(8)]
    m8 = [S.sbuf("m8_%d" % i, [128, 16], F32) for i in range(8)]
    sn = [S.sbuf("sn%d" % i, [128, NSEL], BF16) for i in range(8)]
    ybf = S.sbuf("ybf", [128, 4, 512], BF16)
    yT = S.sbuf("yT", [128, 4, 512], BF16)
    fin = Fin(S, "a")
    st = Stream(S, g, "a")
    hd = [0]

    def load_slot(P):
        b = P % 2
        S.dma("sp", Qa[b][0:64], g.qnsaT[P], reads=["qnsaT"], writes=[Qa[b]])
        S.dma("sp", Qa[b][64:68], I["qaug"][P], writes=[Qa[b]])
        S.dma("act", cm[b][:], I["cmask"][P], writes=[cm[b]])
        S.dma("act", bon[b][:], I["bonus"][P], writes=[bon[b]])
        S.dma("act", gt[b][:], g.gnsa[P], reads=["gnsa"], writes=[gt[b]])

    load_residents()
    for P in range(NP):
        b = P % 2
        if P + 1 < NP:
            load_slot(P + 1)
        Q, cmk, bn, gates = Qa[b], cm[b], bon[b], gt[b]
        fin.gt = gates
        ncmp = min(NCT, ((2 * P + 2) * 32 + 127) // 128)
        csets = []
        for si in range(2):
            b0, b1 = g.pG[2 + 2 * si], g.pG[3 + 2 * si]
            A = b0[:].rearrange("p (c e) -> p c e", e=256)
            Bk = b1[:].rearrange("p (c e) -> p c e", e=256)
            csets.append(((b0, b1), (A, Bk)))
        for gi in range(2):
            for hh in range(4):
                head = 4 * gi + hh
                (bks, vws) = csets[head % 2]
                cacc = [(bks[c // 2], vws[c // 2][:, c % 2, 0:NV], c % 2 == 0) for c in range(4)]
                groups = [list(range(t0, min(ncmp, t0 + 2))) for t0 in range(0, ncmp, 2)]

                def after(gi=gi, hh=hh, head=head, gates=gates, bks=bks, vws=vws):
                    accs = [(bks[c // 2], vws[c // 2][:, c % 2, :]) for c in range(4)]

                    def extra(c, rs_ap):
                        if hh == 0:
                            S.op("dve", lambda e: e.tensor_scalar(imp[gi][:, c, :], accs[c][1][:, 65:NV], rs_ap, None, ALU.mult),
                                 reads=[accs[c][0], (fin.sc, "rs")], writes=[(imp[gi], c)])
                        else:
                            S.op("dve", lambda e: e.scalar_tensor_tensor(imp[gi][:, c, :], accs[c][1][:, 65:NV], rs_ap, imp[gi][:, c, :], ALU.mult, ALU.add),
                                 reads=[accs[c][0], (fin.sc, "rs"), (imp[gi], c)], writes=[(imp[gi], c)])

                    fin.gt = gates
                    fin.run(accs, gates[:, :, head * 3 + 0], y, head * 64, True, extra)

                for gidx, tl in enumerate(groups):
                    st.add(score=[(Kc[gi][0:68, it * 128:(it + 1) * 128], Q[0:68, head, :], [Kc[gi], Q], None) for it in tl],
                           mask=(cmk[:, tl[0]:tl[-1] + 1, :], cmk), d=0, defer=1,
                           pv=[(Vc[gi][:, it, :], [Vc[gi]]) for it in tl],
                           acc=cacc, first=(gidx == 0), last=(gidx == len(groups) - 1),
                           after=(after if gidx == len(groups) - 1 else None))
        st.run()
        chains = [(gi, c) for gi in range(2) for c in range(4)]
        for i, (gi, c) in enumerate(chains):
            S.op("dve", lambda e, i=i, c=c, gi=gi, bn=bn: e.tensor_tensor(impb[i][:], imp[gi][:, c, :], bn[:, c, :], ALU.add), reads=[(imp[gi], c), bn], writes=[impb[i]])
        for i in range(8):
            S.op("dve", lambda e, i=i: e.max(m8[i][:, 0:8], impb[i][:]), reads=[impb[i]], writes=[(m8[i], 0)])
        for i in range(8):
            S.op("dve", lambda e, i=i: e.match_replace(wk[i][:], m8[i][:, 0:8], impb[i][:], -3.0e38), reads=[impb[i], (m8[i], 0)], writes=[wk[i]])
        for i in range(8):
            S.op("dve", lambda e, i=i: e.max(m8[i][:, 8:16], wk[i][:]), reads=[wk[i]], writes=[(m8[i], 1)])
        for i in range(8):
            S.op("dve", lambda e, i=i: e.tensor_scalar(sn[i][:], impb[i][:], m8[i][:, 15:16], NEGM, ALU.is_lt, ALU.mult), reads=[impb[i], (m8[i], 1)], writes=[sn[i]])
        for i, (gi, c) in enumerate(chains):
            ptg = g.pTb[gi]
            S.op("pe", lambda e, i=i, c=c, ptg=ptg: e.transpose(ptg[0:NSEL, c * 128:(c + 1) * 128], sn[i][:], g.ident_bf[:]), reads=[sn[i], g.ident_bf], writes=[ptg])
        for gi in range(2):
            ptg = g.pTb[gi]
            if gi == 0:
                S.op("act", lambda e, gi=gi, ptg=ptg: e.activation(selT[:, gi, :], ptg[0:NSEL, 0:512], AF.Copy), reads=[ptg], writes=[(selT, gi)])
            else:
                S.op("dve", lambda e, gi=gi, ptg=ptg: e.tensor_copy(selT[:, gi, :], ptg[0:NSEL, 0:512]), reads=[ptg], writes=[(selT, gi)])
        for gi in range(2):
            for hh in range(4):
                head = 4 * gi + hh
                bank = accb[hd[0] % 2]
                hd[0] += 1
                av = acc_views(bank)

                def after_w(bank=bank, head=head, gates=gates):
                    a4 = bank[:].rearrange("p (c e) -> p c e", e=128)
                    fin.gt = gates
                    fin.run([(bank, a4[:, c, :]) for c in range(4)], gates[:, :, head * 3 + 2], y, head * 64, False, sum_ap=a4[:, :, 64])

                kts = [kt for kt in range((2 * P - 1) * 4, (2 * P + 2) * 4) if kt >= 0]
                groups = [kts[i:i + 2] for i in range(0, len(kts), 2)]
                for gidx, tl in enumerate(groups):
                    r = tl[0] - (2 * P - 1) * 4
                    st.add(score=[(Kwin[gi][0:68, kt * 128:(kt + 1) * 128], Q[0:68, head, :], [Kwin[gi], Q], None) for kt in tl],
                           mask=(wmask[:, r:r + 2, :], wmask),
                           pv=[(V4[:, kt, 2 + gi, :], [(V4, kt // 16)]) for kt in tl],
                           acc=av, first=(gidx == 0), last=(gidx == len(groups) - 1),
                           after=(after_w if gidx == len(groups) - 1 else None))
        nkt = (2 * P + 2) * 4
        for gi in range(2):
            for hh in range(4):
                head = 4 * gi + hh
                bank = accb[hd[0] % 2]
                hd[0] += 1
                av = acc_views(bank)

                def after_s(bank=bank, head=head, gates=gates):
                    a4 = bank[:].rearrange("p (c e) -> p c e", e=128)
                    fin.gt = gates
                    fin.run([(bank, a4[:, c, :]) for c in range(4)], gates[:, :, head * 3 + 1], y, head * 64, False, sum_ap=a4[:, :, 64])

                dmax = int(np.ceil(ALIBI_CUT * 2.0 ** (head + 1)))
                kt_min = max(0, (2 * P * 512 - dmax) // 128) // 2 * 2
                kt_min = min(kt_min, nkt - 8)
                groups = [list(range(t0, t0 + 2)) for t0 in range(kt_min, nkt, 2)]
                for gidx, tl in enumerate(groups):
                    r = tl[0] - (nkt - 8)
                    st.add(score=[(Kslc[gi][0:68, kt * 128:(kt + 1) * 128], Q[0:68, head, :], [Kslc[gi], Q],
                                   (Et[:, kt, :], selT[:, gi, :], [Et, (selT, gi)])) for kt in tl],
                           mask=((dmask[:, r:r + 2, :], dmask) if r >= 0 else None),
                           pv=[(V4[:, kt, gi, :], [(V4, kt // 16)]) for kt in tl],
                           acc=av, first=(gidx == 0), last=(gidx == len(groups) - 1),
                           after=(after_s if gidx == len(groups) - 1 else None))
        st.run()
        S.op("act", lambda e: e.activation(ybf[:], y[:], AF.Copy), reads=[y], writes=[ybf])
        transpose_to(S, g, ybf, yT, nk=4)
        S.dma("sp", g.ynsaT[P], yT[:], reads=[yT], writes=["ynsaT"])
        if g.debug:
            S.dma("sp", g.dbg_y[P], y[:], reads=[y], writes=["dbg_y"])
            S.dma("sp", g.dbg_sel[P], selT[:], reads=[selT], writes=["dbg_sel"])


def phase_2b(S, g, I, hs):
    SEQ, NP, NKT = g.SEQ, g.NP, g.NKT
    scale = 96.0 ** -0.5
    accb = [g.pG[4], g.pG[5]]
    Kmla = S.sbuf("Kmla", [96, 4, SEQ], BF16)
    Vm = S.sbuf("Vm", [128, NKT, 4, 65], BF16)
    dmask = S.sbuf("dmaskb", [128, 8, 512], BF16)
    def load_k(j):
        S.dma("act", Kmla[0:64, j, :], g.knopeT[4 * hs + j], reads=["knopeT"], writes=[(Kmla, j)])
        S.dma("act", Kmla[64:96, j, :], g.krotT, reads=["krotT"], writes=[(Kmla, j)])

    load_k(0)
    S.dma("act", dmask[:], I["dmask"], writes=[dmask])
    for k0 in range(0, NKT, 16):
        k1 = min(NKT, k0 + 16)
        S.dma("sp", Vm[:, k0:k1], g.vmla[k0:k1, :, 4 * hs:4 * hs + 4, :].rearrange("k p h e -> p k h e"), reads=["vmla"], writes=[(Vm, k0 // 16)])
    for j in range(1, 4):
        load_k(j)
    Qm = [S.sbuf("Qm%d" % i, [96, 4, 512], BF16) for i in range(2)]
    y = [S.sbuf("ymla%d" % i, [128, 4, 256], F32) for i in range(2)]
    ybf = S.sbuf("ybfb", [128, 4, 256], BF16)
    yT = S.sbuf("yTb", [128, 2, 512], BF16)
    fin = Fin(S, "b%d" % hs)
    st = Stream(S, g, "b%d" % hs)
    hd = [0]
    S.dma("sp", Qm[0][:], g.qmlaT[0][:, 4 * hs:4 * hs + 4, :], reads=["qmlaT"], writes=[Qm[0]])
    for P in range(NP):
        b = P % 2
        if P + 1 < NP:
            S.dma("sp", Qm[1 - b][:], g.qmlaT[P + 1][:, 4 * hs:4 * hs + 4, :], reads=["qmlaT"], writes=[Qm[1 - b]])
        Q, yy = Qm[b], y[b]
        nkt = (2 * P + 2) * 4
        for j in range(4):
            bank = accb[hd[0] % 2]
            hd[0] += 1
            av = acc_views(bank)

            def after(bank=bank, j=j, yy=yy):
                a4 = bank[:].rearrange("p (c e) -> p c e", e=128)
                fin.run([(bank, a4[:, c, :]) for c in range(4)], None, yy, j * 64, True, sum_ap=a4[:, :, 64])

            groups = [list(range(t0, t0 + 2)) for t0 in range(0, nkt, 2)]
            for gidx, tl in enumerate(groups):
                r = tl[0] - (nkt - 8)
                st.add(score=[(Kmla[0:96, j, kt * 128:(kt + 1) * 128], Q[0:96, j, :], [(Kmla, j), Q], None) for kt in tl],
                       mask=((dmask[:, r:r + 2, :], dmask) if r >= 0 else None), scale=scale,
                       pv=[(Vm[:, kt, j, :], [(Vm, kt // 16)]) for kt in tl],
                       acc=av, first=(gidx == 0), last=(gidx == len(groups) - 1),
                       after=(after if gidx == len(groups) - 1 else None))
        st.run()
        S.op("act", lambda e, yy=yy: e.activation(ybf[:], yy[:], AF.Copy), reads=[yy], writes=[ybf])
        transpose_to(S, g, ybf, yT, nk=2)
        S.dma("sp", g.ymlaT[P][:, 2 * hs:2 * hs + 2, :], yT[:], reads=[yT], writes=["ymlaT"])


def phase_3a(S, g, I):
    NP = g.NP
    rot = Rot(g.pG)
    Wpn = S.sbuf("Wpn", [128, 4, D], BF16)
    Wpm = S.sbuf("Wpm", [128, 4, D], BF16)
    Wo = S.sbuf("Wo", [128, 8, D], BF16)
    Wgm = S.sbuf("Wgm", [128, 8, 2048], BF16)
    gpost = S.sbuf("gpostM", [128, D], F32)
    cols = S.sbuf("cols3a", [128, 16], F32)
    S.dma("sp", cols[:, 0:8], I["mix_precol"], writes=[(cols, 0)])
    S.dma("sp", cols[:, 12:14], I["hm"], writes=[(cols, 3)])
    S.dma("sp", gpost[:], I["mix_postrow"], writes=[gpost])
    m0 = S.mark()
    st = [S.sbuf("st3a%d" % i, [128, 2048], F32) for i in range(3)]
    i = 0

    def load_plain(dst, src, nk):
        nonlocal i
        sv = src.rearrange("(k p) n -> p k n", p=128)
        for k0 in range(0, nk, 2):
            s_ = st[i % 3]
            S.dma(("sp", "act")[i % 2], s_[:].rearrange("p (j n) -> p j n", j=2), sv[:, k0:k0 + 2, :], writes=[s_])
            cast_copy(S, i, dst[:, k0:k0 + 2, :], s_[:].rearrange("p (j n) -> p j n", j=2), [s_], [dst])
            i += 1

    load_plain(Wpn, I["w_proj_nsa"], 4)
    load_plain(Wpm, I["w_proj_mla"], 4)
    for k in range(8):
        s_ = st[i % 3]
        S.dma(("sp", "act")[i % 2], s_[:], I["w_in_p"][k * 128:(k + 1) * 128, OGM:OGM + 2048], writes=[s_])
        cast_copy(S, i, Wgm[:, k, :], s_[:], [s_, (cols, 0)], [Wgm], scale=cols[:, k:k + 1])
        i += 1
    load_plain(Wo, I["w_out"], 8)
    hOs = [S.sbuf("hO3%d" % i, [128, 8, 512], BF16) for i in range(2)]
    ynTs = [S.sbuf("ynT%d" % i, [128, 4, 512], BF16) for i in range(2)]
    ymTs = [S.sbuf("ymT%d" % i, [128, 4, 512], BF16) for i in range(2)]
    xAs = [S.sbuf("x1A%d" % i, [128, 4, D], F32) for i in range(2)]
    xBs = [S.sbuf("x1B%d" % i, [128, 4, D], F32) for i in range(2)]
    mg = S.sbuf("mg", [128, 8, 512], BF16)
    ga = S.sbuf("ga", [128, 512], F32)
    gb = S.sbuf("gb", [128, 512], F32)
    t1 = S.sbuf("t13", [128, 512], F32)
    t2 = S.sbuf("t23", [128, 512], F32)
    junk = S.sbuf("junk3", [128, 512], BF16)
    B = {"st": S.sbuf("stat3", [128, 32], F32)}
    stt = B["st"]

    def load_slot(P):
        b = P % 2
        S.dma("sp", hOs[b][:], g.hmTown[P], reads=["hmTown"], writes=[hOs[b]])
        S.dma("sp", ynTs[b][:], g.ynsaT[P], reads=["ynsaT"], writes=[ynTs[b]])
        S.dma("sp", ymTs[b][:], g.ymlaT[P], reads=["ymlaT"], writes=[ymTs[b]])
        S.dma("sp", xAs[b][:], g.x1s[2 * P], reads=["x1s"], writes=[xAs[b]])
        S.dma("sp", xBs[b][:], g.x1s[2 * P + 1], reads=["x1s"], writes=[xBs[b]])

    load_slot(0)
    for P in range(NP):
        if P + 1 < NP:
            load_slot(P + 1)
        hO, ynT, ymT, xA, xB = hOs[P % 2], ynTs[P % 2], ymTs[P % 2], xAs[P % 2], xBs[P % 2]
        for c in range(4):
            S.op("act", lambda e, c=c, xA=xA: e.activation(xA[:, c, :], xA[:, c, :], AF.Copy, scale=cols[:, 12:13]), reads=[(xA, c), (cols, 3)], writes=[(xA, c)])
            S.op("dve", lambda e, c=c, xA=xA, xB=xB: e.scalar_tensor_tensor(xA[:, c, :], xB[:, c, :], cols[:, 13:14], xA[:, c, :], ALU.mult, ALU.add),
                 reads=[(xB, c), (xA, c), (cols, 3)], writes=[(xA, c)])
        for m in range(8):
            ms = slice(m * 128, (m + 1) * 128)
            pn, pm, pa, pb = rot(), rot(), rot(), rot()
            for f in range(4):
                S.op("pe", lambda e, f=f, pn=pn, ms=ms, ynT=ynT: e.matmul(pn[:], Wpn[:, f, ms], ynT[:, f, :], start=(f == 0), stop=(f == 3)), reads=[Wpn, ynT], writes=[pn])
            for f in range(4):
                S.op("pe", lambda e, f=f, pm=pm, ms=ms, ymT=ymT: e.matmul(pm[:], Wpm[:, f, ms], ymT[:, f, :], start=(f == 0), stop=(f == 3)), reads=[Wpm, ymT], writes=[pm])
            for k in range(8):
                S.op("pe", lambda e, k=k, pa=pa, m=m, hO=hO: e.matmul(pa[:], Wgm[:, k, m * 128:(m + 1) * 128], hO[:, k, :], start=(k == 0), stop=(k == 7)), reads=[Wgm, hO], writes=[pa])
            for k in range(8):
                S.op("pe", lambda e, k=k, pb=pb, m=m, hO=hO: e.matmul(pb[:], Wgm[:, k, 1024 + m * 128:1024 + (m + 1) * 128], hO[:, k, :], start=(k == 0), stop=(k == 7)), reads=[Wgm, hO], writes=[pb])
            S.op("act", lambda e, pa=pa: e.activation(ga[:], pa[:], AF.Sigmoid), reads=[pa], writes=[ga])
            S.op("act", lambda e, pb=pb: e.activation(gb[:], pb[:], AF.Sigmoid), reads=[pb], writes=[gb])
            S.op("dve", lambda e, pn=pn: e.tensor_tensor(t1[:], ga[:], pn[:], ALU.mult), reads=[ga, pn], writes=[t1])
            S.op("dve", lambda e, pm=pm: e.tensor_tensor(t2[:], gb[:], pm[:], ALU.mult), reads=[gb, pm], writes=[t2])
            S.op("dve", lambda e, m=m: e.tensor_tensor(mg[:, m, :], t1[:], t2[:], ALU.add), reads=[t1, t2], writes=[(mg, m)])
        for c in range(4):
            po = [rot(), rot()]
            for nh in range(2):
                for m in range(8):
                    S.op("pe", lambda e, c=c, nh=nh, m=m, po=po: e.matmul(po[nh][:], mg[:, m, c * 128:(c + 1) * 128], Wo[:, m, nh * 512:(nh + 1) * 512], start=(m == 0), stop=(m == 7)),
                         reads=[(mg, m), Wo], writes=[po[nh]])
                S.op("act", lambda e, nh=nh, po=po: e.activation(junk[:], po[nh][:], AF.Square, accum_out=stt[:, 4 + nh:5 + nh]), reads=[po[nh]], writes=[junk, (stt, "ss2")])
            S.op("dve", lambda e: e.tensor_tensor(stt[:, 6:7], stt[:, 4:5], stt[:, 5:6], ALU.add), reads=[(stt, "ss2")], writes=[(stt, "ss2s")])
            rms_rstd(S, g, B, stt[:, 6:7], 1, stt[:, 12:13], [(stt, "ss2s")], D)
            for nh in range(2):
                S.op("dve", lambda e, nh=nh, po=po: e.scalar_tensor_tensor(t1[:], po[nh][:], stt[:, 12:13], gpost[:, nh * 512:(nh + 1) * 512], ALU.mult, ALU.mult),
                     reads=[po[nh], (stt, "rstd"), gpost], writes=[t1])
                S.op("dve", lambda e, c=c, nh=nh, xA=xA: e.tensor_tensor(xA[:, c, nh * 512:(nh + 1) * 512], t1[:], xA[:, c, nh * 512:(nh + 1) * 512], ALU.add),
                     reads=[t1, (xA, c)], writes=[(xA, c)])
        S.dma("sp", g.x2s[P], xA[:], reads=[xA], writes=["x2s"])


def dram_in(nc, name, shape, dt):
    return nc.dram_tensor(name, list(shape), dt, kind="ExternalInput").ap()


def build(SEQ, debug=False, stop_after=None):
    NT = SEQ // 512
    NP = NT // 2
    NKT = SEQ // 128
    NSEL = SEQ // 64
    NC = SEQ // 16
    NCT = max(1, NC // 128)
    nc = bass.Bass("TRN2", target_bir_lowering=False)
    S = Sched(nc)
    g = G()
    g.nc, g.S = nc, S
    g.SEQ, g.NT, g.NP, g.NKT, g.NSEL, g.NC, g.NCT = SEQ, NT, NP, NKT, NSEL, NC, NCT
    I = {}

    def inp(name, shape, dt=F32):
        I[name] = dram_in(nc, name, shape, dt)
        return I[name]

    nc.in_names = I

    def scratch(name, shape, dt):
        kind = "ExternalOutput" if debug else "Internal"
        return nc.dram_tensor(name, list(shape), dt, kind=kind).ap()

    inp("x", [SEQ, D])
    for f in ("ff1", "ff2"):
        inp(f + "_wg", [D, DFF]); inp(f + "_wu", [D, DFF]); inp(f + "_wd", [DFF, D])
        inp(f + "_precol", [128, 8]); inp(f + "_postrow", [128, D])
    inp("mix_precol", [128, 8]); inp("mix_postrow", [128, D])
    inp("ident_bf", [128, 128], BF16); inp("ident_f", [128, 128])
    out = nc.dram_tensor("out", [NP * 512, D], F32, kind="ExternalOutput").ap()

    inp("w_in_p", [D, 3768]); inp("mla_w_uq", [256, 768]); inp("mla_w_ukv_p", [128, 1024])
    inp("q_norm_col", [128, 2]); inp("kv_norm_col", [128, 1]); inp("hm", [128, 2])
    inp("cosk", [32, SEQ]); inp("sink", [32, SEQ]); inp("cosq", [32, NP * 512]); inp("sinq", [32, NP * 512])

    g.x1s = scratch("x1s", [NT, 128, 4, D], F32)
    g.hmTs = scratch("hmTs", [NT, 128, 8, 512], BF16)
    g.hmTown = scratch("hmTown", [NP, 128, 8, 512], BF16)
    g.kTs = scratch("kTs", [8, 64, SEQ + 16], BF16)
    g.vtm = scratch("vtm", [NKT, 128, 4, 65], BF16)
    g.knopeT = scratch("knopeT", [8, 64, SEQ], BF16)
    g.krotT = scratch("krotT", [32, SEQ], BF16)
    g.vmla = scratch("vmla", [NKT, 128, 8, 65], BF16)
    g.qnsaT = scratch("qnsaT", [NP, 64, 8, 512], BF16)
    g.qmlaT = scratch("qmlaT", [NP, 96, 8, 512], BF16)
    g.gnsa = scratch("gnsa", [NP, 128, 4, 24], F32)
    for kvn in ("k", "v"):
        inp("cmp_w1_" + kvn, [2048, 256]); inp("cmp_w2_" + kvn, [256, 64]); inp("cmp_posT_" + kvn, [64, 32])
    inp("ovl", [128, NCT, NSEL], BF16)
    inp("kaug", [4, SEQ], BF16); inp("kaug_c", [4, NCT * 128], BF16); inp("qaug", [NP, 4, 8, 512], BF16)
    inp("dmask", [128, 8, 512], BF16); inp("wmask", [128, 12, 512], BF16); inp("cmask", [NP, 128, NCT, 512], BF16)
    inp("bonus", [NP, 128, 4, NSEL]); inp("E", [NSEL, NKT, 128], BF16)
    inp("w_proj_nsa", [512, D]); inp("w_proj_mla", [512, D]); inp("w_out", [D, D])
    g.x2s = scratch("x2s", [NP, 128, 4, D], F32)
    g.ynsaT = scratch("ynsaT", [NP, 128, 4, 512], BF16)
    g.ymlaT = scratch("ymlaT", [NP, 128, 4, 512], BF16)
    g.debug = debug
    if debug:
        g.dbg_y = scratch("dbg_y", [NP, 128, 4, 512], F32)
        g.dbg_sel = scratch("dbg_sel", [NP, NSEL, 2, 512], BF16)
    g.kcT = scratch("kcT", [2, 64, NCT * 128], BF16)
    g.vcaug = scratch("vcaug", [2, 128, NCT, 65 + NSEL], BF16)

    g.pD = [S.psum("pD%d" % i, [128, 1024], F32) for i in range(3)]
    g.pG = [T("pG%d" % i, g.pD[i // 2].h[:, (i % 2) * 512:(i % 2 + 1) * 512]) for i in range(6)]
    g.pTb = [S.psum("pTb%d" % i, [128, 1024], BF16) for i in range(2)]

    g.ident_bf = S.sbuf("ident_bf", [128, 128], BF16)
    g.ident_f = S.sbuf("ident_f", [128, 128], F32)
    S.dma("sp", g.ident_bf[:], I["ident_bf"], writes=[g.ident_bf])
    S.dma("sp", g.ident_f[:], I["ident_f"], writes=[g.ident_f])
    base = S.mark()

    W = load_ffn_weights(S, g, I["ff1_wg"], I["ff1_wu"], I["ff1_wd"], I["ff1_precol"], I["ff1_postrow"], "f1")
    B = alloc_ffn_work(S, "f1")
    xt = S.sbuf("xt", [128, 4, D], F32)
    xv = I["x"].rearrange("(t c p) d -> t p c d", c=4, p=128)
    st1 = B["st"]
    for c in range(4):
        S.dma("sp", xt[:, c, :], xv[0][:, c, :], writes=[(xt, c)])
    for t in range(NT):
        def hmix_tr(c):
            pt = g.pTb[c % 2]
            for k in range(8):
                S.op("pe", lambda e, k=k: e.transpose(pt[:, k * 128:(k + 1) * 128], B["xn"][:, c, k * 128:(k + 1) * 128], g.ident_bf[:]),
                     reads=[(B["xn"], c), g.ident_bf], writes=[pt])
            dst = B["xnT"][:, :, c * 128:(c + 1) * 128]
            srcp = pt[:, 0:1024].rearrange("p (k n) -> p k n", n=128)
            if c % 2 == 0:
                S.op("dve", lambda e: e.tensor_copy(dst, srcp), reads=[pt], writes=[B["xnT"]])
            else:
                S.op("act", lambda e: e.activation(dst, srcp, AF.Copy), reads=[pt], writes=[B["xnT"]])

        def after_chunk(c, t=t):
            S.dma("sp", g.x1s[t][:, c, :], xt[:, c, :], reads=[(xt, c)], writes=["x1s"])
            S.op("act", lambda e: e.activation(B["junk"][:], xt[:, c, :], AF.Square, accum_out=st1[:, 20 + c:21 + c]),
                 reads=[(xt, c)], writes=[B["junk"], (st1, "hss%d" % c)])
            S.op("dve", lambda e: e.tensor_scalar(st1[:, 28:29], st1[:, 20 + c:21 + c], 1.0 / D, EPS, ALU.mult, ALU.add), reads=[(st1, "hss%d" % c)], writes=[(st1, "hms")])
            S.op("act", lambda e: e.activation(st1[:, 29:30], st1[:, 28:29], AF.Sqrt), reads=[(st1, "hms")], writes=[(st1, "hsd")])
            S.op("dve", lambda e: e.reciprocal(st1[:, 30:31], st1[:, 29:30]), reads=[(st1, "hsd")], writes=[(st1, "hrs")])
            if c % 2 == 0:
                S.op("act", lambda e: e.activation(B["xn"][:, c, :], xt[:, c, :], AF.Copy, scale=st1[:, 30:31]), reads=[(xt, c), (st1, "hrs")], writes=[(B["xn"], c)])
            else:
                S.op("dve", lambda e: e.tensor_scalar(B["xn"][:, c, :], xt[:, c, :], st1[:, 30:31], None, ALU.mult), reads=[(xt, c), (st1, "hrs")], writes=[(B["xn"], c)])
            if t + 1 < NT:
                S.dma("sp", xt[:, c, :], xv[t + 1][:, c, :], writes=[(xt, c)])
            if c > 0:
                hmix_tr(c - 1)

        wd_hook = (lambda j: W["load_wd_pair"](j) if j % 2 == 0 else None) if t == 0 else None
        ffn_tile(S, g, W, B, xt, after_chunk, gu_hook=wd_hook)
        hmix_tr(3)
        S.dma("act", g.hmTs[t], B["xnT"][:], reads=[B["xnT"]], writes=["hmTs"])
    S.release(base)
    if stop_after == "1a":
        S.finish()
        return nc

    phase_1b(S, g, I)
    S.release(base)
    if stop_after == "1b":
        S.finish()
        return nc

    phase_1c(S, g, I)
    S.release(base)
    if stop_after == "1c":
        S.finish()
        return nc

    phase_2a(S, g, I)
    S.release(base)
    if stop_after == "2a":
        S.finish()
        return nc

    for hs in range(2):
        phase_2b(S, g, I, hs)
        S.release(base)
    if stop_after == "2b":
        S.finish()
        return nc

    phase_3a(S, g, I)
    S.release(base)
    if stop_after == "3a":
        S.finish()
        return nc

    W2 = load_ffn_weights(S, g, I["ff2_wg"], I["ff2_wu"], I["ff2_wd"], I["ff2_precol"], I["ff2_postrow"], "f2")
    B2 = alloc_ffn_work(S, "f2")
    xt2 = S.sbuf("xt2", [128, 4, D], F32)
    ov = out.rearrange("(t c p) d -> t p c d", c=4, p=128)
    for c in range(4):
        S.dma("sp", xt2[:, c, :], g.x2s[0][:, c, :], reads=["x2s"], writes=[(xt2, c)])
    for P in range(NP):
        def after_chunk2(c, P=P):
            S.dma("sp", ov[P][:, c, :], xt2[:, c, :], reads=[(xt2, c)], writes=["out"])
            if P + 1 < NP:
                S.dma("sp", xt2[:, c, :], g.x2s[P + 1][:, c, :], reads=["x2s"], writes=[(xt2, c)])

        wd_hook2 = (lambda j: W2["load_wd_pair"](j) if j % 2 == 0 else None) if P == 0 else None
        ffn_tile(S, g, W2, B2, xt2, after_chunk2, gu_hook=wd_hook2)

    S.finish()
    return nc


BFNP = ml_dtypes.bfloat16


def host_tables(SEQ, half):
    NT = SEQ // 512
    NP = NT // 2
    NKT = SEQ // 128
    NSEL = SEQ // 64
    NC = SEQ // 16
    NCT = max(1, NC // 128)
    f32 = np.float32
    Tb = {}
    pos = np.arange(SEQ)
    qpos = np.concatenate([np.arange((2 * P + half) * 512, (2 * P + half + 1) * 512) for P in range(NP)])
    slopes = (2.0 ** -(np.arange(8) + 1.0)).astype(f32)
    freqs = (10000.0 ** (-np.arange(16, dtype=f32) / 16)).astype(f32)

    def cs(p):
        ang = (p.astype(f32)[None, :] * freqs[:, None]).astype(f32)
        c, s = np.cos(ang).astype(f32), np.sin(ang).astype(f32)
        return np.concatenate([c, c], 0), np.concatenate([s, s], 0)

    Tb["cosk"], Tb["sink"] = cs(pos)
    Tb["cosq"], Tb["sinq"] = cs(qpos)

    def aug_k(p):
        return np.stack([np.ones_like(p), np.ones_like(p), 128 * (p // 128), p % 128]).astype(f32)

    Tb["kaug"] = aug_k(pos).astype(BFNP)
    cend = 16 * np.arange(NCT * 128) + 31
    Tb["kaug_c"] = aug_k(cend).astype(BFNP)
    qa = np.zeros((NP, 4, 8, 512), f32)
    for P in range(NP):
        t = qpos[P * 512:(P + 1) * 512]
        for h in range(8):
            s = slopes[h]
            qa[P, 0, h] = -s * (128 * (t // 128))
            qa[P, 1, h] = -s * (t % 128)
            qa[P, 2, h] = s
            qa[P, 3, h] = s
    Tb["qaug"] = qa.astype(BFNP)
    j = np.arange(128)[:, None, None]
    i = np.arange(512)[None, None, :]
    r = np.arange(8)[None, :, None]
    Tb["dmask"] = np.where(128 * r + j > i + 512 * half, NEGM, 0.0).astype(BFNP)
    r = np.arange(12)[None, :, None]
    dist = (i + 512 * half) - (128 * r + j - 512)
    Tb["wmask"] = np.where((dist >= 0) & (dist < 512), 0.0, NEGM).astype(BFNP)
    cm = np.zeros((NP, 128, NCT, 512), f32)
    ce = cend.reshape(NCT, 128).T
    for P in range(NP):
        t = qpos[P * 512:(P + 1) * 512]
        cm[P] = np.where(ce[:, :, None] > t[None, None, :], NEGM, 0.0)
    Tb["cmask"] = cm.astype(BFNP)
    bo = np.zeros((NP, 128, 4, NSEL), f32)
    blk = np.arange(NSEL)[None, None, :]
    for P in range(NP):
        t = qpos[P * 512:(P + 1) * 512].reshape(4, 128).T
        cur = (t // 64)[:, :, None]
        forced = (blk == 0) | (blk == cur) | (blk == cur - 1)
        bo[P] = np.where(blk <= cur, np.where(forced, 1e4, 0.0), -1e30)
    Tb["bonus"] = bo
    E = np.zeros((NSEL, NKT, 128), f32)
    for kt in range(NKT):
        E[2 * kt, kt, 0:64] = 1.0
        E[2 * kt + 1, kt, 64:128] = 1.0
    Tb["E"] = E.astype(BFNP)
    n_c = (SEQ - 32) // 16 + 1
    c0 = np.arange(NCT * 128) * 16
    s0 = np.arange(NSEL) * 64
    ov = np.clip(np.minimum(c0[:, None] + 32, s0[None, :] + 64) - np.maximum(c0[:, None], s0[None, :]), 0, None) / 32.0
    ov[n_c:] = 0.0
    Tb["ovl"] = np.ascontiguousarray(ov.reshape(NCT, 128, NSEL).transpose(1, 0, 2)).astype(BFNP)
    Tb["hm"] = np.tile(np.array([[1.0 - half, float(half)]], f32), (128, 1))
    Tb["ident_bf"] = np.eye(128).astype(BFNP)
    Tb["ident_f"] = np.eye(128, dtype=f32)
    return Tb


W_IN_PERM = None


def w_in_perm():
    kv0 = 512
    def six(s):
        return list(range(kv0 + s * 128, kv0 + (s + 1) * 128))
    perm = list(range(0, 512)) + six(0) + six(1) + six(2) + six(4) + six(3) + six(5)
    perm += list(range(1560, 1688)) + list(range(1688, 1720)) + list(range(1304, 1560)) + list(range(1280, 1304))
    perm += list(range(1720, 3768))
    assert len(perm) == 3768 and len(set(perm)) == 3768
    return np.array(perm)


def host_weights(inp):
    f32 = np.float32
    col = lambda v, k: np.ascontiguousarray(np.asarray(v, f32).reshape(k, 128).T)
    row = lambda v: np.ascontiguousarray(np.broadcast_to(np.asarray(v, f32), (128, D)))
    Wm = {}
    for f in ("ff1", "ff2"):
        Wm[f + "_wg"] = np.asarray(inp[f + "_w_gate"], f32)
        Wm[f + "_wu"] = np.asarray(inp[f + "_w_up"], f32)
        Wm[f + "_wd"] = np.asarray(inp[f + "_w_down"], f32)
        Wm[f + "_precol"] = col(inp[f + "_pre_g"], 8)
        Wm[f + "_postrow"] = row(inp[f + "_post_g"])
    Wm["mix_precol"] = col(inp["mix_pre_g"], 8)
    Wm["mix_postrow"] = row(inp["mix_post_g"])
    Wm["w_in_p"] = np.ascontiguousarray(np.asarray(inp["w_in"], f32)[:, w_in_perm()])
    Wm["mla_w_uq"] = np.asarray(inp["mla_w_uq"], f32)
    ukv = np.asarray(inp["mla_w_ukv"], f32).reshape(128, 8, 2, 64)
    Wm["mla_w_ukv_p"] = np.ascontiguousarray(np.concatenate([ukv[:, :, 0, :].reshape(128, 512), ukv[:, :, 1, :].reshape(128, 512)], 1))
    Wm["q_norm_col"] = col(inp["mla_q_norm_g"], 2)
    Wm["kv_norm_col"] = col(inp["mla_kv_norm_g"], 1)
    for kvn in ("k", "v"):
        Wm["cmp_w1_" + kvn] = np.asarray(inp["cmp_w1_" + kvn], f32)
        Wm["cmp_w2_" + kvn] = np.asarray(inp["cmp_w2_" + kvn], f32)
        Wm["cmp_posT_" + kvn] = np.ascontiguousarray(np.asarray(inp["cmp_pos_" + kvn], f32).T)
    Wm["w_proj_nsa"] = np.asarray(inp["w_proj_nsa"], f32)
    Wm["w_proj_mla"] = np.asarray(inp["w_proj_mla"], f32)
    Wm["w_out"] = np.asarray(inp["w_out"], f32)
    return Wm


_CACHE = {}


def kernel(**inputs):
    x = np.asarray(inputs["x"], np.float32)
    Bn, SEQ, _ = x.shape
    assert 2 * Bn == 8
    if SEQ not in _CACHE:
        _CACHE[SEQ] = build(SEQ)
    nc = _CACHE[SEQ]
    Wm = host_weights(inputs)
    tabs = [host_tables(SEQ, 0), host_tables(SEQ, 1)]
    in_maps = []
    for core in range(8):
        b, half = core // 2, core % 2
        m = dict(Wm)
        m.update(tabs[half])
        m["x"] = np.ascontiguousarray(x[b])
        in_maps.append({k: v for k, v in m.items() if k in nc.in_names})
    res = run_bass_kernel_spmd(nc, in_maps, core_ids=list(range(8)))
    NP = SEQ // 1024
    out = np.empty((Bn, SEQ, D), np.float32)
    for core in range(8):
        b, half = core // 2, core % 2
        o = res.results[core]["out"]
        for P in range(NP):
            t = 2 * P + half
            out[b, t * 512:(t + 1) * 512] = o[P * 512:(P + 1) * 512]
    return out
```
